# Optimizing a Trainium2 kernel written in Bass

```python
import jax, jax.numpy as jnp
from jax import lax
import numpy as np

D_MODEL = 1024
BATCH = 8
SEQ = 4096
DEPTH = 4
DEC_BATCH = 16
DEC_SEQ = 32
PAST_LEN = 4096

CHUNK = 64
EPS = 1e-6
D_FF = 2816
PLE_DIM = 256
N_BRANCH = 4
BRANCH_WIDTH = D_MODEL // 2
POOL_WINDOWS = (2, 4, 8, 16)
POOL_GROUPS = len(POOL_WINDOWS)
POOL_GROUP_DIM = BRANCH_WIDTH // POOL_GROUPS
POOL_HIST = max(POOL_WINDOWS) - 1
SB_HEAD_DIM = 64
SB_HEADS = BRANCH_WIDTH // SB_HEAD_DIM
SB_WIDTH = SB_HEADS * SB_HEAD_DIM
SB_BLOCK = 128
GMLP_CHUNK = 128
GMLP_GROUPS = 4
GMLP_GROUP_DIM = BRANCH_WIDTH // GMLP_GROUPS
GLA_HEADS = 4
GLA_DK = BRANCH_WIDTH // (2 * GLA_HEADS)
GLA_DV = BRANCH_WIDTH // GLA_HEADS
GLA_GATE_RANK = 16
GLA_GATE_NORMALIZER = 16.0
IN_SPLITS = (BRANCH_WIDTH,
             SB_WIDTH, SB_WIDTH, SB_WIDTH,
             BRANCH_WIDTH, BRANCH_WIDTH,
             GLA_HEADS * GLA_DK, GLA_HEADS * GLA_DK,
             GLA_HEADS * GLA_DV, GLA_GATE_RANK,
             GLA_HEADS * GLA_DV,
             N_BRANCH * D_MODEL)
IN_COLS = sum(IN_SPLITS)

kernel_name = "hybrid_streaming_encoder_step"


def rms_norm(x, g):
    xf = x.astype(jnp.float32)
    y = xf * lax.rsqrt(jnp.mean(xf * xf, axis=-1, keepdims=True) + EPS)
    return (y * g.astype(jnp.float32)).astype(x.dtype)


def swiglu(x, w1, w3, w2):
    return (jax.nn.silu(x @ w1) * (x @ w3)) @ w2


def split_columns(z):
    idx = np.cumsum(IN_SPLITS)[:-1].tolist()
    return jnp.split(z, idx, axis=-1)


def pool_mixer(xp, hist, pos0, pool_w, pool_scale):
    B, L, W = xp.shape
    ext_in = jnp.concatenate([hist, xp], axis=1)
    ext = ext_in.astype(jnp.float32)
    cs = jnp.concatenate([jnp.zeros((B, 1, W), jnp.float32), jnp.cumsum(ext, axis=1)], axis=1)
    end = cs[:, POOL_HIST + 1:]
    pos = pos0 + jnp.arange(L)
    means = []
    for g, w in enumerate(POOL_WINDOWS):
        sl = slice(g * POOL_GROUP_DIM, (g + 1) * POOL_GROUP_DIM)
        start = cs[:, POOL_HIST + 1 - w:POOL_HIST + 1 - w + L, sl]
        cnt = jnp.minimum(w, pos + 1).astype(jnp.float32)[None, :, None]
        means.append((end[..., sl] - start) / cnt)
    d = jnp.concatenate(means, axis=-1) - ext[:, POOL_HIST:]
    d = d.reshape(B, L, POOL_GROUPS, POOL_GROUP_DIM)
    y = jnp.einsum('blgc,gcd->blgd', d, pool_w.astype(jnp.float32)).reshape(B, L, W)
    y = (y * pool_scale.astype(jnp.float32)).astype(xp.dtype)
    return y, ext_in[:, -POOL_HIST:]


def sb_block(qb, qpos, k, v, kpos):
    z = jnp.einsum('bqhd,bkhd->bhqk', qb, k) * (SB_HEAD_DIM ** -0.5)
    mask = kpos[None, :] < qpos[:, None]
    l = jnp.where(mask, jax.nn.log_sigmoid(-z), 0.0)
    after = lax.cumsum(l, axis=3, reverse=True) - l
    a = jnp.where(mask, jnp.exp(jax.nn.log_sigmoid(z) + after), 0.0)
    return jnp.einsum('bhqk,bkhd->bqhd', a, v)


def stick_breaking_attention(q, k_all, v_all):
    B, L, H, d = q.shape
    Lk = k_all.shape[1]
    kpos = jnp.arange(Lk)
    qpos = (Lk - L) + jnp.arange(L)
    qf, kf, vf = q.astype(jnp.float32), k_all.astype(jnp.float32), v_all.astype(jnp.float32)
    if L <= SB_BLOCK:
        o = sb_block(qf, qpos, kf, vf, kpos)
    else:
        nb = L // SB_BLOCK
        qb = qf.reshape(B, nb, SB_BLOCK, H, d).transpose(1, 0, 2, 3, 4)
        pb = qpos.reshape(nb, SB_BLOCK)
        o = lax.map(lambda a: sb_block(a[0], a[1], kf, vf, kpos), (qb, pb))
        o = o.transpose(1, 0, 2, 3, 4).reshape(B, L, H, d)
    return o.astype(q.dtype)


def spatial_gating(u, v, ws, bs):
    B, L, W = u.shape
    c = min(L, GMLP_CHUNK)
    n = L // c
    idx = jnp.arange(c)
    mask = (idx[None, :] // CHUNK) <= (idx[:, None] // CHUNK)
    wm = jnp.where(mask[None], ws[:, :c, :c], 0.0)
    vc = v.reshape(B, n, c, GMLP_GROUPS, GMLP_GROUP_DIM)
    s = jnp.einsum('gij,bnjgc->bnigc', wm, vc) + bs[:, :c].T[None, None, :, :, None]
    return u * s.reshape(B, L, W)


def gla_chunk(S, q, k, v, g):
    c = q.shape[1]
    b = jnp.cumsum(g, axis=1)
    inter = jnp.einsum('bthk,bhkv->bthv', q * jnp.exp(b), S)
    mask = jnp.tril(jnp.ones((c, c), bool))[None, :, :, None, None]
    diff = b[:, :, None] - b[:, None, :]
    decay = jnp.where(mask, jnp.exp(jnp.where(mask, diff, 0.0)), 0.0)
    att = jnp.einsum('bthk,bshk,btshk->bhts', q, k, decay)
    intra = jnp.einsum('bhts,bshv->bthv', att, v)
    b_last = b[:, -1]
    S_new = jnp.exp(b_last)[..., None] * S + jnp.einsum('bshk,bshv->bhkv', k * jnp.exp(b_last[:, None] - b), v)
    return S_new, inter + intra


def gla_recurrence(q, k, v, g, s0):
    B, L, H, _ = q.shape
    qf, kf, vf, gf = (a.astype(jnp.float32) for a in (q, k, v, g))
    S0 = s0.astype(jnp.float32)
    c = min(L, CHUNK)
    n = L // c
    if n == 1:
        S, o = gla_chunk(S0, qf, kf, vf, gf)
    else:
        def to_chunks(a):
            return a.reshape(B, n, c, *a.shape[2:]).swapaxes(0, 1)
        S, o = lax.scan(lambda s, xs: gla_chunk(s, *xs), S0,
                        (to_chunks(qf), to_chunks(kf), to_chunks(vf), to_chunks(gf)))
        o = o.swapaxes(0, 1).reshape(B, L, H, GLA_DV)
    return S.astype(s0.dtype), o.astype(v.dtype)


def token_mixing(h, pool_hist, k_past, v_past, s0, w_in, pool_w, pool_scale, sb_qn, sb_kn,
                 gmlp_ws, gmlp_b, gla_wa2, gla_ba, gla_on, w_branch, w_out):
    B, L, _ = h.shape
    pos0 = k_past.shape[1]
    (xp, q, k, v, gu, gv, lq, lk, lv, la, lr, gates) = split_columns(h @ w_in)
    y_pool, pool_new = pool_mixer(xp, pool_hist, pos0, pool_w, pool_scale)
    q = rms_norm(q.reshape(B, L, SB_HEADS, SB_HEAD_DIM), sb_qn)
    k = rms_norm(k.reshape(B, L, SB_HEADS, SB_HEAD_DIM), sb_kn)
    v = v.reshape(B, L, SB_HEADS, SB_HEAD_DIM)
    y_sb = stick_breaking_attention(q, jnp.concatenate([k_past, k], axis=1),
                                    jnp.concatenate([v_past, v], axis=1)).reshape(B, L, SB_WIDTH)
    gu, gv = jax.nn.gelu(gu), jax.nn.gelu(gv)
    y_gmlp = spatial_gating(gu, gv, gmlp_ws, gmlp_b)
    log_a = jax.nn.log_sigmoid((la @ gla_wa2 + gla_ba).astype(jnp.float32)) / GLA_GATE_NORMALIZER
    s_new, o = gla_recurrence(lq.reshape(B, L, GLA_HEADS, GLA_DK) * (GLA_DK ** -0.5),
                              lk.reshape(B, L, GLA_HEADS, GLA_DK),
                              lv.reshape(B, L, GLA_HEADS, GLA_DV),
                              log_a.reshape(B, L, GLA_HEADS, GLA_DK), s0)
    y_gla = rms_norm(o, gla_on).reshape(B, L, BRANCH_WIDTH) * jax.nn.silu(lr)
    gates = jax.nn.sigmoid(gates.reshape(B, L, N_BRANCH, D_MODEL))
    merged = gates[:, :, 0] * (y_pool @ w_branch[0])
    merged = merged + gates[:, :, 1] * (y_sb @ w_branch[1])
    merged = merged + gates[:, :, 2] * (y_gmlp @ w_branch[2])
    merged = merged + gates[:, :, 3] * (y_gla @ w_branch[3])
    return merged @ w_out, pool_new, k, v, s_new, gv


def layer(x, p_i, pool_hist, k_past, v_past, s0, lw):
    (n1, f1a, f1b, f1c, nm, w_in, pool_w, pool_scale, qn, kn, ws, bs, wa2, ba, on,
     w_branch, w_out, n2, f2a, f2b, f2c, npl, wpg, wpp) = lw
    x = x + 0.5 * swiglu(rms_norm(x, n1), f1a, f1b, f1c)
    m, pool_new, k, v, s_new, gv = token_mixing(rms_norm(x, nm), pool_hist, k_past, v_past, s0, w_in,
                                                pool_w, pool_scale, qn, kn, ws, bs, wa2, ba, on,
                                                w_branch, w_out)
    x = x + m
    x = x + 0.5 * swiglu(rms_norm(x, n2), f2a, f2b, f2c)
    x = x + jax.nn.sigmoid(rms_norm(x, npl) @ wpg) * (p_i @ wpp)
    return x, pool_new, k, v, s_new, gv


def setup_inputs(seed: int = 0) -> dict:
    key = jax.random.key(seed)
    ks = iter(list(jax.random.split(key, 40)))

    def nrm(shape, scale):
        return jax.random.normal(next(ks), shape, jnp.float32) * scale

    def gain(shape):
        return 1.0 + 0.05 * jax.random.normal(next(ks), shape, jnp.float32)

    return {
        'x_prompt': nrm((BATCH, SEQ, D_MODEL), 1.0),
        'x_sample': nrm((DEC_BATCH, DEC_SEQ, D_MODEL), 1.0),
        'p_prompt': nrm((DEPTH, BATCH, SEQ, PLE_DIM), 1.0),
        'p_sample': nrm((DEPTH, DEC_BATCH, DEC_SEQ, PLE_DIM), 1.0),
        'cache_sb_k': nrm((DEPTH, DEC_BATCH, PAST_LEN, SB_HEADS, SB_HEAD_DIM), 1.0),
        'cache_sb_v': nrm((DEPTH, DEC_BATCH, PAST_LEN, SB_HEADS, SB_HEAD_DIM), 1.0),
        'state_pool': nrm((DEPTH, DEC_BATCH, POOL_HIST, BRANCH_WIDTH), 1.0),
        'state_gla': nrm((DEPTH, DEC_BATCH, GLA_HEADS, GLA_DK, GLA_DV), 0.3),
        'norm_ffn1': gain((DEPTH, D_MODEL)),
        'ffn1_w1': nrm((DEPTH, D_MODEL, D_FF), D_MODEL ** -0.5),
        'ffn1_w3': nrm((DEPTH, D_MODEL, D_FF), D_MODEL ** -0.5),
        'ffn1_w2': nrm((DEPTH, D_FF, D_MODEL), D_FF ** -0.5),
        'norm_mix': gain((DEPTH, D_MODEL)),
        'w_in': nrm((DEPTH, D_MODEL, IN_COLS), D_MODEL ** -0.5),
        'pool_w': nrm((DEPTH, POOL_GROUPS, POOL_GROUP_DIM, POOL_GROUP_DIM), POOL_GROUP_DIM ** -0.5),
        'pool_scale': gain((DEPTH, BRANCH_WIDTH)),
        'sb_q_norm': gain((DEPTH, SB_HEAD_DIM)),
        'sb_k_norm': gain((DEPTH, SB_HEAD_DIM)),
        'gmlp_ws': nrm((DEPTH, GMLP_GROUPS, GMLP_CHUNK, GMLP_CHUNK), GMLP_CHUNK ** -0.5),
        'gmlp_b': gain((DEPTH, GMLP_GROUPS, GMLP_CHUNK)),
        'gla_wa2': nrm((DEPTH, GLA_GATE_RANK, GLA_HEADS * GLA_DK), GLA_GATE_RANK ** -0.5),
        'gla_ba': nrm((DEPTH, GLA_HEADS * GLA_DK), 0.01),
        'gla_out_norm': gain((DEPTH, GLA_DV)),
        'w_branch': nrm((DEPTH, N_BRANCH, BRANCH_WIDTH, D_MODEL), BRANCH_WIDTH ** -0.5),
        'w_out': nrm((DEPTH, D_MODEL, D_MODEL), D_MODEL ** -0.5),
        'norm_ffn2': gain((DEPTH, D_MODEL)),
        'ffn2_w1': nrm((DEPTH, D_MODEL, D_FF), D_MODEL ** -0.5),
        'ffn2_w3': nrm((DEPTH, D_MODEL, D_FF), D_MODEL ** -0.5),
        'ffn2_w2': nrm((DEPTH, D_FF, D_MODEL), D_FF ** -0.5),
        'norm_ple': gain((DEPTH, D_MODEL)),
        'ple_w_gate': nrm((DEPTH, D_MODEL, D_MODEL), D_MODEL ** -0.5),
        'ple_w_proj': nrm((DEPTH, PLE_DIM, D_MODEL), PLE_DIM ** -0.5),
    }


def reference(x_prompt, x_sample, p_prompt, p_sample, cache_sb_k, cache_sb_v, state_pool, state_gla,
              norm_ffn1, ffn1_w1, ffn1_w3, ffn1_w2, norm_mix, w_in, pool_w, pool_scale,
              sb_q_norm, sb_k_norm, gmlp_ws, gmlp_b, gla_wa2, gla_ba, gla_out_norm,
              w_branch, w_out, norm_ffn2, ffn2_w1, ffn2_w3, ffn2_w2, norm_ple, ple_w_gate, ple_w_proj):
    lw_all = (norm_ffn1, ffn1_w1, ffn1_w3, ffn1_w2, norm_mix, w_in, pool_w, pool_scale,
              sb_q_norm, sb_k_norm, gmlp_ws, gmlp_b, gla_wa2, gla_ba, gla_out_norm,
              w_branch, w_out, norm_ffn2, ffn2_w1, ffn2_w3, ffn2_w2, norm_ple, ple_w_gate, ple_w_proj)
    B, S, _ = x_prompt.shape
    dt = x_prompt.dtype
    pool_hist0 = jnp.zeros((B, POOL_HIST, BRANCH_WIDTH), dt)
    kv_past0 = jnp.zeros((B, 0, SB_HEADS, SB_HEAD_DIM), dt)
    gla0 = jnp.zeros((B, GLA_HEADS, GLA_DK, GLA_DV), dt)
    yp, ys = x_prompt, x_sample
    pk, pv, ppool, pgla = [], [], [], []
    sk, sv, spool, sgla, sgmlp = [], [], [], [], []
    for i in range(DEPTH):
        lw = tuple(w[i] for w in lw_all)
        yp, pool_p, k_p, v_p, s_p, _ = layer(yp, p_prompt[i], pool_hist0, kv_past0, kv_past0, gla0, lw)
        ys, pool_s, k_s, v_s, s_s, gv_s = layer(ys, p_sample[i], state_pool[i], cache_sb_k[i],
                                                cache_sb_v[i], state_gla[i], lw)
        pk.append(k_p); pv.append(v_p); ppool.append(pool_p); pgla.append(s_p)
        sk.append(k_s); sv.append(v_s); spool.append(pool_s); sgla.append(s_s); sgmlp.append(gv_s)
    return (yp, ys,
            jnp.stack(pk), jnp.stack(pv), jnp.stack(ppool), jnp.stack(pgla),
            jnp.stack(sk), jnp.stack(sv), jnp.stack(spool), jnp.stack(sgla), jnp.stack(sgmlp))
```

```python
import numpy as np
from contextlib import ExitStack
import concourse.bass as bass
import concourse.mybir as mybir
from concourse.bass_utils import run_bass_kernel_spmd

F32 = mybir.dt.float32
BF16 = mybir.dt.bfloat16
AF = mybir.ActivationFunctionType
ALU = mybir.AluOpType

D = 1024
KD = 8
DFF = 2816
PLE = 256
BW = 512
NCOLS_IN = 8720
EPS = 1e-6
TT = 512
DEC = 32
NEGV = -30000.0
O_XP, O_Q, O_K, O_V, O_GU, O_GV, O_LQ, O_LK, O_LV, O_LA, O_LR, O_G = (
    0, 512, 1024, 1536, 2048, 2560, 3072, 3328, 3584, 4096, 4112, 4624)
NSLOT = 4
import os
DBG = int(os.environ.get("KDBG", "99"))
NCOL = 39
C_N1, C_NM, C_N2, C_NPL, C_PS, C_QN, C_KN, C_GON = 0, 8, 16, 24, 32, 36, 37, 38


class Rec:
    def __init__(self):
        self.engs = ["pe", "act", "dve", "pool", "sp"]
        self.ops = {e: [] for e in self.engs}
        self.count = {e: 0 for e in self.engs}
        self.waited = {e: {} for e in self.engs}
        self.res = {}
        self.dma_n = {"sp": 0, "pool": 0}
        self.dma_nsem = {"sp": 24, "pool": 68}
        self.dma_events = []
        self.pending = {}

    PERSIST = ("xT", "hT", "KT", "VC", "ws", "lw", "const", "phist", "S32", "Sbf", "ypool", "ysb", "ygm", "ygl",
               "sq0", "sq1", "rstd", "pT", "ps", "xd")

    def retire(self):
        for name in list(self.res.keys()):
            if name.startswith(self.PERSIST):
                continue
            w, rs = self.res.pop(name)
            for ev in ([w] if w else []) + rs:
                k, v = ev
                if self.pending.get(k, 0) < v:
                    self.pending[k] = v

    def _r(self, name):
        if name not in self.res:
            self.res[name] = [None, []]
        return self.res[name]

    def _deps(self, eng, reads, writes):
        deps = {}
        def add(ev):
            if ev is None:
                return
            k, v = ev
            if deps.get(k, 0) < v:
                deps[k] = v
        for nm in list(reads) + list(writes):
            if nm not in self.res and not nm.startswith(self.PERSIST):
                for k, v in self.pending.items():
                    add((k, v))
                break
        for r in reads:
            add(self._r(r)[0])
        for w in writes:
            e = self._r(w)
            add(e[0])
            for ev in e[1]:
                add(ev)
        waits = []
        for k, v in deps.items():
            if k == "pe" and eng == "pe":
                continue
            if self.waited[eng].get(k, 0) >= v:
                continue
            self.waited[eng][k] = v
            waits.append((k, v))
        return waits

    def _commit(self, ev, reads, writes):
        for r in reads:
            self._r(r)[1].append(ev)
        for w in writes:
            e = self._r(w)
            e[0] = ev
            e[1] = []

    def op(self, eng, fn, reads=(), writes=()):
        waits = self._deps(eng, reads, writes)
        self.count[eng] += 1
        ev = (eng, self.count[eng])
        self.ops[eng].append((waits, fn, ("cnt", eng)))
        self._commit(ev, reads, writes)

    def dma(self, q, fn, reads=(), writes=()):
        waits = self._deps(q, reads, writes)
        i = self.dma_n[q]
        self.dma_n[q] += 1
        n = self.dma_nsem[q]
        key = ("dma", q, i % n)
        if i >= n:
            prev = 16 * (i // n)
            if self.waited[q].get(key, 0) < prev:
                self.waited[q][key] = prev
                waits.append((key, prev))
        ev = (key, 16 * (i // n + 1))
        self.ops[q].append((waits, fn, ("dma", key)))
        self._commit(ev, reads, writes)
        self.dma_events.append(ev)
        return ev

    def fence(self, eng, events):
        waits = []
        for k, v in events:
            if self.waited[eng].get(k, 0) >= v:
                continue
            self.waited[eng][k] = v
            waits.append((k, v))
        if waits:
            self.ops[eng].append((waits, None, None))

    def barrier(self, engs=("pe", "act", "dve", "sp")):
        evs = [(e, self.count[e]) for e in ("pe", "act", "dve", "pool") if self.count[e] > 0]
        evs += self.dma_events
        self.dma_events = []
        for e in engs:
            waits = []
            for k, v in evs:
                if k == e:
                    continue
                if self.waited[e].get(k, 0) >= v:
                    continue
                self.waited[e][k] = v
                waits.append((k, v))
            if waits:
                self.ops[e].append((waits, None, None))

    def drop(self, names):
        for n in names:
            self.res.pop(n, None)


def build_program(S, PAST, DEPTH):
    NT = S // TT
    NTOK = S + 2 * DEC
    NPC = PAST // 128
    KTW = max(S, PAST + DEC)
    NVC = max(S // 128, NPC + 1)
    nc = bass.Bass("TRN2", target_bir_lowering=False)

    def din(name, shape):
        return nc.dram_tensor(name, list(shape), F32, kind="ExternalInput").ap()

    def dout(name, shape):
        return nc.dram_tensor(name, list(shape), F32, kind="ExternalOutput").ap()

    xin = din("xin", [D, NTOK])
    pin = din("pin", [DEPTH, PLE, NTOK])
    ckT = din("ckT", [DEPTH, 2, BW, PAST])
    cv = din("cv", [DEPTH, 2, PAST, BW])
    spT = din("spT", [DEPTH, 2, BW, 15])
    sgl = din("sgl", [DEPTH, 2, 4, 64, 128])
    cols_d = din("cols", [DEPTH, 128, NCOL])
    w_f1a = din("f1a", [DEPTH, D, DFF]); w_f1b = din("f1b", [DEPTH, D, DFF]); w_f1c = din("f1c", [DEPTH, DFF, D])
    w_f2a = din("f2a", [DEPTH, D, DFF]); w_f2b = din("f2b", [DEPTH, D, DFF]); w_f2c = din("f2c", [DEPTH, DFF, D])
    w_in = din("w_in", [DEPTH, D, NCOLS_IN])
    pool_w = din("pool_w", [DEPTH, 4, 128, 128])
    wsT_d = din("wsT", [DEPTH, 4, 128, 128])
    gb_d = din("gb", [DEPTH, 1, 512])
    wa2_d = din("wa2", [DEPTH, 16, 256])
    ba_d = din("ba", [DEPTH, 1, 256])
    w_br = din("w_br", [DEPTH, 4, BW, D])
    w_o = din("w_o", [DEPTH, D, D])
    w_pg = din("w_pg", [DEPTH, D, D])
    w_pp = din("w_pp", [DEPTH, PLE, D])
    c_ident = din("c_ident", [128, 128])
    c_tinc = din("c_tinc", [128, 128])
    c_trile = din("c_trile", [128, 128])
    c_negp = din("c_negp", [128, 4 * 512])
    c_negs = din("c_negs", [32, 256])
    c_bmask = din("c_bmask", [128, 128])
    c_invc = din("c_invc", [128, 4 * 16])

    yout = dout("yout", [D, NTOK])
    kpT = dout("kpT", [DEPTH, BW, S]); vp = dout("vp", [DEPTH, S, BW])
    poolpT = dout("poolpT", [DEPTH, BW, 15]); glap = dout("glap", [DEPTH, 4, 64, 128])
    ksT = dout("ksT", [DEPTH, BW, 2 * DEC]); vs = dout("vs", [DEPTH, 2 * DEC, BW])
    poolsT = dout("poolsT", [DEPTH, 2, BW, 15]); glas = dout("glas", [DEPTH, 2, 4, 64, 128])
    gms = dout("gms", [DEPTH, 2 * DEC, BW])
    xscr = nc.dram_tensor("xscr", [D, NTOK], F32, kind="Internal").ap()

    R = Rec()
    es = ExitStack()
    uniq = [0]

    def sbt(name, shape, dt=F32):
        uniq[0] += 1
        return nc.sbuf_tensor(f"{name}_{uniq[0]}", list(shape), dt)

    def sb(name, shape, dt=F32):
        return es.enter_context(sbt(name, list(shape), dt))

    xT = sb("xT", [128, KD, TT])
    hT = sb("hT", [128, KD, TT], BF16)
    KT = sb("KT", [128, 4, KTW], BF16)
    VC = sb("VC", [128, NVC, BW], BF16)
    WS = [sb(f"ws{i}", [128, 8, 512], BF16) for i in range(NSLOT)]
    cols = sb("cols_sb", [128, NCOL])
    poolw_sb = sb("poolw_sb", [128, 4, 128], BF16)
    wmT = sb("wmT", [128, 4, 128], BF16)
    gb_hi = sb("gb_hi", [1, 512], BF16); gb_lo = sb("gb_lo", [1, 512], BF16); gb_f = sb("gb_f", [1, 512])
    wa2_sb = sb("wa2_sb", [16, 256], BF16)
    ba_sb = sb("ba_sb", [1, 256], BF16)
    wla_sb = sb("wla_sb", [128, 8, 16], BF16)
    wpp_sb = sb("wpp_sb", [128, 2, D], BF16)
    ident = sb("ident", [128, 128], BF16)
    tinc = sb("tinc", [128, 128], BF16)
    ones_bf = sb("ones_bf", [128, 128], BF16)
    bdiag = sb("bdiag", [128, 128], BF16)
    trile = sb("trile", [128, 128])
    negp = sb("negp", [128, 4 * 512], BF16)
    negs = sb("negs", [32, 256], BF16)
    bmask = sb("bmask", [128, 128], BF16)
    invc = sb("invc", [128, 4, 16])
    phist = sb("phist", [128, 4, 15])
    S32 = sb("S32", [128, 512]); Sbf = sb("Sbf", [128, 512], BF16)
    SQ = sb("SQ", [128, 2, TT], BF16); RSTD = sb("RSTD", [128, TT])
    pT = sb("pT", [128, 2, TT], BF16)
    nident = sb("nident", [128, 128], BF16)
    ypool = sb("ypool", [128, 4, TT], BF16); ysb = sb("ysb", [128, 4, TT], BF16)
    ygm = sb("ygm", [128, 4, TT], BF16); ygl = sb("ygl", [128, 4, TT], BF16)
    PS = [es.enter_context(nc.psum_tensor(f"ps{i}", [128, 512], F32)) for i in range(7)]
    PSB = es.enter_context(nc.psum_tensor("psb", [128, 1024], BF16))

    slot_ctr = [0]

    WIN_OFFS = [O_XP, O_Q, O_K, O_V, O_GU, O_GV, O_LQ, O_LV, O_LR] + [O_G + i * 512 for i in range(8)]
    FBLK = [(cb * 512, min(512, DFF - cb * 512)) for cb in range((DFF + 511) // 512)]
    KBS = [(0, 8), (8, 8), (16, 6)]
    wsrc = {"f1a": w_f1a, "f1b": w_f1b, "f1c": w_f1c, "f2a": w_f2a, "f2b": w_f2b, "f2c": w_f2c, "w_in": w_in,
            "w_br": w_br, "w_o": w_o, "w_pg": w_pg}
    wblocks = {}
    for nm in ("f1a", "f1b", "f2a", "f2b"):
        wblocks[nm] = [(cb, (lambda l, nm=nm, c0=c0, nco=nco: wsrc[nm][l][:, c0:c0 + nco]), 8, nco)
                       for cb, (c0, nco) in enumerate(FBLK)]
    for nm in ("f1c", "f2c"):
        wblocks[nm] = [((nb, bi), (lambda l, nm=nm, nb=nb, k0=k0, kn=kn: wsrc[nm][l][k0 * 128:(k0 + kn) * 128,
                                                                                      nb * 512:(nb + 1) * 512]), kn, 512)
                       for nb in range(2) for bi, (k0, kn) in enumerate(KBS)]
    wblocks["w_in"] = [(off, (lambda l, off=off: w_in[l][:, off:off + 512]), 8, 512) for off in WIN_OFFS]
    wblocks["w_br"] = [((b_, nb), (lambda l, b_=b_, nb=nb: w_br[l, b_][:, nb * 512:(nb + 1) * 512]), 4, 512)
                       for b_ in range(4) for nb in range(2)]
    for nm in ("w_o", "w_pg"):
        wblocks[nm] = [(nb, (lambda l, nm=nm, nb=nb: wsrc[nm][l][:, nb * 512:(nb + 1) * 512]), 8, 512) for nb in range(2)]
    wscr = {}
    widx = {}
    for nm, bl in wblocks.items():
        wscr[nm] = nc.dram_tensor(f"{nm}_bf", [DEPTH, len(bl), 128, 8, 512], BF16, kind="Internal").ap()
        widx[nm] = {b[0]: (i, b[2], b[3]) for i, b in enumerate(bl)}
    conv_ev = {}

    def convert_layer(l):
        for nm in ("f1a", "f1b", "f1c", "w_in", "w_br", "w_o", "f2a", "f2b", "f2c", "w_pg"):
            evs = []
            for i, (idx, srcf, kcs, nco) in enumerate(wblocks[nm]):
                src = srcf(l).rearrange("(kc p) n -> p kc n", p=128)
                dst = wscr[nm][l, i][:, 0:kcs, 0:nco]
                evs.append(R.dma("pool", lambda e, dst=dst, src=src: e.dma_start(out=dst, in_=src)))
            conv_ev[(nm, l)] = evs

    def load_w(nm, l, idx):
        i, kcs, nco = widx[nm][idx]
        s = slot_ctr[0] % NSLOT
        slot_ctr[0] += 1
        t = WS[s]
        R.fence("sp", conv_ev[(nm, l)])
        src = wscr[nm][l, i][:, 0:kcs, 0:nco]
        R.dma("sp", lambda e, t=t, src=src, kcs=kcs, nco=nco: e.dma_start(out=t[:, 0:kcs, 0:nco], in_=src),
              writes=[f"ws{s}"])
        return t, f"ws{s}"

    def mm(out, lhsT, rhs, start, stop):
        return lambda pe: pe.matmul(out, lhsT, rhs, start=start, stop=stop, skip_group_check=True)

    def pe_ops(fns, reads, writes):
        def run(pe, fns=fns):
            last = None
            for f in fns:
                last = f(pe)
            return last
        R.op("pe", run, reads=reads, writes=writes)

    R.dma("pool", lambda e: e.dma_start(out=ident[:], in_=c_ident), writes=["const"])
    R.dma("pool", lambda e: e.dma_start(out=tinc[:], in_=c_tinc), writes=["const"])
    R.dma("pool", lambda e: e.dma_start(out=negp[:], in_=c_negp), writes=["const"])
    R.dma("pool", lambda e: e.dma_start(out=negs[:], in_=c_negs), writes=["const"])
    R.dma("pool", lambda e: e.dma_start(out=bmask[:], in_=c_bmask), writes=["const"])
    R.dma("sp", lambda e: e.dma_start(out=trile[:], in_=c_trile), writes=["const"])
    R.dma("sp", lambda e: e.dma_start(out=invc[:], in_=c_invc.rearrange("p (g t) -> p g t", g=4)), writes=["const"])
    R.op("dve", lambda e: e.memset(ones_bf[:], 1.0), writes=["const2"])
    R.op("dve", lambda e: e.memset(bdiag[:], 0.0), writes=["const2"])
    R.op("dve", lambda e: e.memset(bdiag[0:64, 0:64], 1.0), writes=["const2"])
    R.op("dve", lambda e: e.memset(bdiag[64:128, 64:128], 1.0), writes=["const2"])
    R.op("dve", lambda e: e.tensor_scalar(out=nident[:], in0=ident[:], scalar1=-1.0, scalar2=None, op0=ALU.mult),
         reads=["const"], writes=["const2"])
    R.barrier(("pe", "act", "dve", "sp", "pool"))

    def rsqrt_ip(ap, rn="rstd"):
        R.op("act", lambda e: e.activation(out=ap, in_=ap, func=AF.Ln), reads=[rn], writes=[rn])
        R.op("act", lambda e: e.activation(out=ap, in_=ap, func=AF.Exp, scale=-0.5), reads=[rn], writes=[rn])

    def rmsnorm(ntok, ccol, ph):
        rstd = RSTD
        for k in range(KD):
            if k % 2 == 0:
                R.op("act", lambda e, k=k: e.activation(out=SQ[:, 0, 0:ntok], in_=xT[:, k, 0:ntok], func=AF.Square),
                     reads=["xT"], writes=["sq0"])
            else:
                R.op("pool", lambda e, k=k: e.tensor_tensor(out=SQ[:, 1, 0:ntok], in0=xT[:, k, 0:ntok], in1=xT[:, k, 0:ntok],
                                                            op=ALU.mult), reads=["xT"], writes=["sq1"])
            pe_ops([mm(PS[6][:, 0:ntok], ones_bf[:], SQ[:, k % 2, 0:ntok], k == 0, k == KD - 1)],
                   reads=[f"sq{k % 2}", "const2"], writes=["ps6"])
        R.op("dve", lambda e: e.tensor_scalar(out=rstd[:, 0:ntok], in0=PS[6][:, 0:ntok], scalar1=1.0 / D, scalar2=EPS,
                                              op0=ALU.mult, op1=ALU.add), reads=["ps6"], writes=["rstd"])
        rsqrt_ip(rstd[:, 0:ntok])
        for k in range(KD):
            R.op("dve", lambda e, k=k: e.scalar_tensor_tensor(out=hT[:, k, 0:ntok], in0=xT[:, k, 0:ntok],
                                                            scalar=cols[:, ccol + k:ccol + k + 1], in1=rstd[:, 0:ntok],
                                                            op0=ALU.mult, op1=ALU.mult),
                 reads=["xT", "rstd", "lw"], writes=[f"hT{k}"])

    HT_ALL = [f"hT{k}" for k in range(KD)]
    psrot = [0]

    def next_ps(n=6):
        i = psrot[0] % n
        psrot[0] += 1
        return i

    def ffn(l, ntok, wa, wb, wc, ccol):
        with ExitStack() as ph_es:
            ph = None
            act = ph_es.enter_context(sbt("f_act", [128, 22, TT], BF16))
            sl = [ph_es.enter_context(sbt(f"f_sl{i}", [128, TT], F32)) for i in range(2)]
            rmsnorm(ntok, ccol, ph)
            nblk = (DFF + 511) // 512
            ci = 0
            for cb in range(nblk):
                c0 = cb * 512
                ncol = min(512, DFF - c0)
                ta, ra = load_w(wa, l, cb)
                tb, rb = load_w(wb, l, cb)
                for j in range(ncol // 128):
                    f = cb * 4 + j
                    ia = next_ps(); ib = next_ps()
                    pe_ops([mm(PS[ia][:, 0:ntok], ta[:, k, j * 128:(j + 1) * 128], hT[:, k, 0:ntok], k == 0, k == KD - 1)
                            for k in range(KD)], reads=HT_ALL + [ra], writes=[f"ps{ia}"])
                    pe_ops([mm(PS[ib][:, 0:ntok], tb[:, k, j * 128:(j + 1) * 128], hT[:, k, 0:ntok], k == 0, k == KD - 1)
                            for k in range(KD)], reads=HT_ALL + [rb], writes=[f"ps{ib}"])
                    s = sl[ci % 2]; ci += 1
                    R.op("act", lambda e, s=s, ia=ia: e.activation(out=s[:, 0:ntok], in_=PS[ia][:, 0:ntok], func=AF.Silu),
                         reads=[f"ps{ia}"], writes=[f"sl{id(s)}"])
                    R.op("dve", lambda e, s=s, ib=ib, f=f: e.tensor_tensor(out=act[:, f, 0:ntok], in0=PS[ib][:, 0:ntok],
                                                                           in1=s[:, 0:ntok], op=ALU.mult),
                         reads=[f"ps{ib}", f"sl{id(s)}"], writes=[f"act{f}"])
            kbs = [(0, 8), (8, 8), (16, 6)]
            for nb in range(2):
                banks = [next_ps() for _ in range(4)]
                for bi, (k0, kn) in enumerate(kbs):
                    t, r = load_w(wc, l, (nb, bi))
                    for j in range(4):
                        pe_ops([mm(PS[banks[j]][:, 0:ntok], t[:, k, j * 128:(j + 1) * 128], act[:, k0 + k, 0:ntok],
                                   (bi == 0 and k == 0), (bi == 2 and k == kn - 1)) for k in range(kn)],
                               reads=[f"act{k0 + k}" for k in range(kn)] + [r], writes=[f"ps{banks[j]}"])
                for j in range(4):
                    n = nb * 4 + j
                    R.op("dve", lambda e, n=n, b=banks[j]: e.scalar_tensor_tensor(
                        out=xT[:, n, 0:ntok], in0=PS[b][:, 0:ntok], scalar=0.5, in1=xT[:, n, 0:ntok],
                        op0=ALU.mult, op1=ALU.add), reads=[f"ps{banks[j]}", "xT"], writes=["xT"])
            R.retire()

    def fm_block(t, r, ncol, ntok, evac):
        for j in range((ncol + 127) // 128):
            w = min(128, ncol - j * 128)
            b = next_ps()
            pe_ops([mm(PS[b][0:w, 0:ntok], t[:, k, j * 128:j * 128 + w], hT[:, k, 0:ntok], k == 0, k == KD - 1)
                    for k in range(KD)], reads=HT_ALL + [r], writes=[f"ps{b}"])
            evac(j, b)

    def tm_block(t, r, subs, evac):
        for si, (c0, n) in enumerate(subs):
            b = next_ps()
            pe_ops([mm(PS[b][0:n, 0:512], hT[:, k, c0:c0 + n], t[:, k, 0:512], k == 0, k == KD - 1)
                    for k in range(KD)], reads=HT_ALL + [r], writes=[f"ps{b}"])
            evac(si, b, n)

    def sb_attend(streams, N, qn):
        nch = len(streams[0]["chunks"])
        for st_ in streams:
            R.op("dve", lambda e, st_=st_: e.memset(st_["R"][:, 0:N], 0.0), writes=[f"R{st_['sid']}"])

        def stage1(st_, i):
            sid = st_["sid"]; groups = st_["groups"]
            c, key0, nk, ng = st_["chunks"][i]
            sbk = sid
            e_ap = st_["e"][i % 2]; en = st_["en"][i % 2]
            fns = []
            for gi, (hc, hp, h, oc, nq, col0, qc0) in enumerate(groups):
                last = (gi == len(groups) - 1) and ng is None
                if hp is None:
                    fns.append(mm(PS[sbk][0:nk, col0:col0 + nq], KT[:, hc, key0:key0 + nk],
                                  qn[:, h, qc0:qc0 + nq], gi == 0, last))
                else:
                    fns.append(mm(PS[sbk][0:nk, col0:col0 + nq], KT[hp:hp + 64, hc, key0:key0 + nk],
                                  qn[hp:hp + 64, hc, qc0:qc0 + nq], gi == 0, last))
            if ng is not None:
                fns.append(mm(PS[sbk][0:nk, 0:N], ident[0:nk, 0:nk], ng, False, True))
            pe_ops(fns, reads=["KT", "qn", "const", "const2"], writes=[f"ps{sbk}"])
            R.op("act", lambda e: e.activation(out=e_ap[0:nk, 0:N], in_=PS[sbk][0:nk, 0:N], func=AF.Exp, scale=0.125),
                 reads=[f"ps{sbk}"], writes=[en])
            R.op("act", lambda e: e.activation(out=st_["sp"][i % 2][0:nk, 0:N], in_=e_ap[0:nk, 0:N], func=AF.Ln,
                                               bias=1.0, scale=1.0), reads=[en], writes=[f"sp{sid}{i % 2}"])

        def stage2(st_, i):
            sid = st_["sid"]
            c, key0, nk, ng = st_["chunks"][i]
            pb = 2 + sid; qb = 4 + sid
            sp_ap = st_["sp"][i % 2]; spn = f"sp{sid}{i % 2}"
            e_ap = st_["e"][i % 2]; en = st_["en"][i % 2]
            t_ap = st_["tmp"][i % 2]; tn = st_["tn"][i % 2]
            a_ap = st_["a"][i % 2]; an = f"a{sid}{i % 2}"
            Rr = st_["R"]; rn = f"R{sid}"
            pe_ops([mm(PS[pb][0:nk, 0:N], tinc[0:nk, 0:nk], sp_ap[0:nk, 0:N], True, True)], reads=[spn, "const"],
                   writes=[f"ps{pb}"])
            if i < nch - 1:
                pe_ops([mm(PS[qb][:, 0:N], ones_bf[0:nk, :], sp_ap[0:nk, 0:N], True, True)], reads=[spn, "const2"],
                       writes=[f"ps{qb}"])
            R.op("dve", lambda e: e.tensor_tensor(out=t_ap[0:nk, 0:N], in0=PS[pb][0:nk, 0:N], in1=Rr[0:nk, 0:N], op=ALU.add),
                 reads=[f"ps{pb}", rn], writes=[tn])
            R.op("act", lambda e: e.activation(out=t_ap[0:nk, 0:N], in_=t_ap[0:nk, 0:N], func=AF.Exp, scale=-1.0),
                 reads=[tn], writes=[tn])
            R.op("pool", lambda e: e.tensor_tensor(out=a_ap[0:nk, 0:N], in0=e_ap[0:nk, 0:N], in1=t_ap[0:nk, 0:N], op=ALU.mult),
                 reads=[en, tn], writes=[an])
            if i < nch - 1:
                R.op("dve", lambda e: e.tensor_tensor(out=Rr[:, 0:N], in0=PS[qb][:, 0:N], in1=Rr[:, 0:N], op=ALU.add),
                     reads=[f"ps{qb}", rn], writes=[rn])

        def stage3(st_, i):
            sid = st_["sid"]; groups = st_["groups"]
            c, key0, nk, ng = st_["chunks"][i]
            a_ap = st_["a"][i % 2]; an = f"a{sid}{i % 2}"
            fns = []
            for gi, (hc, hp, h, oc, nq, col0, qc0) in enumerate(groups):
                ohp = (h % 2) * 64
                st0 = (i == 0) and (gi < 2)
                fns.append(mm(PS[6][ohp:ohp + 64, oc:oc + nq], VC[0:nk, c, h * 64:(h + 1) * 64],
                              a_ap[0:nk, col0:col0 + nq], st0, i == nch - 1))
            pe_ops(fns, reads=[an, "VC"], writes=["ps6"])

        for i in range(nch + 2):
            if i < nch:
                for st_ in streams:
                    stage1(st_, i)
            if 1 <= i <= nch:
                for st_ in streams:
                    stage2(st_, i - 1)
            if i >= 2:
                for st_ in streams:
                    stage3(st_, i - 2)

    def mixer(l, tile):
        ntok, subs, segs, c0g, is_sample, tix = tile
        nsub = len(subs)
        nseg = len(segs)
        L = segs[0][1]
        mx = ExitStack()

        def ph_sb(name, shape, dt=F32):
            return mx.enter_context(sbt(name, list(shape), dt))
        rmsnorm(ntok, C_NM, None)

        with ExitStack() as pa:
            xpe = pa.enter_context(sbt("a_xpe", [128, 4, nseg, 15 + L], F32))
            tb_ = [pa.enter_context(sbt(f"a_t{i}", [128, nseg, 15 + L], F32)) for i in range(4)]
            dd = pa.enter_context(sbt("a_d", [128, 4, nseg, L], BF16))
            XPE = [f"xpe{g}" for g in range(4)]
            t, r = load_w("w_in", l, O_XP)
            R.op("dve", lambda e: e.memset(tb_[0][:], 0.0), writes=["t0"])
            R.op("dve", lambda e: e.memset(tb_[1][:], 0.0), writes=["t1"])
            R.op("pool", lambda e: e.memset(tb_[2][:], 0.0), writes=["t2"])
            R.op("pool", lambda e: e.memset(tb_[3][:], 0.0), writes=["t3"])
            if is_sample:
                for s in range(2):
                    R.dma("sp", lambda e, s=s: e.dma_start(out=xpe[:, :, s, 0:15],
                                                           in_=spT[l, s].rearrange("(g p) t -> p g t", p=128)),
                          writes=XPE)
            else:
                R.op("dve", lambda e: e.tensor_copy(out=xpe[:, :, 0, 0:15], in_=phist[:]), reads=["phist"], writes=XPE)

            def ev_xp(j, b):
                R.op("act", lambda e, j=j, b=b: e.activation(
                    out=xpe[:, j, :, 15:15 + L], in_=PS[b][:, 0:ntok].rearrange("p (s t) -> p s t", s=nseg), func=AF.Copy),
                    reads=[f"ps{b}"], writes=[f"xpe{j}"])
            fm_block(t, r, 512, ntok, ev_xp)
            for g in (3, 0, 1, 2):
                w = 2 << g
                eng = "pool" if g == 3 else "dve"
                ti = (2, 3) if g == 3 else (0, 1)
                src = xpe[:, g]
                cur = None
                curn = None
                sh = 1
                for step in range(g + 1):
                    dst = tb_[ti[step % 2]]; dstn = f"t{ti[step % 2]}"
                    a_in = src if cur is None else cur
                    a_n = f"xpe{g}" if cur is None else curn
                    R.op(eng, lambda e, dst=dst, a_in=a_in, sh=sh: e.tensor_tensor(
                        out=dst[:, :, sh:15 + L], in0=a_in[:, :, sh:15 + L], in1=a_in[:, :, 0:15 + L - sh], op=ALU.add),
                        reads=[a_n], writes=[dstn])
                    cur = dst; curn = dstn
                    sh *= 2
                if eng == "pool":
                    oth = tb_[ti[(g + 1) % 2]]; othn = f"t{ti[(g + 1) % 2]}"
                    R.op(eng, lambda e, cur=cur, oth=oth, w=w: e.tensor_scalar(out=oth[:, :, 15:15 + L], in0=cur[:, :, 15:15 + L],
                                                                               scalar1=1.0 / w, scalar2=None, op0=ALU.mult),
                         reads=[curn], writes=[othn])
                    R.op(eng, lambda e, oth=oth, g=g: e.tensor_tensor(out=dd[:, g], in0=oth[:, :, 15:15 + L],
                                                                      in1=xpe[:, g, :, 15:15 + L], op=ALU.subtract),
                         reads=[othn, f"xpe{g}"], writes=[f"dd{g}"])
                else:
                    R.op(eng, lambda e, cur=cur, g=g, w=w: e.scalar_tensor_tensor(
                        out=dd[:, g], in0=cur[:, :, 15:15 + L], scalar=1.0 / w, in1=xpe[:, g, :, 15:15 + L],
                        op0=ALU.mult, op1=ALU.subtract), reads=[curn, f"xpe{g}"], writes=[f"dd{g}"])
                if (not is_sample) and tix == 0:
                    R.op(eng, lambda e, cur=cur, g=g: e.tensor_tensor(out=cur[:, 0, 15:31], in0=cur[:, 0, 15:31],
                                                                      in1=invc[:, g, :], op=ALU.mult),
                         reads=[curn, "const", f"dd{g}"], writes=[curn])
                    R.op(eng, lambda e, cur=cur, g=g: e.tensor_tensor(out=dd[:, g, 0, 0:16], in0=cur[:, 0, 15:31],
                                                                      in1=xpe[:, g, 0, 15:31], op=ALU.subtract),
                         reads=[curn, f"xpe{g}"], writes=[f"dd{g}"])
            for g in range(4):
                b = next_ps()
                pe_ops([mm(PS[b][:, 0:ntok], poolw_sb[:, g, :], dd[:, g].rearrange("p s t -> p (s t)"), True, True)],
                       reads=[f"dd{g}", "lw"], writes=[f"ps{b}"])
                R.op("act", lambda e, g=g, b=b: e.activation(out=ypool[:, g, 0:ntok], in_=PS[b][:, 0:ntok], func=AF.Copy,
                                                             scale=cols[:, C_PS + g:C_PS + g + 1]),
                     reads=[f"ps{b}", "lw"], writes=["ypool"])
            if is_sample:
                for s in range(2):
                    R.dma("sp", lambda e, s=s: e.dma_start(out=poolsT[l, s].rearrange("(g p) t -> p g t", p=128),
                                                           in_=xpe[:, :, s, L:L + 15]), reads=XPE)
            else:
                R.op("dve", lambda e: e.tensor_copy(out=phist[:], in_=xpe[:, :, 0, L:L + 15]), reads=XPE, writes=["phist"])
                if tix == NT - 1:
                    R.dma("sp", lambda e: e.dma_start(out=poolpT[l].rearrange("(g p) t -> p g t", p=128),
                                                      in_=xpe[:, :, 0, L:L + 15]), reads=XPE)
            R.retire()

        if DBG < 3:
            mx.close()
            return
        with ExitStack() as pb_:
            def bsb(name, shape, dt=F32):
                return pb_.enter_context(sbt(name, list(shape), dt))
            qf = bsb("b_qf", [128, 4, TT]); kf = bsb("b_kf", [128, 4, TT])
            qn = bsb("b_qn", [128, 4, TT], BF16)
            rsb = [RSTD, bsb("b_rs2", [128, TT])]; rsn = ["rstd", "rs2"]
            vst = bsb("b_vst", [128, nsub, BW])
            QF = [f"qf{j}" for j in range(4)]; KF = [f"kf{j}" for j in range(4)]
            nstream = 1 if is_sample else 2
            strm = []
            for sid in range(nstream):
                strm.append({"sid": sid,
                             "e": [qf[:, 2 * sid + i, :] for i in range(2)], "en": [f"qf{2 * sid + i}" for i in range(2)],
                             "tmp": [kf[:, 2 * sid + i, :] for i in range(2)], "tn": [f"kf{2 * sid + i}" for i in range(2)],
                             "sp": [bsb(f"b_sp{sid}{i}", [128, 512], BF16) for i in range(2)],
                             "a": [bsb(f"b_a{sid}{i}", [128, 512], BF16) for i in range(2)],
                             "R": bsb(f"b_R{sid}", [128, 512])})
            t, r = load_w("w_in", l, O_Q)
            fm_block(t, r, 512, ntok, lambda j, b: R.op("act", lambda e, j=j, b=b: e.activation(
                out=qf[:, j, 0:ntok], in_=PS[b][:, 0:ntok], func=AF.Copy), reads=[f"ps{b}"], writes=[f"qf{j}"]))
            t, r = load_w("w_in", l, O_K)
            fm_block(t, r, 512, ntok, lambda j, b: R.op("act", lambda e, j=j, b=b: e.activation(
                out=kf[:, j, 0:ntok], in_=PS[b][:, 0:ntok], func=AF.Copy), reads=[f"ps{b}"], writes=[f"kf{j}"]))

            def qknorm(src, pre, ccol, outs):
                for j in range(4):
                    R.op("act", lambda e, j=j: e.activation(out=SQ[:, j % 2, 0:ntok], in_=src[:, j, 0:ntok], func=AF.Square),
                         reads=[f"{pre}{j}"], writes=[f"sq{j % 2}"])
                    b = next_ps()
                    pe_ops([mm(PS[b][:, 0:ntok], bdiag[:], SQ[:, j % 2, 0:ntok], True, True)], reads=[f"sq{j % 2}", "const2"],
                           writes=[f"ps{b}"])
                    R.op("dve", lambda e, b=b, j=j: e.tensor_scalar(out=rsb[j % 2][:, 0:ntok], in0=PS[b][:, 0:ntok],
                                                                    scalar1=1.0 / 64, scalar2=EPS, op0=ALU.mult, op1=ALU.add),
                         reads=[f"ps{b}"], writes=[rsn[j % 2]])
                    rsqrt_ip(rsb[j % 2][:, 0:ntok], rsn[j % 2])
                    outs(j)
            if is_sample:
                qz = bsb("b_qz", [128, 8, 2 * DEC], BF16)
                R.op("dve", lambda e: e.memset(qz[:], 0.0), writes=["qn"])
            def q_out(j):
                R.op("dve", lambda e, j=j: e.scalar_tensor_tensor(out=qf[:, j, 0:ntok], in0=qf[:, j, 0:ntok],
                                                                  scalar=cols[:, C_QN:C_QN + 1], in1=rsb[j % 2][:, 0:ntok],
                                                                  op0=ALU.mult, op1=ALU.mult),
                     reads=[f"qf{j}", rsn[j % 2], "lw"], writes=[f"qf{j}"])
                if is_sample:
                    for hh in range(2):
                        h = 2 * j + hh
                        R.op("act", lambda e, j=j, h=h, hh=hh: e.activation(
                            out=qz[hh * 64:hh * 64 + 64, h, :], in_=qf[hh * 64:hh * 64 + 64, j, 0:ntok], func=AF.Copy),
                            reads=[f"qf{j}"], writes=["qn"])
                else:
                    R.op("act", lambda e, j=j: e.activation(out=qn[:, j, 0:ntok], in_=qf[:, j, 0:ntok], func=AF.Copy),
                         reads=[f"qf{j}"], writes=["qn"])
            qknorm(qf, "qf", C_QN, q_out)
            def k_out(j):
                R.op("dve", lambda e, j=j: e.scalar_tensor_tensor(out=kf[:, j, 0:ntok], in0=kf[:, j, 0:ntok],
                                                                  scalar=cols[:, C_KN:C_KN + 1], in1=rsb[j % 2][:, 0:ntok],
                                                                  op0=ALU.mult, op1=ALU.mult),
                     reads=[f"kf{j}", rsn[j % 2], "lw"], writes=[f"kf{j}"])
            qknorm(kf, "kf", C_KN, k_out)
            t, r = load_w("w_in", l, O_V)
            tm_block(t, r, subs, lambda si, b, n: R.op("act", lambda e, si=si, b=b, n=n: e.activation(
                out=vst[0:n, si, :], in_=PS[b][0:n, :], func=AF.Copy), reads=[f"ps{b}"], writes=["vst"]))
            if is_sample:
                R.dma("sp", lambda e: e.dma_start(out=ksT[l].rearrange("(j p) t -> p j t", p=128), in_=kf[:, :, 0:ntok]),
                      reads=KF)
                for si in range(2):
                    R.dma("sp", lambda e, si=si: e.dma_start(out=vs[l, si * DEC:(si + 1) * DEC, :], in_=vst[0:DEC, si, :]),
                          reads=["vst"])
            else:
                R.dma("sp", lambda e: e.dma_start(out=kpT[l].rearrange("(j p) t -> p j t", p=128)[:, :, c0g:c0g + ntok],
                                                  in_=kf[:, :, 0:ntok]), reads=KF)
                R.dma("sp", lambda e: e.dma_start(out=vp[l, c0g:c0g + ntok, :].rearrange("(s p) f -> p s f", p=128),
                                                  in_=vst[:, :, :]), reads=["vst"])
            if is_sample:
                knew = bsb("b_knew", [128, 4, 2 * DEC], BF16)
                R.op("act", lambda e: e.activation(out=knew[:], in_=kf[:, :, 0:ntok], func=AF.Copy), reads=KF, writes=["knew"])
                for s in range(2):
                    for j in range(4):
                        R.dma("pool", lambda e, s=s, j=j: e.dma_start(out=KT[:, j, 0:PAST],
                                                                      in_=ckT[l, s, j * 128:(j + 1) * 128, :]),
                              writes=["KT"])
                    for c8 in range(0, NPC, 8):
                        c9 = min(NPC, c8 + 8)
                        R.dma("pool", lambda e, s=s, c8=c8, c9=c9: e.dma_start(
                            out=VC[:, c8:c9, :], in_=cv[l, s, c8 * 128:c9 * 128, :].rearrange("(c p) f -> p c f", p=128)),
                            writes=["VC"])
                    R.op("act", lambda e, s=s: e.activation(out=KT[:, :, PAST:PAST + DEC], in_=knew[:, :, s * DEC:(s + 1) * DEC],
                                                            func=AF.Copy), reads=["knew"], writes=["KT"])
                    R.op("act", lambda e, s=s: e.activation(out=VC[0:DEC, NPC, :], in_=vst[0:DEC, s, :], func=AF.Copy),
                         reads=["vst"], writes=["VC"])
                    strm[0]["groups"] = [(h // 2, None, h, (h // 2) * DEC, DEC, h * DEC, s * DEC) for h in range(8)]
                    strm[0]["chunks"] = [(NPC, PAST, DEC, negs[:, :])] + [(c, c * 128, 128, None) for c in range(NPC - 1, -1, -1)]
                    sb_attend(strm, 8 * DEC, qz)
                    R.op("act", lambda e, s=s: e.activation(
                        out=ysb[:, :, s * DEC:(s + 1) * DEC], in_=PS[6][:, 0:4 * DEC].rearrange("p (m t) -> p m t", m=4),
                        func=AF.Copy), reads=["ps6"], writes=["ysb"])
            else:
                pc0 = c0g // 128
                R.op("act", lambda e: e.activation(out=KT[:, :, c0g:c0g + ntok], in_=kf[:, :, 0:ntok], func=AF.Copy),
                     reads=KF, writes=["KT"])
                R.op("act", lambda e: e.activation(out=VC[:, pc0:pc0 + 4, :], in_=vst[:, :, :], func=AF.Copy),
                     reads=["vst"], writes=["VC"])
                chunks = []
                for c in range(pc0 + 3, -1, -1):
                    dgi = c - pc0
                    chunks.append((c, c * 128, 128, negp[:, dgi * 512:(dgi + 1) * 512] if dgi >= 0 else None))
                for m in range(4):
                    for hh in range(2):
                        strm[hh]["groups"] = [(m, hh * 64, 2 * m + hh, 0, TT, 0, 0)]
                        strm[hh]["chunks"] = chunks
                    sb_attend(strm, TT, qn)
                    R.op("act", lambda e, m=m: e.activation(out=ysb[:, m, 0:ntok], in_=PS[6][:, 0:ntok], func=AF.Copy),
                         reads=["ps6"], writes=["ysb"])
            R.retire()

        if DBG < 4:
            mx.close()
            return
        with ExitStack() as pc_:
            gu = pc_.enter_context(sbt("c_gu", [128, 4, TT], F32))
            gvb = pc_.enter_context(sbt("c_gvb", [128, nsub, BW], BF16))
            gvf = pc_.enter_context(sbt("c_gvf", [128, nsub, BW], F32))
            t, r = load_w("w_in", l, O_GU)
            fm_block(t, r, 512, ntok, lambda j, b: R.op("act", lambda e, j=j, b=b: e.activation(
                out=gu[:, j, 0:ntok], in_=PS[b][:, 0:ntok], func=AF.Gelu_apprx_tanh), reads=[f"ps{b}"], writes=["gu"]))
            t, r = load_w("w_in", l, O_GV)
            def ev_gv(si, b, n):
                R.op("act", lambda e: e.activation(out=gvf[0:n, si, :], in_=PS[b][0:n, :], func=AF.Gelu_apprx_tanh),
                     reads=[f"ps{b}"], writes=["gvf"])
                R.op("dve", lambda e: e.tensor_copy(out=gvb[0:n, si, :], in_=gvf[0:n, si, :]), reads=["gvf"], writes=["gvb"])
            tm_block(t, r, subs, ev_gv)
            if is_sample:
                for si in range(2):
                    R.dma("sp", lambda e, si=si: e.dma_start(out=gms[l, si * DEC:(si + 1) * DEC, :], in_=gvf[0:DEC, si, :]),
                          reads=["gvf"])
            for si, (c0, n) in enumerate(subs):
                b = next_ps()
                fns = []
                for g in range(4):
                    fns.append(mm(PS[b][:, g * 128:g * 128 + n], gvb[0:n, si, g * 128:(g + 1) * 128], wmT[0:n, g, 0:n],
                                  g == 0, False))
                    fns.append(mm(PS[b][:, g * 128:g * 128 + n], ones_bf[0:1, :], gb_hi[0:1, g * 128:g * 128 + n], False, False))
                    fns.append(mm(PS[b][:, g * 128:g * 128 + n], ones_bf[0:1, :], gb_lo[0:1, g * 128:g * 128 + n], False, g == 3))
                pe_ops(fns, reads=["gvb", "lw", "const2"], writes=[f"ps{b}"])
                R.op("dve", lambda e, b=b, c0=c0, n=n: e.tensor_tensor(
                    out=ygm[:, :, c0:c0 + n], in0=PS[b][:, :].rearrange("p (g i) -> p g i", g=4)[:, :, 0:n],
                    in1=gu[:, :, c0:c0 + n], op=ALU.mult), reads=[f"ps{b}", "gu"], writes=["ygm"])
            R.retire()

        if DBG < 5:
            mx.close()
            return
        with ExitStack() as pd_:
            def dsb(name, shape, dt=F32):
                return pd_.enter_context(sbt(name, list(shape), dt))
            lq = dsb("d_lq", [128, 2, TT]); lk = dsb("d_lk", [128, 2, TT])
            lvb = dsb("d_lvb", [128, nsub, BW], BF16)
            laT = dsb("d_la", [16, TT], BF16)
            lr = dsb("d_lr", [128, 4, TT], BF16)
            eg = dsb("d_eg", [128, 2, 256]); spg = dsb("d_spg", [128, 2, 256])
            eb = dsb("d_eb", [128, 2, TT])
            qt = dsb("d_qt", [128, 2, TT], BF16); kt = dsb("d_kt", [128, 2, TT], BF16)
            ktok = dsb("d_ktok", [128, nsub, 256], BF16)
            attb = [dsb(f"d_att{i}", [128, 128], BF16) for i in range(2)]
            oT = dsb("d_oT", [128, 4, TT])
            enb = oT[:, 0:2, :]
            ors2 = dsb("d_ors2", [128, TT])
            t, r = load_w("w_in", l, O_LQ)
            def ev_lqk(j, b):
                dst = lq if j < 2 else lk
                R.op("act", lambda e: e.activation(out=dst[:, j % 2, 0:ntok], in_=PS[b][:, 0:ntok], func=AF.Copy),
                     reads=[f"ps{b}"], writes=["lq" if j < 2 else "lk"])
            fm_block(t, r, 512, ntok, ev_lqk)
            t, r = load_w("w_in", l, O_LV)
            tm_block(t, r, subs, lambda si, b, n: R.op("act", lambda e, si=si, b=b, n=n: e.activation(
                out=lvb[0:n, si, :], in_=PS[b][0:n, :], func=AF.Copy), reads=[f"ps{b}"], writes=["lvb"]))
            b = next_ps()
            pe_ops([mm(PS[b][0:16, 0:ntok], wla_sb[:, k, :], hT[:, k, 0:ntok], k == 0, k == KD - 1) for k in range(KD)],
                   reads=HT_ALL + ["lw"], writes=[f"ps{b}"])
            R.op("act", lambda e, b=b: e.activation(out=laT[:, 0:ntok], in_=PS[b][0:16, 0:ntok], func=AF.Copy),
                 reads=[f"ps{b}"], writes=["laT"])
            t, r = load_w("w_in", l, O_LR)
            fm_block(t, r, 512, ntok, lambda j, b: R.op("act", lambda e, j=j, b=b: e.activation(
                out=lr[:, j, 0:ntok], in_=PS[b][:, 0:ntok], func=AF.Silu), reads=[f"ps{b}"], writes=["lr"]))
            for si, (c0, n) in enumerate(subs):
                b = next_ps()
                pe_ops([mm(PS[b][0:n, 0:256], laT[:, c0:c0 + n], wa2_sb[:, :], True, False),
                        mm(PS[b][0:n, 0:256], ones_bf[0:1, 0:n], ba_sb[0:1, :], False, True)],
                       reads=["laT", "lw", "const2"], writes=[f"ps{b}"])
                R.op("act", lambda e, si=si, b=b, n=n: e.activation(out=eg[0:n, si % 2, :], in_=PS[b][0:n, 0:256], func=AF.Exp,
                                                                    scale=-1.0), reads=[f"ps{b}"], writes=[f"eg{si % 2}"])
                R.op("act", lambda e, si=si, n=n: e.activation(out=spg[0:n, si % 2, :], in_=eg[0:n, si % 2, :], func=AF.Ln, bias=1.0,
                                                               scale=1.0), reads=[f"eg{si % 2}"], writes=[f"spg{si % 2}"])
                for fc in range(2):
                    b2 = next_ps()
                    pe_ops([mm(PS[b2][:, 0:n], spg[0:n, si % 2, fc * 128:(fc + 1) * 128], trile[0:n, 0:n], True, True)],
                           reads=[f"spg{si % 2}", "const"], writes=[f"ps{b2}"])
                    R.op("act", lambda e, fc=fc, b2=b2, c0=c0, n=n: e.activation(
                        out=eb[:, fc, c0:c0 + n], in_=PS[b2][:, 0:n], func=AF.Exp, scale=-1.0 / 16),
                        reads=[f"ps{b2}"], writes=["eb"])
                    R.op("act", lambda e, fc=fc, b2=b2, c0=c0, n=n: e.activation(
                        out=enb[:, fc, c0:c0 + n], in_=PS[b2][:, 0:n], func=AF.Exp, scale=1.0 / 16),
                        reads=[f"ps{b2}"], writes=["oT0", "oT1"])
            for fc in range(2):
                R.op("dve", lambda e, fc=fc: e.scalar_tensor_tensor(out=qt[:, fc, 0:ntok], in0=lq[:, fc, 0:ntok], scalar=0.125,
                                                                    in1=eb[:, fc, 0:ntok], op0=ALU.mult, op1=ALU.mult),
                     reads=["lq", "eb"], writes=["qt"])
                R.op("dve", lambda e, fc=fc: e.tensor_tensor(out=kt[:, fc, 0:ntok], in0=lk[:, fc, 0:ntok],
                                                             in1=enb[:, fc, 0:ntok], op=ALU.mult),
                     reads=["lk", "oT0", "oT1"], writes=["kt"])
            for si, (c0, n) in enumerate(subs):
                for fc in range(2):
                    pe_ops([lambda pe, fc=fc, c0=c0, n=n: pe.transpose(PSB[0:n, fc * 128:(fc + 1) * 128],
                                                                        kt[:, fc, c0:c0 + n], ident[:, :])],
                           reads=["kt", "const"], writes=["psb"])
                R.op("act", lambda e, si=si, n=n: e.activation(out=ktok[0:n, si, :], in_=PSB[0:n, 0:256], func=AF.Copy),
                     reads=["psb"], writes=["ktok"])
            for si, (c0, n) in enumerate(subs):
                seq = si if is_sample else 0
                if is_sample or (tix == 0 and si == 0):
                    if is_sample:
                        for hh in range(4):
                            fp = (hh % 2) * 64
                            R.dma("sp", lambda e, hh=hh, fp=fp, seq=seq: e.dma_start(
                                out=S32[fp:fp + 64, hh * 128:(hh + 1) * 128], in_=sgl[l, seq, hh]), writes=[f"S32{hh}"])
                        for hh in range(4):
                            fp = (hh % 2) * 64
                            R.op("act", lambda e, hh=hh, fp=fp: e.activation(out=Sbf[fp:fp + 64, hh * 128:(hh + 1) * 128],
                                                                             in_=S32[fp:fp + 64, hh * 128:(hh + 1) * 128],
                                                                             func=AF.Copy), reads=[f"S32{hh}"], writes=[f"Sbf{hh}"])
                    else:
                        R.op("dve", lambda e: e.memset(S32[:], 0.0), writes=[f"S32{h_}" for h_ in range(4)])
                        R.op("dve", lambda e: e.memset(Sbf[:], 0.0), writes=[f"Sbf{h_}" for h_ in range(4)])
                for hh in range(4):
                    fc = hh // 2; fp = (hh % 2) * 64
                    ab = hh % 2
                    ob = 2 + hh % 2
                    pe_ops([mm(PS[ab][0:n, 0:n], kt[fp:fp + 64, fc, c0:c0 + n], qt[fp:fp + 64, fc, c0:c0 + n], True, True)],
                           reads=["kt", "qt"], writes=[f"ps{ab}"])
                    R.op("dve", lambda e, ab=ab, n=n: e.tensor_tensor(out=attb[ab][0:n, 0:n], in0=PS[ab][0:n, 0:n],
                                                                      in1=trile[0:n, 0:n], op=ALU.mult),
                         reads=[f"ps{ab}", "const"], writes=[f"att{ab}"])
                    pe_ops([mm(PS[ob][:, 0:n], Sbf[fp:fp + 64, hh * 128:(hh + 1) * 128], qt[fp:fp + 64, fc, c0:c0 + n], True, False),
                            mm(PS[ob][:, 0:n], lvb[0:n, si, hh * 128:(hh + 1) * 128], attb[ab][0:n, 0:n], False, True)],
                           reads=[f"Sbf{hh}", "qt", "lvb", f"att{ab}"], writes=[f"ps{ob}"])
                    R.op("act", lambda e, hh=hh, ob=ob, c0=c0, n=n: e.activation(out=oT[:, hh, c0:c0 + n], in_=PS[ob][:, 0:n],
                                                                                func=AF.Copy), reads=[f"ps{ob}"], writes=[f"oT{hh}"])
                    db = 4 + hh % 2
                    pe_ops([mm(PS[db][fp:fp + 64, hh * 128:(hh + 1) * 128], ktok[0:n, si, fc * 128 + fp:fc * 128 + fp + 64],
                               lvb[0:n, si, hh * 128:(hh + 1) * 128], True, True)], reads=["ktok", "lvb"], writes=[f"ps{db}"])
                    R.op("dve", lambda e, hh=hh, fp=fp, db=db: e.tensor_tensor(
                        out=S32[fp:fp + 64, hh * 128:(hh + 1) * 128], in0=PS[db][fp:fp + 64, hh * 128:(hh + 1) * 128],
                        in1=S32[fp:fp + 64, hh * 128:(hh + 1) * 128], op=ALU.add), reads=[f"ps{db}", f"S32{hh}"], writes=[f"S32{hh}"])
                    R.op("dve", lambda e, hh=hh, fp=fp, fc=fc, c0=c0, n=n: e.tensor_scalar(
                        out=S32[fp:fp + 64, hh * 128:(hh + 1) * 128], in0=S32[fp:fp + 64, hh * 128:(hh + 1) * 128],
                        scalar1=eb[fp:fp + 64, fc, c0 + n - 1:c0 + n], scalar2=None, op0=ALU.mult),
                        reads=[f"S32{hh}", "eb"], writes=[f"S32{hh}"])
                    R.op("act", lambda e, hh=hh, fp=fp: e.activation(out=Sbf[fp:fp + 64, hh * 128:(hh + 1) * 128],
                                                                     in_=S32[fp:fp + 64, hh * 128:(hh + 1) * 128], func=AF.Copy),
                         reads=[f"S32{hh}"], writes=[f"Sbf{hh}"])
                if is_sample or (tix == NT - 1 and si == nsub - 1):
                    for hh in range(4):
                        fp = (hh % 2) * 64
                        dst = glas[l, seq, hh] if is_sample else glap[l, hh]
                        R.dma("sp", lambda e, hh=hh, fp=fp, dst=dst: e.dma_start(
                            out=dst, in_=S32[fp:fp + 64, hh * 128:(hh + 1) * 128]), reads=[f"S32{hh}"])
            orsb = [RSTD, ors2]; orsn = ["rstd", "ors2"]
            for hh in range(4):
                R.op("act", lambda e, hh=hh: e.activation(out=SQ[:, hh % 2, 0:ntok], in_=oT[:, hh, 0:ntok], func=AF.Square),
                     reads=[f"oT{hh}"], writes=[f"sq{hh % 2}"])
                b = next_ps()
                pe_ops([mm(PS[b][:, 0:ntok], ones_bf[:], SQ[:, hh % 2, 0:ntok], True, True)], reads=[f"sq{hh % 2}", "const2"],
                       writes=[f"ps{b}"])
                R.op("dve", lambda e, b=b, hh=hh: e.tensor_scalar(out=orsb[hh % 2][:, 0:ntok], in0=PS[b][:, 0:ntok],
                                                                  scalar1=1.0 / 128, scalar2=EPS, op0=ALU.mult, op1=ALU.add),
                     reads=[f"ps{b}"], writes=[orsn[hh % 2]])
                rsqrt_ip(orsb[hh % 2][:, 0:ntok], orsn[hh % 2])
                R.op("dve", lambda e, hh=hh: e.scalar_tensor_tensor(out=oT[:, hh, 0:ntok], in0=oT[:, hh, 0:ntok],
                                                                    scalar=cols[:, C_GON:C_GON + 1], in1=orsb[hh % 2][:, 0:ntok],
                                                                    op0=ALU.mult, op1=ALU.mult),
                     reads=[f"oT{hh}", orsn[hh % 2], "lw"], writes=[f"oT{hh}"])
                R.op("pool", lambda e, hh=hh: e.tensor_tensor(out=ygl[:, hh, 0:ntok], in0=oT[:, hh, 0:ntok],
                                                              in1=lr[:, hh, 0:ntok], op=ALU.mult),
                     reads=[f"oT{hh}", "lr"], writes=["ygl"])
            R.retire()

        if DBG < 6:
            mx.close()
            return
        with ExitStack() as pe_:
            acc = pe_.enter_context(sbt("e_acc", [128, 4, TT], F32))
            sg = [pe_.enter_context(sbt(f"e_sg{i}", [128, TT], F32)) for i in range(2)]
            tm = [pe_.enter_context(sbt(f"e_tm{i}", [128, TT], F32)) for i in range(2)]
            mg = pe_.enter_context(sbt("e_mg", [128, KD, TT], BF16))
            ybs = [ypool, ysb, ygm, ygl]
            ynm = ["ypool", "ysb", "ygm", "ygl"]
            ci = 0
            for nb in range(2):
                for bnum in range(4):
                    tb, rb = load_w("w_br", l, (bnum, nb))
                    tg, rg = load_w("w_in", l, O_G + bnum * D + nb * 512)
                    for j in range(4):
                        ip = next_ps(); ig = next_ps()
                        pe_ops([mm(PS[ip][:, 0:ntok], tb[:, k, j * 128:(j + 1) * 128], ybs[bnum][:, k, 0:ntok], k == 0, k == 3)
                                for k in range(4)], reads=[ynm[bnum], rb], writes=[f"ps{ip}"])
                        pe_ops([mm(PS[ig][:, 0:ntok], tg[:, k, j * 128:(j + 1) * 128], hT[:, k, 0:ntok], k == 0, k == KD - 1)
                                for k in range(KD)], reads=HT_ALL + [rg], writes=[f"ps{ig}"])
                        s_ = sg[ci % 2]; t_ = tm[ci % 2]; sn = f"sg{ci % 2}"; tn = f"tm{ci % 2}"; ci += 1
                        R.op("act", lambda e, s_=s_, ig=ig: e.activation(out=s_[:, 0:ntok], in_=PS[ig][:, 0:ntok], func=AF.Sigmoid),
                             reads=[f"ps{ig}"], writes=[sn])
                        if bnum == 0:
                            R.op("dve", lambda e, s_=s_, ip=ip, j=j: e.tensor_tensor(out=acc[:, j, 0:ntok], in0=PS[ip][:, 0:ntok],
                                                                                     in1=s_[:, 0:ntok], op=ALU.mult),
                                 reads=[f"ps{ip}", sn], writes=[f"acc{j}"])
                        else:
                            R.op("dve", lambda e, s_=s_, t_=t_, ip=ip: e.tensor_tensor(out=t_[:, 0:ntok], in0=PS[ip][:, 0:ntok],
                                                                                       in1=s_[:, 0:ntok], op=ALU.mult),
                                 reads=[f"ps{ip}", sn], writes=[tn])
                            if bnum < 3:
                                R.op("dve", lambda e, t_=t_, j=j: e.tensor_tensor(out=acc[:, j, 0:ntok], in0=acc[:, j, 0:ntok],
                                                                                  in1=t_[:, 0:ntok], op=ALU.add),
                                     reads=[tn, f"acc{j}"], writes=[f"acc{j}"])
                            else:
                                R.op("dve", lambda e, t_=t_, j=j, nb=nb: e.tensor_tensor(
                                    out=mg[:, nb * 4 + j, 0:ntok], in0=acc[:, j, 0:ntok], in1=t_[:, 0:ntok], op=ALU.add),
                                    reads=[tn, f"acc{j}"], writes=[f"mg{nb * 4 + j}"])
            for nb in range(2):
                t, r = load_w("w_o", l, nb)
                for j in range(4):
                    n = nb * 4 + j
                    b = next_ps()
                    pe_ops([mm(PS[b][:, 0:ntok], t[:, k, j * 128:(j + 1) * 128], mg[:, k, 0:ntok], k == 0, k == KD - 1)
                            for k in range(KD)], reads=[f"mg{k}" for k in range(KD)] + [r], writes=[f"ps{b}"])
                    R.op("dve", lambda e, n=n, b=b: e.tensor_tensor(out=xT[:, n, 0:ntok], in0=PS[b][:, 0:ntok],
                                                                    in1=xT[:, n, 0:ntok], op=ALU.add),
                         reads=[f"ps{b}", "xT"], writes=["xT"])
            R.retire()
        mx.close()

    def ple(l, tile):
        ntok, subs, segs, c0g, is_sample, tix = tile
        cg = (S if is_sample else c0g)
        with ExitStack() as pp_:
            ph = None
            sg = [pp_.enter_context(sbt(f"p_sg{i}", [128, TT], F32)) for i in range(2)]
            tm = [pp_.enter_context(sbt(f"p_tm{i}", [128, TT], F32)) for i in range(2)]
            R.dma("pool", lambda e: e.dma_start(out=pT[:, :, 0:ntok],
                                                in_=pin[l].rearrange("(k p) t -> p k t", p=128)[:, :, cg:cg + ntok]),
                  writes=["pT"])
            rmsnorm(ntok, C_NPL, ph)
            ci = 0
            for nb in range(2):
                t, r = load_w("w_pg", l, nb)
                for j in range(4):
                    n = nb * 4 + j
                    ig = next_ps(); ip = next_ps()
                    pe_ops([mm(PS[ig][:, 0:ntok], t[:, k, j * 128:(j + 1) * 128], hT[:, k, 0:ntok], k == 0, k == KD - 1)
                            for k in range(KD)], reads=HT_ALL + [r], writes=[f"ps{ig}"])
                    pe_ops([mm(PS[ip][:, 0:ntok], wpp_sb[:, k, n * 128:(n + 1) * 128], pT[:, k, 0:ntok], k == 0, k == 1)
                            for k in range(2)], reads=["pT", "lw"], writes=[f"ps{ip}"])
                    s_ = sg[ci % 2]; t_ = tm[ci % 2]; sn = f"sg{ci % 2}"; tn = f"tm{ci % 2}"; ci += 1
                    R.op("act", lambda e, s_=s_, ig=ig: e.activation(out=s_[:, 0:ntok], in_=PS[ig][:, 0:ntok], func=AF.Sigmoid),
                         reads=[f"ps{ig}"], writes=[sn])
                    R.op("dve", lambda e, s_=s_, t_=t_, ip=ip: e.tensor_tensor(out=t_[:, 0:ntok], in0=PS[ip][:, 0:ntok],
                                                                               in1=s_[:, 0:ntok], op=ALU.mult),
                         reads=[f"ps{ip}", sn], writes=[tn])
                    R.op("dve", lambda e, t_=t_, n=n: e.tensor_tensor(out=xT[:, n, 0:ntok], in0=xT[:, n, 0:ntok],
                                                                      in1=t_[:, 0:ntok], op=ALU.add),
                         reads=[tn, "xT"], writes=["xT"])
            R.retire()

    tiles = [(2 * DEC, [(0, DEC), (DEC, DEC)], [(0, DEC), (DEC, DEC)], S, True, 0)]
    for t_ in range(NT):
        tiles.append((TT, [(i * 128, 128) for i in range(4)], [(0, TT)], t_ * TT, False, t_))

    convert_layer(0)
    for l in range(DEPTH):
        R.dma("sp", lambda e, l=l: e.dma_start(out=cols[:], in_=cols_d[l]), writes=["lw"])
        R.dma("pool", lambda e, l=l: e.dma_start(out=poolw_sb[:], in_=pool_w[l].rearrange("g c d -> c g d")), writes=["lw"])
        R.dma("pool", lambda e, l=l: e.dma_start(out=wmT[:], in_=wsT_d[l].rearrange("g j i -> j g i")), writes=["lw"])
        R.dma("sp", lambda e, l=l: e.dma_start(out=gb_f[:], in_=gb_d[l]), writes=["lw"])
        R.dma("pool", lambda e, l=l: e.dma_start(out=wa2_sb[:], in_=wa2_d[l]), writes=["lw"])
        R.dma("pool", lambda e, l=l: e.dma_start(out=ba_sb[:], in_=ba_d[l]), writes=["lw"])
        R.dma("pool", lambda e, l=l: e.dma_start(out=wla_sb[:], in_=w_in[l][:, O_LA:O_LA + 16].rearrange("(k p) n -> p k n", p=128)),
              writes=["lw"])
        R.dma("pool", lambda e, l=l: e.dma_start(out=wpp_sb[:], in_=w_pp[l].rearrange("(k p) n -> p k n", p=128)), writes=["lw"])
        for g in range(4):
            R.op("dve", lambda e, g=g: e.tensor_tensor(out=wmT[:, g, :], in0=wmT[:, g, :], in1=bmask[:], op=ALU.mult),
                 reads=["lw", "const"], writes=["lw"])
        R.op("act", lambda e: e.activation(out=gb_hi[:], in_=gb_f[:], func=AF.Copy), reads=["lw"], writes=["lw2"])
        R.op("dve", lambda e: e.tensor_tensor(out=gb_lo[:], in0=gb_f[:], in1=gb_hi[:], op=ALU.subtract), reads=["lw", "lw2"],
             writes=["lw3"])
        R.op("dve", lambda e: e.memset(phist[:], 0.0), writes=["phist"])
        R.barrier(("pe", "act", "dve", "sp", "pool"))
        if l + 1 < DEPTH:
            convert_layer(l + 1)
        for tile in tiles:
            ntok, subs, segs, c0g, is_sample, tix = tile
            cg = S if is_sample else c0g
            src = xin if l == 0 else xscr
            dst = yout if l == DEPTH - 1 else xscr
            rname = f"xd{cg}"
            R.dma("sp", lambda e, src=src, cg=cg, ntok=ntok: e.dma_start(
                out=xT[:, :, 0:ntok], in_=src.rearrange("(k p) t -> p k t", p=128)[:, :, cg:cg + ntok]),
                reads=[rname], writes=["xT"])
            if DBG >= 1:
                ffn(l, ntok, "f1a", "f1b", "f1c", C_N1)
            if DBG >= 2:
                mixer(l, tile)
            if DBG >= 8:
                ffn(l, ntok, "f2a", "f2b", "f2c", C_N2)
            if DBG >= 9:
                ple(l, tile)
            R.dma("sp", lambda e, dst=dst, cg=cg, ntok=ntok: e.dma_start(
                out=dst.rearrange("(k p) t -> p k t", p=128)[:, :, cg:cg + ntok], in_=xT[:, :, 0:ntok]),
                reads=["xT"], writes=[rname])
    R.barrier(("pe", "act", "dve", "sp", "pool"))

    sems = {}
    for e in R.engs:
        sems[e] = es.enter_context(nc.semaphore(f"c_{e}"))
    for q in ("sp", "pool"):
        for i in range(R.dma_nsem[q]):
            sems[("dma", q, i)] = es.enter_context(nc.semaphore(f"d_{q}{i}"))
    block = es.enter_context(nc.Block())

    def replay(eng, name):
        for waits, fn, inc in R.ops[name]:
            for k, v in waits:
                eng.wait_ge(sems[k], v)
            if fn is None:
                continue
            ins = fn(eng)
            if inc[0] == "cnt":
                ins.then_inc(sems[inc[1]], 1)
            else:
                ins.then_inc(sems[inc[1]], 16)

    block.tensor(lambda e: replay(e, "pe"))
    block.scalar(lambda e: replay(e, "act"))
    block.vector(lambda e: replay(e, "dve"))
    block.gpsimd(lambda e: replay(e, "pool"))
    block.sync(lambda e: replay(e, "sp"))
    es.close()
    return nc


def make_consts():
    i = np.arange(128)
    c = {}
    c["c_ident"] = np.eye(128, dtype=np.float32)
    c["c_tinc"] = (i[:, None] >= i[None, :]).astype(np.float32)
    c["c_trile"] = (i[:, None] <= i[None, :]).astype(np.float32)
    q = np.arange(512)
    negp = np.zeros((128, 4, 512), np.float32)
    for d in range(4):
        negp[:, d, :] = np.where((i[:, None] + 128 * d) < q[None, :], 0.0, NEGV)
    c["c_negp"] = negp.reshape(128, 2048)
    j = np.arange(32)
    ns = np.where(j[:, None] < j[None, :], 0.0, NEGV).astype(np.float32)
    c["c_negs"] = np.tile(ns, (1, 8))
    c["c_bmask"] = ((i[:, None] // 64) <= (i[None, :] // 64)).astype(np.float32)
    invc = np.zeros((128, 4, 16), np.float32)
    for g in range(4):
        invc[:, g, :] = (1.0 / np.minimum(2 << g, q[:16] + 1)).astype(np.float32)[None, :]
    c["c_invc"] = invc.reshape(128, 64)
    return c


def layout_inputs(inp, S, PAST, DEPTH, n_cores):
    f = lambda a: np.ascontiguousarray(a, dtype=np.float32)
    shared = {}
    for src, dst in [("ffn1_w1", "f1a"), ("ffn1_w3", "f1b"), ("ffn1_w2", "f1c"), ("ffn2_w1", "f2a"), ("ffn2_w3", "f2b"),
                     ("ffn2_w2", "f2c"), ("w_in", "w_in"), ("pool_w", "pool_w"), ("gla_wa2", "wa2"), ("w_branch", "w_br"),
                     ("w_out", "w_o"), ("ple_w_gate", "w_pg"), ("ple_w_proj", "w_pp")]:
        shared[dst] = f(inp[src])
    shared["wsT"] = f(np.transpose(inp["gmlp_ws"], (0, 1, 3, 2)))
    shared["gb"] = f(np.reshape(inp["gmlp_b"], (DEPTH, 1, 512)))
    shared["ba"] = f(np.reshape(inp["gla_ba"], (DEPTH, 1, 256)))
    cols = np.zeros((DEPTH, 128, NCOL), np.float32)
    for nm, c0 in [("norm_ffn1", C_N1), ("norm_mix", C_NM), ("norm_ffn2", C_N2), ("norm_ple", C_NPL)]:
        cols[:, :, c0:c0 + 8] = np.transpose(np.reshape(inp[nm], (DEPTH, 8, 128)), (0, 2, 1))
    cols[:, :, C_PS:C_PS + 4] = np.transpose(np.reshape(inp["pool_scale"], (DEPTH, 4, 128)), (0, 2, 1))
    cols[:, :, C_QN] = np.tile(inp["sb_q_norm"], (1, 2))
    cols[:, :, C_KN] = np.tile(inp["sb_k_norm"], (1, 2))
    cols[:, :, C_GON] = inp["gla_out_norm"]
    shared["cols"] = cols
    shared.update(make_consts())
    maps = []
    for c in range(n_cores):
        m = dict(shared)
        xs = np.reshape(inp["x_sample"][2 * c:2 * c + 2], (2 * DEC, D))
        m["xin"] = f(np.concatenate([inp["x_prompt"][c].T, xs.T], axis=1))
        ps = np.reshape(inp["p_sample"][:, 2 * c:2 * c + 2], (DEPTH, 2 * DEC, PLE))
        m["pin"] = f(np.concatenate([np.transpose(inp["p_prompt"][:, c], (0, 2, 1)), np.transpose(ps, (0, 2, 1))], axis=2))
        m["ckT"] = f(np.transpose(np.reshape(inp["cache_sb_k"][:, 2 * c:2 * c + 2], (DEPTH, 2, PAST, BW)), (0, 1, 3, 2)))
        m["cv"] = f(np.reshape(inp["cache_sb_v"][:, 2 * c:2 * c + 2], (DEPTH, 2, PAST, BW)))
        m["spT"] = f(np.transpose(inp["state_pool"][:, 2 * c:2 * c + 2], (0, 1, 3, 2)))
        m["sgl"] = f(inp["state_gla"][:, 2 * c:2 * c + 2])
        maps.append(m)
    return maps


def assemble(results, S, DEPTH, n_cores):
    B = n_cores
    yp = np.zeros((B, S, D), np.float32); ys = np.zeros((2 * B, DEC, D), np.float32)
    kp = np.zeros((DEPTH, B, S, 8, 64), np.float32); vpo = np.zeros((DEPTH, B, S, 8, 64), np.float32)
    pp = np.zeros((DEPTH, B, 15, BW), np.float32); gp = np.zeros((DEPTH, B, 4, 64, 128), np.float32)
    ks = np.zeros((DEPTH, 2 * B, DEC, 8, 64), np.float32); vso = np.zeros((DEPTH, 2 * B, DEC, 8, 64), np.float32)
    pso = np.zeros((DEPTH, 2 * B, 15, BW), np.float32); gs = np.zeros((DEPTH, 2 * B, 4, 64, 128), np.float32)
    gm = np.zeros((DEPTH, 2 * B, DEC, BW), np.float32)
    for c, r in enumerate(results):
        yo = r["yout"]
        yp[c] = yo[:, :S].T
        ys[2 * c:2 * c + 2] = yo[:, S:].T.reshape(2, DEC, D)
        kp[:, c] = np.transpose(r["kpT"], (0, 2, 1)).reshape(DEPTH, S, 8, 64)
        vpo[:, c] = r["vp"].reshape(DEPTH, S, 8, 64)
        pp[:, c] = np.transpose(r["poolpT"], (0, 2, 1))
        gp[:, c] = r["glap"]
        ks[:, 2 * c:2 * c + 2] = np.transpose(r["ksT"], (0, 2, 1)).reshape(DEPTH, 2, DEC, 8, 64)
        vso[:, 2 * c:2 * c + 2] = r["vs"].reshape(DEPTH, 2, DEC, 8, 64)
        pso[:, 2 * c:2 * c + 2] = np.transpose(r["poolsT"], (0, 1, 3, 2))
        gs[:, 2 * c:2 * c + 2] = r["glas"]
        gm[:, 2 * c:2 * c + 2] = r["gms"].reshape(DEPTH, 2, DEC, BW)
    return (yp, ys, kp, vpo, pp, gp, ks, vso, pso, gs, gm)


def run(inp, S, PAST, DEPTH, n_cores=8):
    nc = build_program(S, PAST, DEPTH)
    maps = layout_inputs(inp, S, PAST, DEPTH, n_cores)
    res = run_bass_kernel_spmd(nc, maps, core_ids=list(range(n_cores)))
    return assemble(res.results, S, DEPTH, n_cores)


def kernel(**inputs):
    inp = {k: np.asarray(v) for k, v in inputs.items()}
    S = inp["x_prompt"].shape[1]
    PAST = inp["cache_sb_k"].shape[2]
    DEPTH = inp["w_in"].shape[0]
    return run(inp, S, PAST, DEPTH, 8)
```

```python
import numpy as np
from contextlib import ExitStack
import concourse.bass as bass
import concourse.mybir as mybir
from concourse.bass_utils import run_bass_kernel_spmd

F32 = mybir.dt.float32
BF16 = mybir.dt.bfloat16
AF = mybir.ActivationFunctionType
ALU = mybir.AluOpType

D = 1024
KD = 8
DFF = 2816
PLE = 256
BW = 512
NCOLS_IN = 8720
EPS = 1e-6
TT = 512
DEC = 32
NEGV = -30000.0
O_XP, O_Q, O_K, O_V, O_GU, O_GV, O_LQ, O_LK, O_LV, O_LA, O_LR, O_G = (
    0, 512, 1024, 1536, 2048, 2560, 3072, 3328, 3584, 4096, 4112, 4624)
NSLOT = 4
import os
DBG = int(os.environ.get("KDBG", "99"))
NCOL = 39
C_N1, C_NM, C_N2, C_NPL, C_PS, C_QN, C_KN, C_GON = 0, 8, 16, 24, 32, 36, 37, 38


class Rec:
    def __init__(self):
        self.engs = ["pe", "act", "dve", "pool", "sp"]
        self.ops = {e: [] for e in self.engs}
        self.count = {e: 0 for e in self.engs}
        self.waited = {e: {} for e in self.engs}
        self.res = {}
        self.dma_n = {"sp": 0, "pool": 0}
        self.dma_nsem = {"sp": 24, "pool": 68}
        self.dma_events = []
        self.pending = {}

    PERSIST = ("xT", "hT", "KT", "VC", "ws", "lw", "const", "phist", "S32", "Sbf", "ypool", "ysb", "ygm", "ygl",
               "sq0", "sq1", "rstd", "pT", "ps", "xd")

    def retire(self):
        for name in list(self.res.keys()):
            if name.startswith(self.PERSIST):
                continue
            w, rs = self.res.pop(name)
            for ev in ([w] if w else []) + rs:
                k, v = ev
                if self.pending.get(k, 0) < v:
                    self.pending[k] = v

    def _r(self, name):
        if name not in self.res:
            self.res[name] = [None, []]
        return self.res[name]

    def _deps(self, eng, reads, writes):
        deps = {}
        def add(ev):
            if ev is None:
                return
            k, v = ev
            if deps.get(k, 0) < v:
                deps[k] = v
        for nm in list(reads) + list(writes):
            if nm not in self.res and not nm.startswith(self.PERSIST):
                for k, v in self.pending.items():
                    add((k, v))
                break
        for r in reads:
            add(self._r(r)[0])
        for w in writes:
            e = self._r(w)
            add(e[0])
            for ev in e[1]:
                add(ev)
        waits = []
        for k, v in deps.items():
            if k == "pe" and eng == "pe":
                continue
            if self.waited[eng].get(k, 0) >= v:
                continue
            self.waited[eng][k] = v
            waits.append((k, v))
        return waits

    def _commit(self, ev, reads, writes):
        for r in reads:
            self._r(r)[1].append(ev)
        for w in writes:
            e = self._r(w)
            e[0] = ev
            e[1] = []

    def op(self, eng, fn, reads=(), writes=()):
        waits = self._deps(eng, reads, writes)
        self.count[eng] += 1
        ev = (eng, self.count[eng])
        self.ops[eng].append((waits, fn, ("cnt", eng)))
        self._commit(ev, reads, writes)

    def dma(self, q, fn, reads=(), writes=()):
        waits = self._deps(q, reads, writes)
        i = self.dma_n[q]
        self.dma_n[q] += 1
        n = self.dma_nsem[q]
        key = ("dma", q, i % n)
        if i >= n:
            prev = 16 * (i // n)
            if self.waited[q].get(key, 0) < prev:
                self.waited[q][key] = prev
                waits.append((key, prev))
        ev = (key, 16 * (i // n + 1))
        self.ops[q].append((waits, fn, ("dma", key)))
        self._commit(ev, reads, writes)
        self.dma_events.append(ev)
        return ev

    def fence(self, eng, events):
        waits = []
        for k, v in events:
            if self.waited[eng].get(k, 0) >= v:
                continue
            self.waited[eng][k] = v
            waits.append((k, v))
        if waits:
            self.ops[eng].append((waits, None, None))

    def barrier(self, engs=("pe", "act", "dve", "sp")):
        evs = [(e, self.count[e]) for e in ("pe", "act", "dve", "pool") if self.count[e] > 0]
        evs += self.dma_events
        self.dma_events = []
        for e in engs:
            waits = []
            for k, v in evs:
                if k == e:
                    continue
                if self.waited[e].get(k, 0) >= v:
                    continue
                self.waited[e][k] = v
                waits.append((k, v))
            if waits:
                self.ops[e].append((waits, None, None))

    def drop(self, names):
        for n in names:
            self.res.pop(n, None)


def build_program(S, PAST, DEPTH):
    NT = S // TT
    NTOK = S + 2 * DEC
    NPC = PAST // 128
    KTW = max(S, PAST + DEC)
    NVC = max(S // 128, NPC + 1)
    nc = bass.Bass("TRN2", target_bir_lowering=False)

    def din(name, shape):
        return nc.dram_tensor(name, list(shape), F32, kind="ExternalInput").ap()

    def dout(name, shape):
        return nc.dram_tensor(name, list(shape), F32, kind="ExternalOutput").ap()

    xin = din("xin", [D, NTOK])
    pin = din("pin", [DEPTH, PLE, NTOK])
    ckT = din("ckT", [DEPTH, 2, BW, PAST])
    cv = din("cv", [DEPTH, 2, PAST, BW])
    spT = din("spT", [DEPTH, 2, BW, 15])
    sgl = din("sgl", [DEPTH, 2, 4, 64, 128])
    cols_d = din("cols", [DEPTH, 128, NCOL])
    w_f1a = din("f1a", [DEPTH, D, DFF]); w_f1b = din("f1b", [DEPTH, D, DFF]); w_f1c = din("f1c", [DEPTH, DFF, D])
    w_f2a = din("f2a", [DEPTH, D, DFF]); w_f2b = din("f2b", [DEPTH, D, DFF]); w_f2c = din("f2c", [DEPTH, DFF, D])
    w_in = din("w_in", [DEPTH, D, NCOLS_IN])
    pool_w = din("pool_w", [DEPTH, 4, 128, 128])
    wsT_d = din("wsT", [DEPTH, 4, 128, 128])
    gb_d = din("gb", [DEPTH, 1, 512])
    wa2_d = din("wa2", [DEPTH, 16, 256])
    ba_d = din("ba", [DEPTH, 1, 256])
    w_br = din("w_br", [DEPTH, 4, BW, D])
    w_o = din("w_o", [DEPTH, D, D])
    w_pg = din("w_pg", [DEPTH, D, D])
    w_pp = din("w_pp", [DEPTH, PLE, D])
    c_ident = din("c_ident", [128, 128])
    c_tinc = din("c_tinc", [128, 128])
    c_trile = din("c_trile", [128, 128])
    c_negp = din("c_negp", [128, 4 * 512])
    c_negs = din("c_negs", [32, 256])
    c_bmask = din("c_bmask", [128, 128])
    c_invc = din("c_invc", [128, 4 * 16])

    yout = dout("yout", [D, NTOK])
    kpT = dout("kpT", [DEPTH, BW, S]); vp = dout("vp", [DEPTH, S, BW])
    poolpT = dout("poolpT", [DEPTH, BW, 15]); glap = dout("glap", [DEPTH, 4, 64, 128])
    ksT = dout("ksT", [DEPTH, BW, 2 * DEC]); vs = dout("vs", [DEPTH, 2 * DEC, BW])
    poolsT = dout("poolsT", [DEPTH, 2, BW, 15]); glas = dout("glas", [DEPTH, 2, 4, 64, 128])
    gms = dout("gms", [DEPTH, 2 * DEC, BW])
    xscr = nc.dram_tensor("xscr", [D, NTOK], F32, kind="Internal").ap()

    R = Rec()
    es = ExitStack()
    uniq = [0]

    def sbt(name, shape, dt=F32):
        uniq[0] += 1
        return nc.sbuf_tensor(f"{name}_{uniq[0]}", list(shape), dt)

    def sb(name, shape, dt=F32):
        return es.enter_context(sbt(name, list(shape), dt))

    xT = sb("xT", [128, KD, TT])
    hT = sb("hT", [128, KD, TT], BF16)
    KT = sb("KT", [128, 4, KTW], BF16)
    VC = sb("VC", [128, NVC, BW], BF16)
    WS = [sb(f"ws{i}", [128, 8, 512], BF16) for i in range(NSLOT)]
    cols = sb("cols_sb", [128, NCOL])
    poolw_sb = sb("poolw_sb", [128, 4, 128], BF16)
    wmT = sb("wmT", [128, 4, 128], BF16)
    gb_hi = sb("gb_hi", [1, 512], BF16); gb_lo = sb("gb_lo", [1, 512], BF16); gb_f = sb("gb_f", [1, 512])
    wa2_sb = sb("wa2_sb", [16, 256], BF16)
    ba_sb = sb("ba_sb", [1, 256], BF16)
    wla_sb = sb("wla_sb", [128, 8, 16], BF16)
    wpp_sb = sb("wpp_sb", [128, 2, D], BF16)
    ident = sb("ident", [128, 128], BF16)
    tinc = sb("tinc", [128, 128], BF16)
    ones_bf = sb("ones_bf", [128, 128], BF16)
    bdiag = sb("bdiag", [128, 128], BF16)
    trile = sb("trile", [128, 128])
    negp = sb("negp", [128, 4 * 512], BF16)
    negs = sb("negs", [32, 256], BF16)
    bmask = sb("bmask", [128, 128], BF16)
    invc = sb("invc", [128, 4, 16])
    phist = sb("phist", [128, 4, 15])
    S32 = sb("S32", [128, 512]); Sbf = sb("Sbf", [128, 512], BF16)
    SQ = sb("SQ", [128, 2, TT], BF16); RSTD = sb("RSTD", [128, TT])
    pT = sb("pT", [128, 2, TT], BF16)
    nident = sb("nident", [128, 128], BF16)
    ypool = sb("ypool", [128, 4, TT], BF16); ysb = sb("ysb", [128, 4, TT], BF16)
    ygm = sb("ygm", [128, 4, TT], BF16); ygl = sb("ygl", [128, 4, TT], BF16)
    PS = [es.enter_context(nc.psum_tensor(f"ps{i}", [128, 512], F32)) for i in range(7)]
    PSB = es.enter_context(nc.psum_tensor("psb", [128, 1024], BF16))

    slot_ctr = [0]

    WIN_OFFS = [O_XP, O_Q, O_K, O_V, O_GU, O_GV, O_LQ, O_LV, O_LR] + [O_G + i * 512 for i in range(8)]
    FBLK = [(cb * 512, min(512, DFF - cb * 512)) for cb in range((DFF + 511) // 512)]
    KBS = [(0, 8), (8, 8), (16, 6)]
    wsrc = {"f1a": w_f1a, "f1b": w_f1b, "f1c": w_f1c, "f2a": w_f2a, "f2b": w_f2b, "f2c": w_f2c, "w_in": w_in,
            "w_br": w_br, "w_o": w_o, "w_pg": w_pg}
    wblocks = {}
    for nm in ("f1a", "f1b", "f2a", "f2b"):
        wblocks[nm] = [(cb, (lambda l, nm=nm, c0=c0, nco=nco: wsrc[nm][l][:, c0:c0 + nco]), 8, nco)
                       for cb, (c0, nco) in enumerate(FBLK)]
    for nm in ("f1c", "f2c"):
        wblocks[nm] = [((nb, bi), (lambda l, nm=nm, nb=nb, k0=k0, kn=kn: wsrc[nm][l][k0 * 128:(k0 + kn) * 128,
                                                                                      nb * 512:(nb + 1) * 512]), kn, 512)
                       for nb in range(2) for bi, (k0, kn) in enumerate(KBS)]
    wblocks["w_in"] = [(off, (lambda l, off=off: w_in[l][:, off:off + 512]), 8, 512) for off in WIN_OFFS]
    wblocks["w_br"] = [((b_, nb), (lambda l, b_=b_, nb=nb: w_br[l, b_][:, nb * 512:(nb + 1) * 512]), 4, 512)
                       for b_ in range(4) for nb in range(2)]
    for nm in ("w_o", "w_pg"):
        wblocks[nm] = [(nb, (lambda l, nm=nm, nb=nb: wsrc[nm][l][:, nb * 512:(nb + 1) * 512]), 8, 512) for nb in range(2)]
    wscr = {}
    widx = {}
    for nm, bl in wblocks.items():
        wscr[nm] = nc.dram_tensor(f"{nm}_bf", [DEPTH, len(bl), 128, 8, 512], BF16, kind="Internal").ap()
        widx[nm] = {b[0]: (i, b[2], b[3]) for i, b in enumerate(bl)}
    conv_ev = {}

    conv_q = []

    def convert_layer(l, defer=False):
        for nm in ("f1a", "f1b", "f1c", "w_in", "w_br", "w_o", "f2a", "f2b", "f2c", "w_pg"):
            conv_ev[(nm, l)] = []
            for i, (idx, srcf, kcs, nco) in enumerate(wblocks[nm]):
                src = srcf(l).rearrange("(kc p) n -> p kc n", p=128)
                dst = wscr[nm][l, i][:, 0:kcs, 0:nco]
                conv_q.append((nm, l, dst, src))
        if not defer:
            conv_step(len(conv_q))

    def conv_step(n=1):
        for _ in range(min(n, len(conv_q))):
            nm, l, dst, src = conv_q.pop(0)
            conv_ev[(nm, l)].append(R.dma("pool", lambda e, dst=dst, src=src: e.dma_start(out=dst, in_=src)))

    def load_w(nm, l, idx):
        i, kcs, nco = widx[nm][idx]
        s = slot_ctr[0] % NSLOT
        slot_ctr[0] += 1
        t = WS[s]
        R.fence("sp", conv_ev[(nm, l)])
        src = wscr[nm][l, i][:, 0:kcs, 0:nco]
        R.dma("sp", lambda e, t=t, src=src, kcs=kcs, nco=nco: e.dma_start(out=t[:, 0:kcs, 0:nco], in_=src),
              writes=[f"ws{s}"])
        return t, f"ws{s}"

    def mm(out, lhsT, rhs, start, stop):
        return lambda pe: pe.matmul(out, lhsT, rhs, start=start, stop=stop, skip_group_check=True)

    def pe_ops(fns, reads, writes):
        def run(pe, fns=fns):
            last = None
            for f in fns:
                last = f(pe)
            return last
        R.op("pe", run, reads=reads, writes=writes)

    R.dma("pool", lambda e: e.dma_start(out=ident[:], in_=c_ident), writes=["const"])
    R.dma("pool", lambda e: e.dma_start(out=tinc[:], in_=c_tinc), writes=["const"])
    R.dma("pool", lambda e: e.dma_start(out=negp[:], in_=c_negp), writes=["const"])
    R.dma("pool", lambda e: e.dma_start(out=negs[:], in_=c_negs), writes=["const"])
    R.dma("pool", lambda e: e.dma_start(out=bmask[:], in_=c_bmask), writes=["const"])
    R.dma("sp", lambda e: e.dma_start(out=trile[:], in_=c_trile), writes=["const"])
    R.dma("sp", lambda e: e.dma_start(out=invc[:], in_=c_invc.rearrange("p (g t) -> p g t", g=4)), writes=["const"])
    R.op("dve", lambda e: e.memset(ones_bf[:], 1.0), writes=["const2"])
    R.op("dve", lambda e: e.memset(bdiag[:], 0.0), writes=["const2"])
    R.op("dve", lambda e: e.memset(bdiag[0:64, 0:64], 1.0), writes=["const2"])
    R.op("dve", lambda e: e.memset(bdiag[64:128, 64:128], 1.0), writes=["const2"])
    R.op("dve", lambda e: e.tensor_scalar(out=nident[:], in0=ident[:], scalar1=-1.0, scalar2=None, op0=ALU.mult),
         reads=["const"], writes=["const2"])
    R.barrier(("pe", "act", "dve", "sp", "pool"))

    def rsqrt_ip(ap, rn="rstd"):
        R.op("act", lambda e: e.activation(out=ap, in_=ap, func=AF.Ln), reads=[rn], writes=[rn])
        R.op("act", lambda e: e.activation(out=ap, in_=ap, func=AF.Exp, scale=-0.5), reads=[rn], writes=[rn])

    def rmsnorm(ntok, ccol, ph):
        rstd = RSTD
        for k in range(KD):
            if k % 2 == 0:
                R.op("act", lambda e, k=k: e.activation(out=SQ[:, 0, 0:ntok], in_=xT[:, k, 0:ntok], func=AF.Square),
                     reads=["xT"], writes=["sq0"])
            else:
                R.op("pool", lambda e, k=k: e.tensor_tensor(out=SQ[:, 1, 0:ntok], in0=xT[:, k, 0:ntok], in1=xT[:, k, 0:ntok],
                                                            op=ALU.mult), reads=["xT"], writes=["sq1"])
            pe_ops([mm(PS[6][:, 0:ntok], ones_bf[:], SQ[:, k % 2, 0:ntok], k == 0, k == KD - 1)],
                   reads=[f"sq{k % 2}", "const2"], writes=["ps6"])
        R.op("dve", lambda e: e.tensor_scalar(out=rstd[:, 0:ntok], in0=PS[6][:, 0:ntok], scalar1=1.0 / D, scalar2=EPS,
                                              op0=ALU.mult, op1=ALU.add), reads=["ps6"], writes=["rstd"])
        rsqrt_ip(rstd[:, 0:ntok])
        for k in range(KD):
            R.op("dve", lambda e, k=k: e.scalar_tensor_tensor(out=hT[:, k, 0:ntok], in0=xT[:, k, 0:ntok],
                                                            scalar=cols[:, ccol + k:ccol + k + 1], in1=rstd[:, 0:ntok],
                                                            op0=ALU.mult, op1=ALU.mult),
                 reads=["xT", "rstd", "lw"], writes=[f"hT{k}"])

    HT_ALL = [f"hT{k}" for k in range(KD)]
    psrot = [0]

    def next_ps(n=6):
        i = psrot[0] % n
        psrot[0] += 1
        return i

    def ffn(l, ntok, wa, wb, wc, ccol):
        conv_step(1)
        with ExitStack() as ph_es:
            ph = None
            act = ph_es.enter_context(sbt("f_act", [128, 22, TT], BF16))
            sl = [ph_es.enter_context(sbt(f"f_sl{i}", [128, TT], F32)) for i in range(2)]
            rmsnorm(ntok, ccol, ph)
            nblk = (DFF + 511) // 512
            ci = 0
            for cb in range(nblk):
                c0 = cb * 512
                ncol = min(512, DFF - c0)
                ta, ra = load_w(wa, l, cb)
                tb, rb = load_w(wb, l, cb)
                for j in range(ncol // 128):
                    f = cb * 4 + j
                    ia = next_ps(); ib = next_ps()
                    pe_ops([mm(PS[ia][:, 0:ntok], ta[:, k, j * 128:(j + 1) * 128], hT[:, k, 0:ntok], k == 0, k == KD - 1)
                            for k in range(KD)], reads=HT_ALL + [ra], writes=[f"ps{ia}"])
                    pe_ops([mm(PS[ib][:, 0:ntok], tb[:, k, j * 128:(j + 1) * 128], hT[:, k, 0:ntok], k == 0, k == KD - 1)
                            for k in range(KD)], reads=HT_ALL + [rb], writes=[f"ps{ib}"])
                    s = sl[ci % 2]; ci += 1
                    R.op("act", lambda e, s=s, ia=ia: e.activation(out=s[:, 0:ntok], in_=PS[ia][:, 0:ntok], func=AF.Silu),
                         reads=[f"ps{ia}"], writes=[f"sl{id(s)}"])
                    R.op("dve", lambda e, s=s, ib=ib, f=f: e.tensor_tensor(out=act[:, f, 0:ntok], in0=PS[ib][:, 0:ntok],
                                                                           in1=s[:, 0:ntok], op=ALU.mult),
                         reads=[f"ps{ib}", f"sl{id(s)}"], writes=[f"act{f}"])
            kbs = [(0, 8), (8, 8), (16, 6)]
            for nb in range(2):
                banks = [next_ps() for _ in range(4)]
                for bi, (k0, kn) in enumerate(kbs):
                    t, r = load_w(wc, l, (nb, bi))
                    for j in range(4):
                        pe_ops([mm(PS[banks[j]][:, 0:ntok], t[:, k, j * 128:(j + 1) * 128], act[:, k0 + k, 0:ntok],
                                   (bi == 0 and k == 0), (bi == 2 and k == kn - 1)) for k in range(kn)],
                               reads=[f"act{k0 + k}" for k in range(kn)] + [r], writes=[f"ps{banks[j]}"])
                for j in range(4):
                    n = nb * 4 + j
                    R.op("dve", lambda e, n=n, b=banks[j]: e.scalar_tensor_tensor(
                        out=xT[:, n, 0:ntok], in0=PS[b][:, 0:ntok], scalar=0.5, in1=xT[:, n, 0:ntok],
                        op0=ALU.mult, op1=ALU.add), reads=[f"ps{banks[j]}", "xT"], writes=["xT"])
            R.retire()

    def fm_block(t, r, ncol, ntok, evac):
        for j in range((ncol + 127) // 128):
            w = min(128, ncol - j * 128)
            b = next_ps()
            pe_ops([mm(PS[b][0:w, 0:ntok], t[:, k, j * 128:j * 128 + w], hT[:, k, 0:ntok], k == 0, k == KD - 1)
                    for k in range(KD)], reads=HT_ALL + [r], writes=[f"ps{b}"])
            evac(j, b)

    def tm_block(t, r, subs, evac):
        for si, (c0, n) in enumerate(subs):
            b = next_ps()
            pe_ops([mm(PS[b][0:n, 0:512], hT[:, k, c0:c0 + n], t[:, k, 0:512], k == 0, k == KD - 1)
                    for k in range(KD)], reads=HT_ALL + [r], writes=[f"ps{b}"])
            evac(si, b, n)

    def sb_attend(streams, N, qn):
        nch = len(streams[0]["chunks"])
        for st_ in streams:
            R.op("dve", lambda e, st_=st_: e.memset(st_["R"][:, 0:N], 0.0), writes=[f"R{st_['sid']}"])

        def stage1(st_, i):
            sid = st_["sid"]; groups = st_["groups"]
            c, key0, nk, ng = st_["chunks"][i]
            sbk = sid
            e_ap = st_["e"][i % 2]; en = st_["en"][i % 2]
            fns = []
            for gi, (hc, hp, h, oc, nq, col0, qc0) in enumerate(groups):
                last = (gi == len(groups) - 1) and ng is None
                if hp is None:
                    fns.append(mm(PS[sbk][0:nk, col0:col0 + nq], KT[:, hc, key0:key0 + nk],
                                  qn[:, h, qc0:qc0 + nq], gi == 0, last))
                else:
                    fns.append(mm(PS[sbk][0:nk, col0:col0 + nq], KT[hp:hp + 64, hc, key0:key0 + nk],
                                  qn[hp:hp + 64, hc, qc0:qc0 + nq], gi == 0, last))
            if ng is not None:
                fns.append(mm(PS[sbk][0:nk, 0:N], ident[0:nk, 0:nk], ng, False, True))
            pe_ops(fns, reads=["KT", "qn", "const", "const2"], writes=[f"ps{sbk}"])
            R.op("act", lambda e: e.activation(out=e_ap[0:nk, 0:N], in_=PS[sbk][0:nk, 0:N], func=AF.Exp, scale=0.125),
                 reads=[f"ps{sbk}"], writes=[en])
            R.op("act", lambda e: e.activation(out=st_["sp"][i % 2][0:nk, 0:N], in_=e_ap[0:nk, 0:N], func=AF.Ln,
                                               bias=1.0, scale=1.0), reads=[en], writes=[f"sp{sid}{i % 2}"])

        def stage2(st_, i):
            sid = st_["sid"]
            c, key0, nk, ng = st_["chunks"][i]
            pb = 2 + sid; qb = 4 + sid
            sp_ap = st_["sp"][i % 2]; spn = f"sp{sid}{i % 2}"
            e_ap = st_["e"][i % 2]; en = st_["en"][i % 2]
            t_ap = st_["tmp"][i % 2]; tn = st_["tn"][i % 2]
            a_ap = st_["a"][i % 2]; an = f"a{sid}{i % 2}"
            Rr = st_["R"]; rn = f"R{sid}"
            pe_ops([mm(PS[pb][0:nk, 0:N], tinc[0:nk, 0:nk], sp_ap[0:nk, 0:N], True, True)], reads=[spn, "const"],
                   writes=[f"ps{pb}"])
            if i < nch - 1:
                pe_ops([mm(PS[qb][:, 0:N], ones_bf[0:nk, :], sp_ap[0:nk, 0:N], True, True)], reads=[spn, "const2"],
                       writes=[f"ps{qb}"])
            R.op("dve", lambda e: e.tensor_tensor(out=t_ap[0:nk, 0:N], in0=PS[pb][0:nk, 0:N], in1=Rr[0:nk, 0:N], op=ALU.add),
                 reads=[f"ps{pb}", rn], writes=[tn])
            R.op("act", lambda e: e.activation(out=t_ap[0:nk, 0:N], in_=t_ap[0:nk, 0:N], func=AF.Exp, scale=-1.0),
                 reads=[tn], writes=[tn])
            R.op("pool", lambda e: e.tensor_tensor(out=a_ap[0:nk, 0:N], in0=e_ap[0:nk, 0:N], in1=t_ap[0:nk, 0:N], op=ALU.mult),
                 reads=[en, tn], writes=[an])
            if i < nch - 1:
                R.op("dve", lambda e: e.tensor_tensor(out=Rr[:, 0:N], in0=PS[qb][:, 0:N], in1=Rr[:, 0:N], op=ALU.add),
                     reads=[f"ps{qb}", rn], writes=[rn])

        def stage3(st_, i):
            sid = st_["sid"]; groups = st_["groups"]
            c, key0, nk, ng = st_["chunks"][i]
            a_ap = st_["a"][i % 2]; an = f"a{sid}{i % 2}"
            fns = []
            for gi, (hc, hp, h, oc, nq, col0, qc0) in enumerate(groups):
                ohp = (h % 2) * 64
                st0 = (i == 0) and (gi < 2)
                fns.append(mm(PS[6][ohp:ohp + 64, oc:oc + nq], VC[0:nk, c, h * 64:(h + 1) * 64],
                              a_ap[0:nk, col0:col0 + nq], st0, i == nch - 1))
            pe_ops(fns, reads=[an, "VC"], writes=["ps6"])

        for i in range(nch + 2):
            if i < nch:
                for st_ in streams:
                    stage1(st_, i)
            if 1 <= i <= nch:
                for st_ in streams:
                    stage2(st_, i - 1)
            if i >= 2:
                for st_ in streams:
                    stage3(st_, i - 2)

    def mixer(l, tile):
        ntok, subs, segs, c0g, is_sample, tix = tile
        nsub = len(subs)
        nseg = len(segs)
        L = segs[0][1]
        mx = ExitStack()

        def ph_sb(name, shape, dt=F32):
            return mx.enter_context(sbt(name, list(shape), dt))
        rmsnorm(ntok, C_NM, None)

        conv_step(1)
        with ExitStack() as pa:
            xpe = pa.enter_context(sbt("a_xpe", [128, 4, nseg, 15 + L], F32))
            tb_ = [pa.enter_context(sbt(f"a_t{i}", [128, nseg, 15 + L], F32)) for i in range(4)]
            dd = pa.enter_context(sbt("a_d", [128, 4, nseg, L], BF16))
            XPE = [f"xpe{g}" for g in range(4)]
            t, r = load_w("w_in", l, O_XP)
            R.op("dve", lambda e: e.memset(tb_[0][:], 0.0), writes=["t0"])
            R.op("dve", lambda e: e.memset(tb_[1][:], 0.0), writes=["t1"])
            R.op("pool", lambda e: e.memset(tb_[2][:], 0.0), writes=["t2"])
            R.op("pool", lambda e: e.memset(tb_[3][:], 0.0), writes=["t3"])
            if is_sample:
                for s in range(2):
                    R.dma("sp", lambda e, s=s: e.dma_start(out=xpe[:, :, s, 0:15],
                                                           in_=spT[l, s].rearrange("(g p) t -> p g t", p=128)),
                          writes=XPE)
            else:
                R.op("dve", lambda e: e.tensor_copy(out=xpe[:, :, 0, 0:15], in_=phist[:]), reads=["phist"], writes=XPE)

            def ev_xp(j, b):
                R.op("act", lambda e, j=j, b=b: e.activation(
                    out=xpe[:, j, :, 15:15 + L], in_=PS[b][:, 0:ntok].rearrange("p (s t) -> p s t", s=nseg), func=AF.Copy),
                    reads=[f"ps{b}"], writes=[f"xpe{j}"])
            fm_block(t, r, 512, ntok, ev_xp)
            for g in (3, 0, 1, 2):
                w = 2 << g
                eng = "pool" if g == 3 else "dve"
                ti = (2, 3) if g == 3 else (0, 1)
                src = xpe[:, g]
                cur = None
                curn = None
                sh = 1
                for step in range(g + 1):
                    dst = tb_[ti[step % 2]]; dstn = f"t{ti[step % 2]}"
                    a_in = src if cur is None else cur
                    a_n = f"xpe{g}" if cur is None else curn
                    R.op(eng, lambda e, dst=dst, a_in=a_in, sh=sh: e.tensor_tensor(
                        out=dst[:, :, sh:15 + L], in0=a_in[:, :, sh:15 + L], in1=a_in[:, :, 0:15 + L - sh], op=ALU.add),
                        reads=[a_n], writes=[dstn])
                    cur = dst; curn = dstn
                    sh *= 2
                if eng == "pool":
                    oth = tb_[ti[(g + 1) % 2]]; othn = f"t{ti[(g + 1) % 2]}"
                    R.op(eng, lambda e, cur=cur, oth=oth, w=w: e.tensor_scalar(out=oth[:, :, 15:15 + L], in0=cur[:, :, 15:15 + L],
                                                                               scalar1=1.0 / w, scalar2=None, op0=ALU.mult),
                         reads=[curn], writes=[othn])
                    R.op(eng, lambda e, oth=oth, g=g: e.tensor_tensor(out=dd[:, g], in0=oth[:, :, 15:15 + L],
                                                                      in1=xpe[:, g, :, 15:15 + L], op=ALU.subtract),
                         reads=[othn, f"xpe{g}"], writes=[f"dd{g}"])
                else:
                    R.op(eng, lambda e, cur=cur, g=g, w=w: e.scalar_tensor_tensor(
                        out=dd[:, g], in0=cur[:, :, 15:15 + L], scalar=1.0 / w, in1=xpe[:, g, :, 15:15 + L],
                        op0=ALU.mult, op1=ALU.subtract), reads=[curn, f"xpe{g}"], writes=[f"dd{g}"])
                if (not is_sample) and tix == 0:
                    R.op(eng, lambda e, cur=cur, g=g: e.tensor_tensor(out=cur[:, 0, 15:31], in0=cur[:, 0, 15:31],
                                                                      in1=invc[:, g, :], op=ALU.mult),
                         reads=[curn, "const", f"dd{g}"], writes=[curn])
                    R.op(eng, lambda e, cur=cur, g=g: e.tensor_tensor(out=dd[:, g, 0, 0:16], in0=cur[:, 0, 15:31],
                                                                      in1=xpe[:, g, 0, 15:31], op=ALU.subtract),
                         reads=[curn, f"xpe{g}"], writes=[f"dd{g}"])
            for g in range(4):
                b = next_ps()
                pe_ops([mm(PS[b][:, 0:ntok], poolw_sb[:, g, :], dd[:, g].rearrange("p s t -> p (s t)"), True, True)],
                       reads=[f"dd{g}", "lw"], writes=[f"ps{b}"])
                R.op("act", lambda e, g=g, b=b: e.activation(out=ypool[:, g, 0:ntok], in_=PS[b][:, 0:ntok], func=AF.Copy,
                                                             scale=cols[:, C_PS + g:C_PS + g + 1]),
                     reads=[f"ps{b}", "lw"], writes=["ypool"])
            if is_sample:
                for s in range(2):
                    R.dma("sp", lambda e, s=s: e.dma_start(out=poolsT[l, s].rearrange("(g p) t -> p g t", p=128),
                                                           in_=xpe[:, :, s, L:L + 15]), reads=XPE)
            else:
                R.op("dve", lambda e: e.tensor_copy(out=phist[:], in_=xpe[:, :, 0, L:L + 15]), reads=XPE, writes=["phist"])
                if tix == NT - 1:
                    R.dma("sp", lambda e: e.dma_start(out=poolpT[l].rearrange("(g p) t -> p g t", p=128),
                                                      in_=xpe[:, :, 0, L:L + 15]), reads=XPE)
            R.retire()

        if DBG < 3:
            mx.close()
            return
        conv_step(1)
        with ExitStack() as pb_:
            def bsb(name, shape, dt=F32):
                return pb_.enter_context(sbt(name, list(shape), dt))
            qf = bsb("b_qf", [128, 4, TT]); kf = bsb("b_kf", [128, 4, TT])
            qn = bsb("b_qn", [128, 4, TT], BF16)
            rsb = [RSTD, bsb("b_rs2", [128, TT])]; rsn = ["rstd", "rs2"]
            vst = bsb("b_vst", [128, nsub, BW])
            QF = [f"qf{j}" for j in range(4)]; KF = [f"kf{j}" for j in range(4)]
            nstream = 1 if is_sample else 2
            strm = []
            for sid in range(nstream):
                strm.append({"sid": sid,
                             "e": [qf[:, 2 * sid + i, :] for i in range(2)], "en": [f"qf{2 * sid + i}" for i in range(2)],
                             "tmp": [kf[:, 2 * sid + i, :] for i in range(2)], "tn": [f"kf{2 * sid + i}" for i in range(2)],
                             "sp": [bsb(f"b_sp{sid}{i}", [128, 512], BF16) for i in range(2)],
                             "a": [bsb(f"b_a{sid}{i}", [128, 512], BF16) for i in range(2)],
                             "R": bsb(f"b_R{sid}", [128, 512])})
            t, r = load_w("w_in", l, O_Q)
            fm_block(t, r, 512, ntok, lambda j, b: R.op("act", lambda e, j=j, b=b: e.activation(
                out=qf[:, j, 0:ntok], in_=PS[b][:, 0:ntok], func=AF.Copy), reads=[f"ps{b}"], writes=[f"qf{j}"]))
            t, r = load_w("w_in", l, O_K)
            fm_block(t, r, 512, ntok, lambda j, b: R.op("act", lambda e, j=j, b=b: e.activation(
                out=kf[:, j, 0:ntok], in_=PS[b][:, 0:ntok], func=AF.Copy), reads=[f"ps{b}"], writes=[f"kf{j}"]))

            def qknorm(src, pre, outs):
                banks = {}
                def stA(j):
                    R.op("pool", lambda e, j=j: e.tensor_tensor(out=SQ[:, j % 2, 0:ntok], in0=src[:, j, 0:ntok],
                                                                in1=src[:, j, 0:ntok], op=ALU.mult),
                         reads=[f"{pre}{j}"], writes=[f"sq{j % 2}"])
                    b = next_ps(); banks[j] = b
                    pe_ops([mm(PS[b][:, 0:ntok], bdiag[:], SQ[:, j % 2, 0:ntok], True, True)], reads=[f"sq{j % 2}", "const2"],
                           writes=[f"ps{b}"])
                    R.op("dve", lambda e, b=b, j=j: e.tensor_scalar(out=rsb[j % 2][:, 0:ntok], in0=PS[b][:, 0:ntok],
                                                                    scalar1=1.0 / 64, scalar2=EPS, op0=ALU.mult, op1=ALU.add),
                         reads=[f"ps{b}"], writes=[rsn[j % 2]])
                def stB(j):
                    rsqrt_ip(rsb[j % 2][:, 0:ntok], rsn[j % 2])
                    outs(j)
                for st in range(5):
                    if st < 4:
                        stA(st)
                    if st >= 1:
                        stB(st - 1)
            if is_sample:
                qz = bsb("b_qz", [128, 8, 2 * DEC], BF16)
                R.op("dve", lambda e: e.memset(qz[:], 0.0), writes=["qn"])
            def q_out(j):
                if is_sample:
                    R.op("dve", lambda e, j=j: e.scalar_tensor_tensor(out=qf[:, j, 0:ntok], in0=qf[:, j, 0:ntok],
                                                                      scalar=cols[:, C_QN:C_QN + 1], in1=rsb[j % 2][:, 0:ntok],
                                                                      op0=ALU.mult, op1=ALU.mult),
                         reads=[f"qf{j}", rsn[j % 2], "lw"], writes=[f"qf{j}"])
                    for hh in range(2):
                        h = 2 * j + hh
                        R.op("act", lambda e, j=j, h=h, hh=hh: e.activation(
                            out=qz[hh * 64:hh * 64 + 64, h, :], in_=qf[hh * 64:hh * 64 + 64, j, 0:ntok], func=AF.Copy),
                            reads=[f"qf{j}"], writes=["qn"])
                else:
                    R.op("dve", lambda e, j=j: e.scalar_tensor_tensor(out=qn[:, j, 0:ntok], in0=qf[:, j, 0:ntok],
                                                                      scalar=cols[:, C_QN:C_QN + 1], in1=rsb[j % 2][:, 0:ntok],
                                                                      op0=ALU.mult, op1=ALU.mult),
                         reads=[f"qf{j}", rsn[j % 2], "lw"], writes=["qn"])
            qknorm(qf, "qf", q_out)
            def k_out(j):
                R.op("dve", lambda e, j=j: e.scalar_tensor_tensor(out=kf[:, j, 0:ntok], in0=kf[:, j, 0:ntok],
                                                                  scalar=cols[:, C_KN:C_KN + 1], in1=rsb[j % 2][:, 0:ntok],
                                                                  op0=ALU.mult, op1=ALU.mult),
                     reads=[f"kf{j}", rsn[j % 2], "lw"], writes=[f"kf{j}"])
            qknorm(kf, "kf", k_out)
            t, r = load_w("w_in", l, O_V)
            tm_block(t, r, subs, lambda si, b, n: R.op("act", lambda e, si=si, b=b, n=n: e.activation(
                out=vst[0:n, si, :], in_=PS[b][0:n, :], func=AF.Copy), reads=[f"ps{b}"], writes=["vst"]))
            if is_sample:
                R.dma("sp", lambda e: e.dma_start(out=ksT[l].rearrange("(j p) t -> p j t", p=128), in_=kf[:, :, 0:ntok]),
                      reads=KF)
                for si in range(2):
                    R.dma("sp", lambda e, si=si: e.dma_start(out=vs[l, si * DEC:(si + 1) * DEC, :], in_=vst[0:DEC, si, :]),
                          reads=["vst"])
            else:
                R.dma("sp", lambda e: e.dma_start(out=kpT[l].rearrange("(j p) t -> p j t", p=128)[:, :, c0g:c0g + ntok],
                                                  in_=kf[:, :, 0:ntok]), reads=KF)
                R.dma("sp", lambda e: e.dma_start(out=vp[l, c0g:c0g + ntok, :].rearrange("(s p) f -> p s f", p=128),
                                                  in_=vst[:, :, :]), reads=["vst"])
            if is_sample:
                knew = bsb("b_knew", [128, 4, 2 * DEC], BF16)
                R.op("act", lambda e: e.activation(out=knew[:], in_=kf[:, :, 0:ntok], func=AF.Copy), reads=KF, writes=["knew"])
                for s in range(2):
                    for j in range(4):
                        R.dma("pool", lambda e, s=s, j=j: e.dma_start(out=KT[:, j, 0:PAST],
                                                                      in_=ckT[l, s, j * 128:(j + 1) * 128, :]),
                              writes=["KT"])
                    for c8 in range(0, NPC, 8):
                        c9 = min(NPC, c8 + 8)
                        R.dma("pool", lambda e, s=s, c8=c8, c9=c9: e.dma_start(
                            out=VC[:, c8:c9, :], in_=cv[l, s, c8 * 128:c9 * 128, :].rearrange("(c p) f -> p c f", p=128)),
                            writes=["VC"])
                    R.op("act", lambda e, s=s: e.activation(out=KT[:, :, PAST:PAST + DEC], in_=knew[:, :, s * DEC:(s + 1) * DEC],
                                                            func=AF.Copy), reads=["knew"], writes=["KT"])
                    R.op("act", lambda e, s=s: e.activation(out=VC[0:DEC, NPC, :], in_=vst[0:DEC, s, :], func=AF.Copy),
                         reads=["vst"], writes=["VC"])
                    strm[0]["groups"] = [(h // 2, None, h, (h // 2) * DEC, DEC, h * DEC, s * DEC) for h in range(8)]
                    strm[0]["chunks"] = [(NPC, PAST, DEC, negs[:, :])] + [(c, c * 128, 128, None) for c in range(NPC - 1, -1, -1)]
                    sb_attend(strm, 8 * DEC, qz)
                    R.op("act", lambda e, s=s: e.activation(
                        out=ysb[:, :, s * DEC:(s + 1) * DEC], in_=PS[6][:, 0:4 * DEC].rearrange("p (m t) -> p m t", m=4),
                        func=AF.Copy), reads=["ps6"], writes=["ysb"])
            else:
                pc0 = c0g // 128
                R.op("act", lambda e: e.activation(out=KT[:, :, c0g:c0g + ntok], in_=kf[:, :, 0:ntok], func=AF.Copy),
                     reads=KF, writes=["KT"])
                R.op("act", lambda e: e.activation(out=VC[:, pc0:pc0 + 4, :], in_=vst[:, :, :], func=AF.Copy),
                     reads=["vst"], writes=["VC"])
                chunks = []
                for c in range(pc0 + 3, -1, -1):
                    dgi = c - pc0
                    chunks.append((c, c * 128, 128, negp[:, dgi * 512:(dgi + 1) * 512] if dgi >= 0 else None))
                for m in range(4):
                    for hh in range(2):
                        strm[hh]["groups"] = [(m, hh * 64, 2 * m + hh, 0, TT, 0, 0)]
                        strm[hh]["chunks"] = chunks
                    sb_attend(strm, TT, qn)
                    R.op("act", lambda e, m=m: e.activation(out=ysb[:, m, 0:ntok], in_=PS[6][:, 0:ntok], func=AF.Copy),
                         reads=["ps6"], writes=["ysb"])
            R.retire()

        if DBG < 4:
            mx.close()
            return
        conv_step(1)
        with ExitStack() as pc_:
            gu = pc_.enter_context(sbt("c_gu", [128, 4, TT], F32))
            gvb = pc_.enter_context(sbt("c_gvb", [128, nsub, BW], BF16))
            gvf = pc_.enter_context(sbt("c_gvf", [128, nsub, BW], F32))
            t, r = load_w("w_in", l, O_GU)
            fm_block(t, r, 512, ntok, lambda j, b: R.op("act", lambda e, j=j, b=b: e.activation(
                out=gu[:, j, 0:ntok], in_=PS[b][:, 0:ntok], func=AF.Gelu_apprx_tanh), reads=[f"ps{b}"], writes=["gu"]))
            t, r = load_w("w_in", l, O_GV)
            def ev_gv(si, b, n):
                R.op("act", lambda e: e.activation(out=gvf[0:n, si, :], in_=PS[b][0:n, :], func=AF.Gelu_apprx_tanh),
                     reads=[f"ps{b}"], writes=["gvf"])
                R.op("dve", lambda e: e.tensor_copy(out=gvb[0:n, si, :], in_=gvf[0:n, si, :]), reads=["gvf"], writes=["gvb"])
            tm_block(t, r, subs, ev_gv)
            if is_sample:
                for si in range(2):
                    R.dma("sp", lambda e, si=si: e.dma_start(out=gms[l, si * DEC:(si + 1) * DEC, :], in_=gvf[0:DEC, si, :]),
                          reads=["gvf"])
            for si, (c0, n) in enumerate(subs):
                b = next_ps()
                fns = []
                for g in range(4):
                    fns.append(mm(PS[b][:, g * 128:g * 128 + n], gvb[0:n, si, g * 128:(g + 1) * 128], wmT[0:n, g, 0:n],
                                  g == 0, False))
                    fns.append(mm(PS[b][:, g * 128:g * 128 + n], ones_bf[0:1, :], gb_hi[0:1, g * 128:g * 128 + n], False, False))
                    fns.append(mm(PS[b][:, g * 128:g * 128 + n], ones_bf[0:1, :], gb_lo[0:1, g * 128:g * 128 + n], False, g == 3))
                pe_ops(fns, reads=["gvb", "lw", "const2"], writes=[f"ps{b}"])
                R.op("dve", lambda e, b=b, c0=c0, n=n: e.tensor_tensor(
                    out=ygm[:, :, c0:c0 + n], in0=PS[b][:, :].rearrange("p (g i) -> p g i", g=4)[:, :, 0:n],
                    in1=gu[:, :, c0:c0 + n], op=ALU.mult), reads=[f"ps{b}", "gu"], writes=["ygm"])
            R.retire()

        if DBG < 5:
            mx.close()
            return
        conv_step(1)
        with ExitStack() as pd_:
            def dsb(name, shape, dt=F32):
                return pd_.enter_context(sbt(name, list(shape), dt))
            lq = dsb("d_lq", [128, 2, TT]); lk = dsb("d_lk", [128, 2, TT])
            lvb = dsb("d_lvb", [128, nsub, BW], BF16)
            laT = dsb("d_la", [16, TT], BF16)
            lr = dsb("d_lr", [128, 4, TT], BF16)
            eg = dsb("d_eg", [128, 2, 256]); spg = dsb("d_spg", [128, 2, 256])
            eb = dsb("d_eb", [128, 2, TT])
            qt = dsb("d_qt", [128, 2, TT], BF16); kt = dsb("d_kt", [128, 2, TT], BF16)
            ktok = dsb("d_ktok", [128, nsub, 256], BF16)
            attb = [dsb(f"d_att{i}", [128, 128], BF16) for i in range(2)]
            oT = dsb("d_oT", [128, 4, TT])
            enb = oT[:, 0:2, :]
            ors2 = dsb("d_ors2", [128, TT])
            t, r = load_w("w_in", l, O_LQ)
            def ev_lqk(j, b):
                dst = lq if j < 2 else lk
                R.op("act", lambda e: e.activation(out=dst[:, j % 2, 0:ntok], in_=PS[b][:, 0:ntok], func=AF.Copy),
                     reads=[f"ps{b}"], writes=["lq" if j < 2 else "lk"])
            fm_block(t, r, 512, ntok, ev_lqk)
            t, r = load_w("w_in", l, O_LV)
            tm_block(t, r, subs, lambda si, b, n: R.op("act", lambda e, si=si, b=b, n=n: e.activation(
                out=lvb[0:n, si, :], in_=PS[b][0:n, :], func=AF.Copy), reads=[f"ps{b}"], writes=["lvb"]))
            b = next_ps()
            pe_ops([mm(PS[b][0:16, 0:ntok], wla_sb[:, k, :], hT[:, k, 0:ntok], k == 0, k == KD - 1) for k in range(KD)],
                   reads=HT_ALL + ["lw"], writes=[f"ps{b}"])
            R.op("act", lambda e, b=b: e.activation(out=laT[:, 0:ntok], in_=PS[b][0:16, 0:ntok], func=AF.Copy),
                 reads=[f"ps{b}"], writes=["laT"])
            t, r = load_w("w_in", l, O_LR)
            fm_block(t, r, 512, ntok, lambda j, b: R.op("act", lambda e, j=j, b=b: e.activation(
                out=lr[:, j, 0:ntok], in_=PS[b][:, 0:ntok], func=AF.Silu), reads=[f"ps{b}"], writes=["lr"]))
            for si, (c0, n) in enumerate(subs):
                b = next_ps()
                pe_ops([mm(PS[b][0:n, 0:256], laT[:, c0:c0 + n], wa2_sb[:, :], True, False),
                        mm(PS[b][0:n, 0:256], ones_bf[0:1, 0:n], ba_sb[0:1, :], False, True)],
                       reads=["laT", "lw", "const2"], writes=[f"ps{b}"])
                R.op("act", lambda e, si=si, b=b, n=n: e.activation(out=eg[0:n, si % 2, :], in_=PS[b][0:n, 0:256], func=AF.Exp,
                                                                    scale=-1.0), reads=[f"ps{b}"], writes=[f"eg{si % 2}"])
                R.op("act", lambda e, si=si, n=n: e.activation(out=spg[0:n, si % 2, :], in_=eg[0:n, si % 2, :], func=AF.Ln, bias=1.0,
                                                               scale=1.0), reads=[f"eg{si % 2}"], writes=[f"spg{si % 2}"])
                for fc in range(2):
                    b2 = next_ps()
                    pe_ops([mm(PS[b2][:, 0:n], spg[0:n, si % 2, fc * 128:(fc + 1) * 128], trile[0:n, 0:n], True, True)],
                           reads=[f"spg{si % 2}", "const"], writes=[f"ps{b2}"])
                    R.op("act", lambda e, fc=fc, b2=b2, c0=c0, n=n: e.activation(
                        out=eb[:, fc, c0:c0 + n], in_=PS[b2][:, 0:n], func=AF.Exp, scale=-1.0 / 16),
                        reads=[f"ps{b2}"], writes=["eb"])
                    R.op("act", lambda e, fc=fc, b2=b2, c0=c0, n=n: e.activation(
                        out=enb[:, fc, c0:c0 + n], in_=PS[b2][:, 0:n], func=AF.Exp, scale=1.0 / 16),
                        reads=[f"ps{b2}"], writes=["oT0", "oT1"])
            for fc in range(2):
                R.op("dve", lambda e, fc=fc: e.scalar_tensor_tensor(out=qt[:, fc, 0:ntok], in0=lq[:, fc, 0:ntok], scalar=0.125,
                                                                    in1=eb[:, fc, 0:ntok], op0=ALU.mult, op1=ALU.mult),
                     reads=["lq", "eb"], writes=["qt"])
                R.op("dve", lambda e, fc=fc: e.tensor_tensor(out=kt[:, fc, 0:ntok], in0=lk[:, fc, 0:ntok],
                                                             in1=enb[:, fc, 0:ntok], op=ALU.mult),
                     reads=["lk", "oT0", "oT1"], writes=["kt"])
            for si, (c0, n) in enumerate(subs):
                for fc in range(2):
                    pe_ops([lambda pe, fc=fc, c0=c0, n=n: pe.transpose(PSB[0:n, fc * 128:(fc + 1) * 128],
                                                                        kt[:, fc, c0:c0 + n], ident[:, :])],
                           reads=["kt", "const"], writes=["psb"])
                R.op("act", lambda e, si=si, n=n: e.activation(out=ktok[0:n, si, :], in_=PSB[0:n, 0:256], func=AF.Copy),
                     reads=["psb"], writes=["ktok"])
            for si, (c0, n) in enumerate(subs):
                seq = si if is_sample else 0
                if is_sample or (tix == 0 and si == 0):
                    if is_sample:
                        for hh in range(4):
                            fp = (hh % 2) * 64
                            R.dma("sp", lambda e, hh=hh, fp=fp, seq=seq: e.dma_start(
                                out=S32[fp:fp + 64, hh * 128:(hh + 1) * 128], in_=sgl[l, seq, hh]), writes=[f"S32{hh}"])
                        for hh in range(4):
                            fp = (hh % 2) * 64
                            R.op("act", lambda e, hh=hh, fp=fp: e.activation(out=Sbf[fp:fp + 64, hh * 128:(hh + 1) * 128],
                                                                             in_=S32[fp:fp + 64, hh * 128:(hh + 1) * 128],
                                                                             func=AF.Copy), reads=[f"S32{hh}"], writes=[f"Sbf{hh}"])
                    else:
                        R.op("dve", lambda e: e.memset(S32[:], 0.0), writes=[f"S32{h_}" for h_ in range(4)])
                        R.op("dve", lambda e: e.memset(Sbf[:], 0.0), writes=[f"Sbf{h_}" for h_ in range(4)])
                for hh in range(4):
                    fc = hh // 2; fp = (hh % 2) * 64
                    ab = hh % 2
                    ob = 2 + hh % 2
                    pe_ops([mm(PS[ab][0:n, 0:n], kt[fp:fp + 64, fc, c0:c0 + n], qt[fp:fp + 64, fc, c0:c0 + n], True, True)],
                           reads=["kt", "qt"], writes=[f"ps{ab}"])
                    R.op("dve", lambda e, ab=ab, n=n: e.tensor_tensor(out=attb[ab][0:n, 0:n], in0=PS[ab][0:n, 0:n],
                                                                      in1=trile[0:n, 0:n], op=ALU.mult),
                         reads=[f"ps{ab}", "const"], writes=[f"att{ab}"])
                    pe_ops([mm(PS[ob][:, 0:n], Sbf[fp:fp + 64, hh * 128:(hh + 1) * 128], qt[fp:fp + 64, fc, c0:c0 + n], True, False),
                            mm(PS[ob][:, 0:n], lvb[0:n, si, hh * 128:(hh + 1) * 128], attb[ab][0:n, 0:n], False, True)],
                           reads=[f"Sbf{hh}", "qt", "lvb", f"att{ab}"], writes=[f"ps{ob}"])
                    R.op("act", lambda e, hh=hh, ob=ob, c0=c0, n=n: e.activation(out=oT[:, hh, c0:c0 + n], in_=PS[ob][:, 0:n],
                                                                                func=AF.Copy), reads=[f"ps{ob}"], writes=[f"oT{hh}"])
                    db = 4 + hh % 2
                    pe_ops([mm(PS[db][fp:fp + 64, hh * 128:(hh + 1) * 128], ktok[0:n, si, fc * 128 + fp:fc * 128 + fp + 64],
                               lvb[0:n, si, hh * 128:(hh + 1) * 128], True, True)], reads=["ktok", "lvb"], writes=[f"ps{db}"])
                    R.op("dve", lambda e, hh=hh, fp=fp, db=db: e.tensor_tensor(
                        out=S32[fp:fp + 64, hh * 128:(hh + 1) * 128], in0=PS[db][fp:fp + 64, hh * 128:(hh + 1) * 128],
                        in1=S32[fp:fp + 64, hh * 128:(hh + 1) * 128], op=ALU.add), reads=[f"ps{db}", f"S32{hh}"], writes=[f"S32{hh}"])
                    R.op("dve", lambda e, hh=hh, fp=fp, fc=fc, c0=c0, n=n: e.tensor_scalar(
                        out=S32[fp:fp + 64, hh * 128:(hh + 1) * 128], in0=S32[fp:fp + 64, hh * 128:(hh + 1) * 128],
                        scalar1=eb[fp:fp + 64, fc, c0 + n - 1:c0 + n], scalar2=None, op0=ALU.mult),
                        reads=[f"S32{hh}", "eb"], writes=[f"S32{hh}"])
                    R.op("act", lambda e, hh=hh, fp=fp: e.activation(out=Sbf[fp:fp + 64, hh * 128:(hh + 1) * 128],
                                                                     in_=S32[fp:fp + 64, hh * 128:(hh + 1) * 128], func=AF.Copy),
                         reads=[f"S32{hh}"], writes=[f"Sbf{hh}"])
                if is_sample or (tix == NT - 1 and si == nsub - 1):
                    for hh in range(4):
                        fp = (hh % 2) * 64
                        dst = glas[l, seq, hh] if is_sample else glap[l, hh]
                        R.dma("sp", lambda e, hh=hh, fp=fp, dst=dst: e.dma_start(
                            out=dst, in_=S32[fp:fp + 64, hh * 128:(hh + 1) * 128]), reads=[f"S32{hh}"])
            orsb = [RSTD, ors2]; orsn = ["rstd", "ors2"]
            for hh in range(4):
                R.op("act", lambda e, hh=hh: e.activation(out=SQ[:, hh % 2, 0:ntok], in_=oT[:, hh, 0:ntok], func=AF.Square),
                     reads=[f"oT{hh}"], writes=[f"sq{hh % 2}"])
                b = next_ps()
                pe_ops([mm(PS[b][:, 0:ntok], ones_bf[:], SQ[:, hh % 2, 0:ntok], True, True)], reads=[f"sq{hh % 2}", "const2"],
                       writes=[f"ps{b}"])
                R.op("dve", lambda e, b=b, hh=hh: e.tensor_scalar(out=orsb[hh % 2][:, 0:ntok], in0=PS[b][:, 0:ntok],
                                                                  scalar1=1.0 / 128, scalar2=EPS, op0=ALU.mult, op1=ALU.add),
                     reads=[f"ps{b}"], writes=[orsn[hh % 2]])
                rsqrt_ip(orsb[hh % 2][:, 0:ntok], orsn[hh % 2])
                R.op("dve", lambda e, hh=hh: e.scalar_tensor_tensor(out=oT[:, hh, 0:ntok], in0=oT[:, hh, 0:ntok],
                                                                    scalar=cols[:, C_GON:C_GON + 1], in1=orsb[hh % 2][:, 0:ntok],
                                                                    op0=ALU.mult, op1=ALU.mult),
                     reads=[f"oT{hh}", orsn[hh % 2], "lw"], writes=[f"oT{hh}"])
                R.op("pool", lambda e, hh=hh: e.tensor_tensor(out=ygl[:, hh, 0:ntok], in0=oT[:, hh, 0:ntok],
                                                              in1=lr[:, hh, 0:ntok], op=ALU.mult),
                     reads=[f"oT{hh}", "lr"], writes=["ygl"])
            R.retire()

        if DBG < 6:
            mx.close()
            return
        conv_step(1)
        with ExitStack() as pe_:
            acc = pe_.enter_context(sbt("e_acc", [128, 4, TT], F32))
            sg = [pe_.enter_context(sbt(f"e_sg{i}", [128, TT], F32)) for i in range(2)]
            tm = [pe_.enter_context(sbt(f"e_tm{i}", [128, TT], F32)) for i in range(2)]
            mg = pe_.enter_context(sbt("e_mg", [128, KD, TT], BF16))
            ybs = [ypool, ysb, ygm, ygl]
            ynm = ["ypool", "ysb", "ygm", "ygl"]
            ci = 0
            for nb in range(2):
                for bnum in range(4):
                    tb, rb = load_w("w_br", l, (bnum, nb))
                    tg, rg = load_w("w_in", l, O_G + bnum * D + nb * 512)
                    for j in range(4):
                        ip = next_ps(); ig = next_ps()
                        pe_ops([mm(PS[ip][:, 0:ntok], tb[:, k, j * 128:(j + 1) * 128], ybs[bnum][:, k, 0:ntok], k == 0, k == 3)
                                for k in range(4)], reads=[ynm[bnum], rb], writes=[f"ps{ip}"])
                        pe_ops([mm(PS[ig][:, 0:ntok], tg[:, k, j * 128:(j + 1) * 128], hT[:, k, 0:ntok], k == 0, k == KD - 1)
                                for k in range(KD)], reads=HT_ALL + [rg], writes=[f"ps{ig}"])
                        s_ = sg[ci % 2]; t_ = tm[ci % 2]; sn = f"sg{ci % 2}"; tn = f"tm{ci % 2}"; ci += 1
                        R.op("act", lambda e, s_=s_, ig=ig: e.activation(out=s_[:, 0:ntok], in_=PS[ig][:, 0:ntok], func=AF.Sigmoid),
                             reads=[f"ps{ig}"], writes=[sn])
                        if bnum == 0:
                            R.op("dve", lambda e, s_=s_, ip=ip, j=j: e.tensor_tensor(out=acc[:, j, 0:ntok], in0=PS[ip][:, 0:ntok],
                                                                                     in1=s_[:, 0:ntok], op=ALU.mult),
                                 reads=[f"ps{ip}", sn], writes=[f"acc{j}"])
                        else:
                            R.op("dve", lambda e, s_=s_, t_=t_, ip=ip: e.tensor_tensor(out=t_[:, 0:ntok], in0=PS[ip][:, 0:ntok],
                                                                                       in1=s_[:, 0:ntok], op=ALU.mult),
                                 reads=[f"ps{ip}", sn], writes=[tn])
                            if bnum < 3:
                                R.op("dve", lambda e, t_=t_, j=j: e.tensor_tensor(out=acc[:, j, 0:ntok], in0=acc[:, j, 0:ntok],
                                                                                  in1=t_[:, 0:ntok], op=ALU.add),
                                     reads=[tn, f"acc{j}"], writes=[f"acc{j}"])
                            else:
                                R.op("dve", lambda e, t_=t_, j=j, nb=nb: e.tensor_tensor(
                                    out=mg[:, nb * 4 + j, 0:ntok], in0=acc[:, j, 0:ntok], in1=t_[:, 0:ntok], op=ALU.add),
                                    reads=[tn, f"acc{j}"], writes=[f"mg{nb * 4 + j}"])
            for nb in range(2):
                t, r = load_w("w_o", l, nb)
                for j in range(4):
                    n = nb * 4 + j
                    b = next_ps()
                    pe_ops([mm(PS[b][:, 0:ntok], t[:, k, j * 128:(j + 1) * 128], mg[:, k, 0:ntok], k == 0, k == KD - 1)
                            for k in range(KD)], reads=[f"mg{k}" for k in range(KD)] + [r], writes=[f"ps{b}"])
                    R.op("dve", lambda e, n=n, b=b: e.tensor_tensor(out=xT[:, n, 0:ntok], in0=PS[b][:, 0:ntok],
                                                                    in1=xT[:, n, 0:ntok], op=ALU.add),
                         reads=[f"ps{b}", "xT"], writes=["xT"])
            R.retire()
        mx.close()

    def ple(l, tile):
        ntok, subs, segs, c0g, is_sample, tix = tile
        cg = (S if is_sample else c0g)
        conv_step(1)
        with ExitStack() as pp_:
            ph = None
            sg = [pp_.enter_context(sbt(f"p_sg{i}", [128, TT], F32)) for i in range(2)]
            tm = [pp_.enter_context(sbt(f"p_tm{i}", [128, TT], F32)) for i in range(2)]
            R.dma("pool", lambda e: e.dma_start(out=pT[:, :, 0:ntok],
                                                in_=pin[l].rearrange("(k p) t -> p k t", p=128)[:, :, cg:cg + ntok]),
                  writes=["pT"])
            rmsnorm(ntok, C_NPL, ph)
            ci = 0
            for nb in range(2):
                t, r = load_w("w_pg", l, nb)
                for j in range(4):
                    n = nb * 4 + j
                    ig = next_ps(); ip = next_ps()
                    pe_ops([mm(PS[ig][:, 0:ntok], t[:, k, j * 128:(j + 1) * 128], hT[:, k, 0:ntok], k == 0, k == KD - 1)
                            for k in range(KD)], reads=HT_ALL + [r], writes=[f"ps{ig}"])
                    pe_ops([mm(PS[ip][:, 0:ntok], wpp_sb[:, k, n * 128:(n + 1) * 128], pT[:, k, 0:ntok], k == 0, k == 1)
                            for k in range(2)], reads=["pT", "lw"], writes=[f"ps{ip}"])
                    s_ = sg[ci % 2]; t_ = tm[ci % 2]; sn = f"sg{ci % 2}"; tn = f"tm{ci % 2}"; ci += 1
                    R.op("act", lambda e, s_=s_, ig=ig: e.activation(out=s_[:, 0:ntok], in_=PS[ig][:, 0:ntok], func=AF.Sigmoid),
                         reads=[f"ps{ig}"], writes=[sn])
                    R.op("dve", lambda e, s_=s_, t_=t_, ip=ip: e.tensor_tensor(out=t_[:, 0:ntok], in0=PS[ip][:, 0:ntok],
                                                                               in1=s_[:, 0:ntok], op=ALU.mult),
                         reads=[f"ps{ip}", sn], writes=[tn])
                    R.op("dve", lambda e, t_=t_, n=n: e.tensor_tensor(out=xT[:, n, 0:ntok], in0=xT[:, n, 0:ntok],
                                                                      in1=t_[:, 0:ntok], op=ALU.add),
                         reads=[tn, "xT"], writes=["xT"])
            R.retire()

    tiles = [(2 * DEC, [(0, DEC), (DEC, DEC)], [(0, DEC), (DEC, DEC)], S, True, 0)]
    for t_ in range(NT):
        tiles.append((TT, [(i * 128, 128) for i in range(4)], [(0, TT)], t_ * TT, False, t_))

    convert_layer(0)
    for l in range(DEPTH):
        R.dma("sp", lambda e, l=l: e.dma_start(out=cols[:], in_=cols_d[l]), writes=["lw"])
        R.dma("pool", lambda e, l=l: e.dma_start(out=poolw_sb[:], in_=pool_w[l].rearrange("g c d -> c g d")), writes=["lw"])
        R.dma("pool", lambda e, l=l: e.dma_start(out=wmT[:], in_=wsT_d[l].rearrange("g j i -> j g i")), writes=["lw"])
        R.dma("sp", lambda e, l=l: e.dma_start(out=gb_f[:], in_=gb_d[l]), writes=["lw"])
        R.dma("pool", lambda e, l=l: e.dma_start(out=wa2_sb[:], in_=wa2_d[l]), writes=["lw"])
        R.dma("pool", lambda e, l=l: e.dma_start(out=ba_sb[:], in_=ba_d[l]), writes=["lw"])
        R.dma("pool", lambda e, l=l: e.dma_start(out=wla_sb[:], in_=w_in[l][:, O_LA:O_LA + 16].rearrange("(k p) n -> p k n", p=128)),
              writes=["lw"])
        R.dma("pool", lambda e, l=l: e.dma_start(out=wpp_sb[:], in_=w_pp[l].rearrange("(k p) n -> p k n", p=128)), writes=["lw"])
        for g in range(4):
            R.op("dve", lambda e, g=g: e.tensor_tensor(out=wmT[:, g, :], in0=wmT[:, g, :], in1=bmask[:], op=ALU.mult),
                 reads=["lw", "const"], writes=["lw"])
        R.op("act", lambda e: e.activation(out=gb_hi[:], in_=gb_f[:], func=AF.Copy), reads=["lw"], writes=["lw2"])
        R.op("dve", lambda e: e.tensor_tensor(out=gb_lo[:], in0=gb_f[:], in1=gb_hi[:], op=ALU.subtract), reads=["lw", "lw2"],
             writes=["lw3"])
        R.op("dve", lambda e: e.memset(phist[:], 0.0), writes=["phist"])
        R.barrier(("pe", "act", "dve", "sp", "pool"))
        conv_step(len(conv_q))
        if l + 1 < DEPTH:
            convert_layer(l + 1, defer=True)
        for tile in tiles:
            ntok, subs, segs, c0g, is_sample, tix = tile
            cg = S if is_sample else c0g
            src = xin if l == 0 else xscr
            dst = yout if l == DEPTH - 1 else xscr
            rname = f"xd{cg}"
            R.dma("sp", lambda e, src=src, cg=cg, ntok=ntok: e.dma_start(
                out=xT[:, :, 0:ntok], in_=src.rearrange("(k p) t -> p k t", p=128)[:, :, cg:cg + ntok]),
                reads=[rname], writes=["xT"])
            if DBG >= 1:
                ffn(l, ntok, "f1a", "f1b", "f1c", C_N1)
            if DBG >= 2:
                mixer(l, tile)
            if DBG >= 8:
                ffn(l, ntok, "f2a", "f2b", "f2c", C_N2)
            if DBG >= 9:
                ple(l, tile)
            R.dma("sp", lambda e, dst=dst, cg=cg, ntok=ntok: e.dma_start(
                out=dst.rearrange("(k p) t -> p k t", p=128)[:, :, cg:cg + ntok], in_=xT[:, :, 0:ntok]),
                reads=["xT"], writes=[rname])
    R.barrier(("pe", "act", "dve", "sp", "pool"))

    sems = {}
    for e in R.engs:
        sems[e] = es.enter_context(nc.semaphore(f"c_{e}"))
    for q in ("sp", "pool"):
        for i in range(R.dma_nsem[q]):
            sems[("dma", q, i)] = es.enter_context(nc.semaphore(f"d_{q}{i}"))
    block = es.enter_context(nc.Block())

    def replay(eng, name):
        for waits, fn, inc in R.ops[name]:
            for k, v in waits:
                eng.wait_ge(sems[k], v)
            if fn is None:
                continue
            ins = fn(eng)
            if inc[0] == "cnt":
                ins.then_inc(sems[inc[1]], 1)
            else:
                ins.then_inc(sems[inc[1]], 16)

    block.tensor(lambda e: replay(e, "pe"))
    block.scalar(lambda e: replay(e, "act"))
    block.vector(lambda e: replay(e, "dve"))
    block.gpsimd(lambda e: replay(e, "pool"))
    block.sync(lambda e: replay(e, "sp"))
    es.close()
    return nc


def make_consts():
    i = np.arange(128)
    c = {}
    c["c_ident"] = np.eye(128, dtype=np.float32)
    c["c_tinc"] = (i[:, None] >= i[None, :]).astype(np.float32)
    c["c_trile"] = (i[:, None] <= i[None, :]).astype(np.float32)
    q = np.arange(512)
    negp = np.zeros((128, 4, 512), np.float32)
    for d in range(4):
        negp[:, d, :] = np.where((i[:, None] + 128 * d) < q[None, :], 0.0, NEGV)
    c["c_negp"] = negp.reshape(128, 2048)
    j = np.arange(32)
    ns = np.where(j[:, None] < j[None, :], 0.0, NEGV).astype(np.float32)
    c["c_negs"] = np.tile(ns, (1, 8))
    c["c_bmask"] = ((i[:, None] // 64) <= (i[None, :] // 64)).astype(np.float32)
    invc = np.zeros((128, 4, 16), np.float32)
    for g in range(4):
        invc[:, g, :] = (1.0 / np.minimum(2 << g, q[:16] + 1)).astype(np.float32)[None, :]
    c["c_invc"] = invc.reshape(128, 64)
    return c


def layout_inputs(inp, S, PAST, DEPTH, n_cores):
    f = lambda a: np.ascontiguousarray(a, dtype=np.float32)
    shared = {}
    for src, dst in [("ffn1_w1", "f1a"), ("ffn1_w3", "f1b"), ("ffn1_w2", "f1c"), ("ffn2_w1", "f2a"), ("ffn2_w3", "f2b"),
                     ("ffn2_w2", "f2c"), ("w_in", "w_in"), ("pool_w", "pool_w"), ("gla_wa2", "wa2"), ("w_branch", "w_br"),
                     ("w_out", "w_o"), ("ple_w_gate", "w_pg"), ("ple_w_proj", "w_pp")]:
        shared[dst] = f(inp[src])
    shared["wsT"] = f(np.transpose(inp["gmlp_ws"], (0, 1, 3, 2)))
    shared["gb"] = f(np.reshape(inp["gmlp_b"], (DEPTH, 1, 512)))
    shared["ba"] = f(np.reshape(inp["gla_ba"], (DEPTH, 1, 256)))
    cols = np.zeros((DEPTH, 128, NCOL), np.float32)
    for nm, c0 in [("norm_ffn1", C_N1), ("norm_mix", C_NM), ("norm_ffn2", C_N2), ("norm_ple", C_NPL)]:
        cols[:, :, c0:c0 + 8] = np.transpose(np.reshape(inp[nm], (DEPTH, 8, 128)), (0, 2, 1))
    cols[:, :, C_PS:C_PS + 4] = np.transpose(np.reshape(inp["pool_scale"], (DEPTH, 4, 128)), (0, 2, 1))
    cols[:, :, C_QN] = np.tile(inp["sb_q_norm"], (1, 2))
    cols[:, :, C_KN] = np.tile(inp["sb_k_norm"], (1, 2))
    cols[:, :, C_GON] = inp["gla_out_norm"]
    shared["cols"] = cols
    shared.update(make_consts())
    maps = []
    for c in range(n_cores):
        m = dict(shared)
        xs = np.reshape(inp["x_sample"][2 * c:2 * c + 2], (2 * DEC, D))
        m["xin"] = f(np.concatenate([inp["x_prompt"][c].T, xs.T], axis=1))
        ps = np.reshape(inp["p_sample"][:, 2 * c:2 * c + 2], (DEPTH, 2 * DEC, PLE))
        m["pin"] = f(np.concatenate([np.transpose(inp["p_prompt"][:, c], (0, 2, 1)), np.transpose(ps, (0, 2, 1))], axis=2))
        m["ckT"] = f(np.transpose(np.reshape(inp["cache_sb_k"][:, 2 * c:2 * c + 2], (DEPTH, 2, PAST, BW)), (0, 1, 3, 2)))
        m["cv"] = f(np.reshape(inp["cache_sb_v"][:, 2 * c:2 * c + 2], (DEPTH, 2, PAST, BW)))
        m["spT"] = f(np.transpose(inp["state_pool"][:, 2 * c:2 * c + 2], (0, 1, 3, 2)))
        m["sgl"] = f(inp["state_gla"][:, 2 * c:2 * c + 2])
        maps.append(m)
    return maps


def assemble(results, S, DEPTH, n_cores):
    B = n_cores
    yp = np.zeros((B, S, D), np.float32); ys = np.zeros((2 * B, DEC, D), np.float32)
    kp = np.zeros((DEPTH, B, S, 8, 64), np.float32); vpo = np.zeros((DEPTH, B, S, 8, 64), np.float32)
    pp = np.zeros((DEPTH, B, 15, BW), np.float32); gp = np.zeros((DEPTH, B, 4, 64, 128), np.float32)
    ks = np.zeros((DEPTH, 2 * B, DEC, 8, 64), np.float32); vso = np.zeros((DEPTH, 2 * B, DEC, 8, 64), np.float32)
    pso = np.zeros((DEPTH, 2 * B, 15, BW), np.float32); gs = np.zeros((DEPTH, 2 * B, 4, 64, 128), np.float32)
    gm = np.zeros((DEPTH, 2 * B, DEC, BW), np.float32)
    for c, r in enumerate(results):
        yo = r["yout"]
        yp[c] = yo[:, :S].T
        ys[2 * c:2 * c + 2] = yo[:, S:].T.reshape(2, DEC, D)
        kp[:, c] = np.transpose(r["kpT"], (0, 2, 1)).reshape(DEPTH, S, 8, 64)
        vpo[:, c] = r["vp"].reshape(DEPTH, S, 8, 64)
        pp[:, c] = np.transpose(r["poolpT"], (0, 2, 1))
        gp[:, c] = r["glap"]
        ks[:, 2 * c:2 * c + 2] = np.transpose(r["ksT"], (0, 2, 1)).reshape(DEPTH, 2, DEC, 8, 64)
        vso[:, 2 * c:2 * c + 2] = r["vs"].reshape(DEPTH, 2, DEC, 8, 64)
        pso[:, 2 * c:2 * c + 2] = np.transpose(r["poolsT"], (0, 1, 3, 2))
        gs[:, 2 * c:2 * c + 2] = r["glas"]
        gm[:, 2 * c:2 * c + 2] = r["gms"].reshape(DEPTH, 2, DEC, BW)
    return (yp, ys, kp, vpo, pp, gp, ks, vso, pso, gs, gm)


def run(inp, S, PAST, DEPTH, n_cores=8):
    nc = build_program(S, PAST, DEPTH)
    maps = layout_inputs(inp, S, PAST, DEPTH, n_cores)
    res = run_bass_kernel_spmd(nc, maps, core_ids=list(range(n_cores)))
    return assemble(res.results, S, DEPTH, n_cores)


def kernel(**inputs):
    inp = {k: np.asarray(v) for k, v in inputs.items()}
    S = inp["x_prompt"].shape[1]
    PAST = inp["cache_sb_k"].shape[2]
    DEPTH = inp["w_in"].shape[0]
    return run(inp, S, PAST, DEPTH, 8)
```

```python
import numpy as np
from contextlib import ExitStack
import concourse.bass as bass
import concourse.mybir as mybir
from concourse.bass_utils import run_bass_kernel_spmd

F32 = mybir.dt.float32
BF16 = mybir.dt.bfloat16
AF = mybir.ActivationFunctionType
ALU = mybir.AluOpType

D = 1024
KD = 8
DFF = 2816
PLE = 256
BW = 512
NCOLS_IN = 8720
EPS = 1e-6
TT = 512
DEC = 32
NEGV = -30000.0
O_XP, O_Q, O_K, O_V, O_GU, O_GV, O_LQ, O_LK, O_LV, O_LA, O_LR, O_G = (
    0, 512, 1024, 1536, 2048, 2560, 3072, 3328, 3584, 4096, 4112, 4624)
NSLOT = 4
import os
DBG = int(os.environ.get("KDBG", "99"))
NCOL = 39
C_N1, C_NM, C_N2, C_NPL, C_PS, C_QN, C_KN, C_GON = 0, 8, 16, 24, 32, 36, 37, 38


class Rec:
    def __init__(self):
        self.engs = ["pe", "act", "dve", "pool", "sp"]
        self.ops = {e: [] for e in self.engs}
        self.count = {e: 0 for e in self.engs}
        self.waited = {e: {} for e in self.engs}
        self.res = {}
        self.dma_n = {"sp": 0, "pool": 0}
        self.dma_nsem = {"sp": 24, "pool": 68}
        self.dma_events = []
        self.pending = {}

    PERSIST = ("xT", "hT", "KT", "VC", "ws", "lw", "const", "phist", "S32", "Sbf", "ypool", "ysb", "ygm", "ygl",
               "sq0", "sq1", "rstd", "pT", "ps", "xd")

    def retire(self):
        for name in list(self.res.keys()):
            if name.startswith(self.PERSIST):
                continue
            w, rs = self.res.pop(name)
            for ev in ([w] if w else []) + rs:
                k, v = ev
                if self.pending.get(k, 0) < v:
                    self.pending[k] = v

    def _r(self, name):
        if name not in self.res:
            self.res[name] = [None, []]
        return self.res[name]

    def _deps(self, eng, reads, writes):
        deps = {}
        def add(ev):
            if ev is None:
                return
            k, v = ev
            if deps.get(k, 0) < v:
                deps[k] = v
        for nm in list(reads) + list(writes):
            if nm not in self.res and not nm.startswith(self.PERSIST):
                for k, v in self.pending.items():
                    add((k, v))
                break
        for r in reads:
            add(self._r(r)[0])
        for w in writes:
            e = self._r(w)
            add(e[0])
            for ev in e[1]:
                add(ev)
        waits = []
        for k, v in deps.items():
            if k == "pe" and eng == "pe":
                continue
            if self.waited[eng].get(k, 0) >= v:
                continue
            self.waited[eng][k] = v
            waits.append((k, v))
        return waits

    def _commit(self, ev, reads, writes):
        for r in reads:
            self._r(r)[1].append(ev)
        for w in writes:
            e = self._r(w)
            e[0] = ev
            e[1] = []

    def op(self, eng, fn, reads=(), writes=()):
        waits = self._deps(eng, reads, writes)
        self.count[eng] += 1
        ev = (eng, self.count[eng])
        self.ops[eng].append((waits, fn, ("cnt", eng)))
        self._commit(ev, reads, writes)

    def dma(self, q, fn, reads=(), writes=()):
        waits = self._deps(q, reads, writes)
        i = self.dma_n[q]
        self.dma_n[q] += 1
        n = self.dma_nsem[q]
        key = ("dma", q, i % n)
        if i >= n:
            prev = 16 * (i // n)
            if self.waited[q].get(key, 0) < prev:
                self.waited[q][key] = prev
                waits.append((key, prev))
        ev = (key, 16 * (i // n + 1))
        self.ops[q].append((waits, fn, ("dma", key)))
        self._commit(ev, reads, writes)
        self.dma_events.append(ev)
        return ev

    def fence(self, eng, events):
        waits = []
        for k, v in events:
            if self.waited[eng].get(k, 0) >= v:
                continue
            self.waited[eng][k] = v
            waits.append((k, v))
        if waits:
            self.ops[eng].append((waits, None, None))

    def barrier(self, engs=("pe", "act", "dve", "sp")):
        evs = [(e, self.count[e]) for e in ("pe", "act", "dve", "pool") if self.count[e] > 0]
        evs += self.dma_events
        self.dma_events = []
        for e in engs:
            waits = []
            for k, v in evs:
                if k == e:
                    continue
                if self.waited[e].get(k, 0) >= v:
                    continue
                self.waited[e][k] = v
                waits.append((k, v))
            if waits:
                self.ops[e].append((waits, None, None))

    def drop(self, names):
        for n in names:
            self.res.pop(n, None)


def build_program(S, PAST, DEPTH):
    NT = S // TT
    NTOK = S + 2 * DEC
    NPC = PAST // 128
    KTW = max(S, PAST + DEC)
    NVC = max(S // 128, NPC + 1)
    nc = bass.Bass("TRN2", target_bir_lowering=False)

    def din(name, shape):
        return nc.dram_tensor(name, list(shape), F32, kind="ExternalInput").ap()

    def dout(name, shape):
        return nc.dram_tensor(name, list(shape), F32, kind="ExternalOutput").ap()

    xin = din("xin", [D, NTOK])
    pin = din("pin", [DEPTH, PLE, NTOK])
    ckT = din("ckT", [DEPTH, 2, BW, PAST])
    cv = din("cv", [DEPTH, 2, PAST, BW])
    spT = din("spT", [DEPTH, 2, BW, 15])
    sgl = din("sgl", [DEPTH, 2, 4, 64, 128])
    cols_d = din("cols", [DEPTH, 128, NCOL])
    w_f1a = din("f1a", [DEPTH, D, DFF]); w_f1b = din("f1b", [DEPTH, D, DFF]); w_f1c = din("f1c", [DEPTH, DFF, D])
    w_f2a = din("f2a", [DEPTH, D, DFF]); w_f2b = din("f2b", [DEPTH, D, DFF]); w_f2c = din("f2c", [DEPTH, DFF, D])
    w_in = din("w_in", [DEPTH, D, NCOLS_IN])
    pool_w = din("pool_w", [DEPTH, 4, 128, 128])
    wsT_d = din("wsT", [DEPTH, 4, 128, 128])
    gb_d = din("gb", [DEPTH, 1, 512])
    wa2_d = din("wa2", [DEPTH, 16, 256])
    ba_d = din("ba", [DEPTH, 1, 256])
    w_br = din("w_br", [DEPTH, 4, BW, D])
    w_o = din("w_o", [DEPTH, D, D])
    w_pg = din("w_pg", [DEPTH, D, D])
    w_pp = din("w_pp", [DEPTH, PLE, D])
    c_ident = din("c_ident", [128, 128])
    c_tinc = din("c_tinc", [128, 128])
    c_trile = din("c_trile", [128, 128])
    c_negp = din("c_negp", [128, 4 * 512])
    c_negs = din("c_negs", [32, 256])
    c_bmask = din("c_bmask", [128, 128])
    c_invc = din("c_invc", [128, 4 * 16])

    yout = dout("yout", [D, NTOK])
    kpT = dout("kpT", [DEPTH, BW, S]); vp = dout("vp", [DEPTH, S, BW])
    poolpT = dout("poolpT", [DEPTH, BW, 15]); glap = dout("glap", [DEPTH, 4, 64, 128])
    ksT = dout("ksT", [DEPTH, BW, 2 * DEC]); vs = dout("vs", [DEPTH, 2 * DEC, BW])
    poolsT = dout("poolsT", [DEPTH, 2, BW, 15]); glas = dout("glas", [DEPTH, 2, 4, 64, 128])
    gms = dout("gms", [DEPTH, 2 * DEC, BW])
    xscr = nc.dram_tensor("xscr", [D, NTOK], F32, kind="Internal").ap()

    R = Rec()
    es = ExitStack()
    uniq = [0]

    def sbt(name, shape, dt=F32):
        uniq[0] += 1
        return nc.sbuf_tensor(f"{name}_{uniq[0]}", list(shape), dt)

    def sb(name, shape, dt=F32):
        return es.enter_context(sbt(name, list(shape), dt))

    xT = sb("xT", [128, KD, TT])
    hT = sb("hT", [128, KD, TT], BF16)
    KT = sb("KT", [128, 4, KTW], BF16)
    VC = sb("VC", [128, NVC, BW], BF16)
    WS = [sb(f"ws{i}", [128, 8, 512], BF16) for i in range(NSLOT)]
    cols = sb("cols_sb", [128, NCOL])
    poolw_sb = sb("poolw_sb", [128, 4, 128], BF16)
    wmT = sb("wmT", [128, 4, 128], BF16)
    gb_hi = sb("gb_hi", [1, 512], BF16); gb_lo = sb("gb_lo", [1, 512], BF16); gb_f = sb("gb_f", [1, 512])
    wa2_sb = sb("wa2_sb", [16, 256], BF16)
    ba_sb = sb("ba_sb", [1, 256], BF16)
    wla_sb = sb("wla_sb", [128, 8, 16], BF16)
    wpp_sb = sb("wpp_sb", [128, 2, D], BF16)
    ident = sb("ident", [128, 128], BF16)
    tinc = sb("tinc", [128, 128], BF16)
    ones_bf = sb("ones_bf", [128, 128], BF16)
    bdiag = sb("bdiag", [128, 128], BF16)
    trile = sb("trile", [128, 128])
    negp = sb("negp", [128, 4 * 512], BF16)
    negs = sb("negs", [32, 256], BF16)
    bmask = sb("bmask", [128, 128], BF16)
    invc = sb("invc", [128, 4, 16])
    phist = sb("phist", [128, 4, 15])
    S32 = sb("S32", [128, 512]); Sbf = sb("Sbf", [128, 512], BF16)
    SQ = sb("SQ", [128, 2, TT], BF16); RSTD = sb("RSTD", [128, TT])
    pT = sb("pT", [128, 2, TT], BF16)
    nident = sb("nident", [128, 128], BF16)
    ypool = sb("ypool", [128, 4, TT], BF16); ysb = sb("ysb", [128, 4, TT], BF16)
    ygm = sb("ygm", [128, 4, TT], BF16); ygl = sb("ygl", [128, 4, TT], BF16)
    PS = [es.enter_context(nc.psum_tensor(f"ps{i}", [128, 512], F32)) for i in range(7)]
    PSB = es.enter_context(nc.psum_tensor("psb", [128, 1024], BF16))

    slot_ctr = [0]

    WIN_OFFS = [O_XP, O_Q, O_K, O_V, O_GU, O_GV, O_LQ, O_LV, O_LR] + [O_G + i * 512 for i in range(8)]
    FBLK = [(cb * 512, min(512, DFF - cb * 512)) for cb in range((DFF + 511) // 512)]
    KBS = [(0, 8), (8, 8), (16, 6)]
    wsrc = {"f1a": w_f1a, "f1b": w_f1b, "f1c": w_f1c, "f2a": w_f2a, "f2b": w_f2b, "f2c": w_f2c, "w_in": w_in,
            "w_br": w_br, "w_o": w_o, "w_pg": w_pg}
    wblocks = {}
    for nm in ("f1a", "f1b", "f2a", "f2b"):
        wblocks[nm] = [(cb, (lambda l, nm=nm, c0=c0, nco=nco: wsrc[nm][l][:, c0:c0 + nco]), 8, nco)
                       for cb, (c0, nco) in enumerate(FBLK)]
    for nm in ("f1c", "f2c"):
        wblocks[nm] = [((nb, bi), (lambda l, nm=nm, nb=nb, k0=k0, kn=kn: wsrc[nm][l][k0 * 128:(k0 + kn) * 128,
                                                                                      nb * 512:(nb + 1) * 512]), kn, 512)
                       for nb in range(2) for bi, (k0, kn) in enumerate(KBS)]
    wblocks["w_in"] = [(off, (lambda l, off=off: w_in[l][:, off:off + 512]), 8, 512) for off in WIN_OFFS]
    wblocks["w_br"] = [((b_, nb), (lambda l, b_=b_, nb=nb: w_br[l, b_][:, nb * 512:(nb + 1) * 512]), 4, 512)
                       for b_ in range(4) for nb in range(2)]
    for nm in ("w_o", "w_pg"):
        wblocks[nm] = [(nb, (lambda l, nm=nm, nb=nb: wsrc[nm][l][:, nb * 512:(nb + 1) * 512]), 8, 512) for nb in range(2)]
    wscr = {}
    widx = {}
    for nm, bl in wblocks.items():
        wscr[nm] = nc.dram_tensor(f"{nm}_bf", [DEPTH, len(bl), 128, 8, 512], BF16, kind="Internal").ap()
        widx[nm] = {b[0]: (i, b[2], b[3]) for i, b in enumerate(bl)}
    conv_ev = {}

    conv_q = []

    def convert_layer(l, defer=False):
        for nm in ("f1a", "f1b", "f1c", "w_in", "w_br", "w_o", "f2a", "f2b", "f2c", "w_pg"):
            conv_ev[(nm, l)] = []
            for i, (idx, srcf, kcs, nco) in enumerate(wblocks[nm]):
                src = srcf(l).rearrange("(kc p) n -> p kc n", p=128)
                dst = wscr[nm][l, i][:, 0:kcs, 0:nco]
                conv_q.append((nm, l, dst, src))
        if not defer:
            conv_step(len(conv_q))

    def conv_step(n=1):
        for _ in range(min(n, len(conv_q))):
            nm, l, dst, src = conv_q.pop(0)
            conv_ev[(nm, l)].append(R.dma("pool", lambda e, dst=dst, src=src: e.dma_start(out=dst, in_=src)))

    def load_w(nm, l, idx):
        i, kcs, nco = widx[nm][idx]
        s = slot_ctr[0] % NSLOT
        slot_ctr[0] += 1
        t = WS[s]
        R.fence("sp", conv_ev[(nm, l)])
        src = wscr[nm][l, i][:, 0:kcs, 0:nco]
        R.dma("sp", lambda e, t=t, src=src, kcs=kcs, nco=nco: e.dma_start(out=t[:, 0:kcs, 0:nco], in_=src),
              writes=[f"ws{s}"])
        return t, f"ws{s}"

    def mm(out, lhsT, rhs, start, stop):
        return lambda pe: pe.matmul(out, lhsT, rhs, start=start, stop=stop, skip_group_check=True)

    def pe_ops(fns, reads, writes):
        def run(pe, fns=fns):
            last = None
            for f in fns:
                last = f(pe)
            return last
        R.op("pe", run, reads=reads, writes=writes)

    R.dma("pool", lambda e: e.dma_start(out=ident[:], in_=c_ident), writes=["const"])
    R.dma("pool", lambda e: e.dma_start(out=tinc[:], in_=c_tinc), writes=["const"])
    R.dma("pool", lambda e: e.dma_start(out=negp[:], in_=c_negp), writes=["const"])
    R.dma("pool", lambda e: e.dma_start(out=negs[:], in_=c_negs), writes=["const"])
    R.dma("pool", lambda e: e.dma_start(out=bmask[:], in_=c_bmask), writes=["const"])
    R.dma("sp", lambda e: e.dma_start(out=trile[:], in_=c_trile), writes=["const"])
    R.dma("sp", lambda e: e.dma_start(out=invc[:], in_=c_invc.rearrange("p (g t) -> p g t", g=4)), writes=["const"])
    R.op("dve", lambda e: e.memset(ones_bf[:], 1.0), writes=["const2"])
    R.op("dve", lambda e: e.memset(bdiag[:], 0.0), writes=["const2"])
    R.op("dve", lambda e: e.memset(bdiag[0:64, 0:64], 1.0), writes=["const2"])
    R.op("dve", lambda e: e.memset(bdiag[64:128, 64:128], 1.0), writes=["const2"])
    R.op("dve", lambda e: e.tensor_scalar(out=nident[:], in0=ident[:], scalar1=-1.0, scalar2=None, op0=ALU.mult),
         reads=["const"], writes=["const2"])
    R.barrier(("pe", "act", "dve", "sp", "pool"))

    ht_fresh = [False]

    def rsqrt_ip(ap, rn="rstd"):
        R.op("act", lambda e: e.activation(out=ap, in_=ap, func=AF.Ln), reads=[rn], writes=[rn])
        R.op("act", lambda e: e.activation(out=ap, in_=ap, func=AF.Exp, scale=-0.5), reads=[rn], writes=[rn])

    def rmsnorm(ntok, ccol, ph):
        rstd = RSTD
        for k in range(KD):
            if k % 2 == 0:
                R.op("act", lambda e, k=k: e.activation(out=SQ[:, 0, 0:ntok], in_=xT[:, k, 0:ntok], func=AF.Square),
                     reads=["xT"], writes=["sq0"])
            else:
                R.op("pool", lambda e, k=k: e.tensor_tensor(out=SQ[:, 1, 0:ntok], in0=xT[:, k, 0:ntok], in1=xT[:, k, 0:ntok],
                                                            op=ALU.mult), reads=["xT"], writes=["sq1"])
            pe_ops([mm(PS[6][:, 0:ntok], ones_bf[:], SQ[:, k % 2, 0:ntok], k == 0, k == KD - 1)],
                   reads=[f"sq{k % 2}", "const2"], writes=["ps6"])
        R.op("act", lambda e: e.activation(out=rstd[:, 0:ntok], in_=PS[6][:, 0:ntok], func=AF.Ln, bias=EPS, scale=1.0 / D),
             reads=["ps6"], writes=["rstd"])
        R.op("act", lambda e: e.activation(out=rstd[:, 0:ntok], in_=rstd[:, 0:ntok], func=AF.Exp, scale=-0.5),
             reads=["rstd"], writes=["rstd"])
        ht_fresh[0] = True
        for k in range(KD):
            R.op("dve", lambda e, k=k: e.scalar_tensor_tensor(out=hT[:, k, 0:ntok], in0=xT[:, k, 0:ntok],
                                                            scalar=cols[:, ccol + k:ccol + k + 1], in1=rstd[:, 0:ntok],
                                                            op0=ALU.mult, op1=ALU.mult),
                 reads=["xT", "rstd", "lw"], writes=[f"hT{k}"])

    HT_ALL = [f"hT{k}" for k in range(KD)]

    def ht_ops(fns, extra_reads, writes):
        if ht_fresh[0] and len(fns) == KD:
            ht_fresh[0] = False
            for k, f in enumerate(fns):
                pe_ops([f], reads=[f"hT{k}"] + list(extra_reads), writes=writes)
        else:
            pe_ops(fns, reads=HT_ALL + list(extra_reads), writes=writes)
    psrot = [0]

    def next_ps(n=6):
        i = psrot[0] % n
        psrot[0] += 1
        return i

    def ffn(l, ntok, wa, wb, wc, ccol):
        conv_step(1)
        with ExitStack() as ph_es:
            ph = None
            act = ph_es.enter_context(sbt("f_act", [128, 22, TT], BF16))
            sl = [ph_es.enter_context(sbt(f"f_sl{i}", [128, TT], F32)) for i in range(2)]
            rmsnorm(ntok, ccol, ph)
            nblk = (DFF + 511) // 512
            ci = 0
            for cb in range(nblk):
                c0 = cb * 512
                ncol = min(512, DFF - c0)
                ta, ra = load_w(wa, l, cb)
                tb, rb = load_w(wb, l, cb)
                for j in range(ncol // 128):
                    f = cb * 4 + j
                    ia = next_ps(); ib = next_ps()
                    ht_ops([mm(PS[ia][:, 0:ntok], ta[:, k, j * 128:(j + 1) * 128], hT[:, k, 0:ntok], k == 0, k == KD - 1)
                            for k in range(KD)], [ra], [f"ps{ia}"])
                    ht_ops([mm(PS[ib][:, 0:ntok], tb[:, k, j * 128:(j + 1) * 128], hT[:, k, 0:ntok], k == 0, k == KD - 1)
                            for k in range(KD)], [rb], [f"ps{ib}"])
                    s = sl[ci % 2]; ci += 1
                    R.op("act", lambda e, s=s, ia=ia: e.activation(out=s[:, 0:ntok], in_=PS[ia][:, 0:ntok], func=AF.Silu),
                         reads=[f"ps{ia}"], writes=[f"sl{id(s)}"])
                    R.op("dve", lambda e, s=s, ib=ib, f=f: e.tensor_tensor(out=act[:, f, 0:ntok], in0=PS[ib][:, 0:ntok],
                                                                           in1=s[:, 0:ntok], op=ALU.mult),
                         reads=[f"ps{ib}", f"sl{id(s)}"], writes=[f"act{f}"])
            kbs = [(0, 8), (8, 8), (16, 6)]
            for nb in range(2):
                banks = [next_ps() for _ in range(4)]
                for bi, (k0, kn) in enumerate(kbs):
                    t, r = load_w(wc, l, (nb, bi))
                    for j in range(4):
                        pe_ops([mm(PS[banks[j]][:, 0:ntok], t[:, k, j * 128:(j + 1) * 128], act[:, k0 + k, 0:ntok],
                                   (bi == 0 and k == 0), (bi == 2 and k == kn - 1)) for k in range(kn)],
                               reads=[f"act{k0 + k}" for k in range(kn)] + [r], writes=[f"ps{banks[j]}"])
                for j in range(4):
                    n = nb * 4 + j
                    R.op("dve", lambda e, n=n, b=banks[j]: e.scalar_tensor_tensor(
                        out=xT[:, n, 0:ntok], in0=PS[b][:, 0:ntok], scalar=0.5, in1=xT[:, n, 0:ntok],
                        op0=ALU.mult, op1=ALU.add), reads=[f"ps{banks[j]}", "xT"], writes=["xT"])
            R.retire()

    def fm_block(t, r, ncol, ntok, evac):
        for j in range((ncol + 127) // 128):
            w = min(128, ncol - j * 128)
            b = next_ps()
            ht_ops([mm(PS[b][0:w, 0:ntok], t[:, k, j * 128:j * 128 + w], hT[:, k, 0:ntok], k == 0, k == KD - 1)
                    for k in range(KD)], [r], [f"ps{b}"])
            evac(j, b)

    def tm_block(t, r, subs, evac):
        for si, (c0, n) in enumerate(subs):
            b = next_ps()
            ht_ops([mm(PS[b][0:n, 0:512], hT[:, k, c0:c0 + n], t[:, k, 0:512], k == 0, k == KD - 1)
                    for k in range(KD)], [r], [f"ps{b}"])
            evac(si, b, n)

    def sb_attend(streams, N, qn):
        nch = len(streams[0]["chunks"])
        for st_ in streams:
            R.op("dve", lambda e, st_=st_: e.memset(st_["R"][:, 0:N], 0.0), writes=[f"R{st_['sid']}"])

        def stage1(st_, i):
            sid = st_["sid"]; groups = st_["groups"]
            c, key0, nk, ng = st_["chunks"][i]
            sbk = sid
            e_ap = st_["e"][i % 2]; en = st_["en"][i % 2]
            fns = []
            for gi, (hc, hp, h, oc, nq, col0, qc0) in enumerate(groups):
                last = (gi == len(groups) - 1) and ng is None
                if hp is None:
                    fns.append(mm(PS[sbk][0:nk, col0:col0 + nq], KT[:, hc, key0:key0 + nk],
                                  qn[:, h, qc0:qc0 + nq], gi == 0, last))
                else:
                    fns.append(mm(PS[sbk][0:nk, col0:col0 + nq], KT[hp:hp + 64, hc, key0:key0 + nk],
                                  qn[hp:hp + 64, hc, qc0:qc0 + nq], gi == 0, last))
            if ng is not None:
                fns.append(mm(PS[sbk][0:nk, 0:N], ident[0:nk, 0:nk], ng, False, True))
            pe_ops(fns, reads=["KT", "qn", "const", "const2"], writes=[f"ps{sbk}"])
            R.op("act", lambda e: e.activation(out=e_ap[0:nk, 0:N], in_=PS[sbk][0:nk, 0:N], func=AF.Exp, scale=0.125),
                 reads=[f"ps{sbk}"], writes=[en])
            R.op("act", lambda e: e.activation(out=st_["sp"][i % 2][0:nk, 0:N], in_=e_ap[0:nk, 0:N], func=AF.Ln,
                                               bias=1.0, scale=1.0), reads=[en], writes=[f"sp{sid}{i % 2}"])

        def stage2(st_, i):
            sid = st_["sid"]
            c, key0, nk, ng = st_["chunks"][i]
            pb = 2 + sid; qb = 4 + sid
            sp_ap = st_["sp"][i % 2]; spn = f"sp{sid}{i % 2}"
            e_ap = st_["e"][i % 2]; en = st_["en"][i % 2]
            t_ap = st_["tmp"][i % 2]; tn = st_["tn"][i % 2]
            a_ap = st_["a"][i % 2]; an = f"a{sid}{i % 2}"
            Rr = st_["R"]; rn = f"R{sid}"
            pe_ops([mm(PS[pb][0:nk, 0:N], tinc[0:nk, 0:nk], sp_ap[0:nk, 0:N], True, True)], reads=[spn, "const"],
                   writes=[f"ps{pb}"])
            if i < nch - 1:
                pe_ops([mm(PS[qb][:, 0:N], ones_bf[0:nk, :], sp_ap[0:nk, 0:N], True, True)], reads=[spn, "const2"],
                       writes=[f"ps{qb}"])
            R.op("dve", lambda e: e.tensor_tensor(out=t_ap[0:nk, 0:N], in0=PS[pb][0:nk, 0:N], in1=Rr[0:nk, 0:N], op=ALU.add),
                 reads=[f"ps{pb}", rn], writes=[tn])
            R.op("act", lambda e: e.activation(out=t_ap[0:nk, 0:N], in_=t_ap[0:nk, 0:N], func=AF.Exp, scale=-1.0),
                 reads=[tn], writes=[tn])
            R.op("pool", lambda e: e.tensor_tensor(out=a_ap[0:nk, 0:N], in0=e_ap[0:nk, 0:N], in1=t_ap[0:nk, 0:N], op=ALU.mult),
                 reads=[en, tn], writes=[an])
            if i < nch - 1:
                R.op("dve", lambda e: e.tensor_tensor(out=Rr[:, 0:N], in0=PS[qb][:, 0:N], in1=Rr[:, 0:N], op=ALU.add),
                     reads=[f"ps{qb}", rn], writes=[rn])

        def stage3(st_, i):
            sid = st_["sid"]; groups = st_["groups"]
            c, key0, nk, ng = st_["chunks"][i]
            a_ap = st_["a"][i % 2]; an = f"a{sid}{i % 2}"
            fns = []
            for gi, (hc, hp, h, oc, nq, col0, qc0) in enumerate(groups):
                ohp = (h % 2) * 64
                st0 = (i == 0) and (gi < 2)
                fns.append(mm(PS[6][ohp:ohp + 64, oc:oc + nq], VC[0:nk, c, h * 64:(h + 1) * 64],
                              a_ap[0:nk, col0:col0 + nq], st0, i == nch - 1))
            pe_ops(fns, reads=[an, "VC"], writes=["ps6"])

        for i in range(nch + 2):
            if i < nch:
                for st_ in streams:
                    stage1(st_, i)
            if 1 <= i <= nch:
                for st_ in streams:
                    stage2(st_, i - 1)
            if i >= 2:
                for st_ in streams:
                    stage3(st_, i - 2)

    def mixer(l, tile):
        ntok, subs, segs, c0g, is_sample, tix = tile
        nsub = len(subs)
        nseg = len(segs)
        L = segs[0][1]
        mx = ExitStack()

        def ph_sb(name, shape, dt=F32):
            return mx.enter_context(sbt(name, list(shape), dt))
        rmsnorm(ntok, C_NM, None)

        conv_step(1)
        with ExitStack() as pa:
            xpe = pa.enter_context(sbt("a_xpe", [128, 4, nseg, 15 + L], F32))
            tb_ = [pa.enter_context(sbt(f"a_t{i}", [128, nseg, 15 + L], F32)) for i in range(4)]
            dd = pa.enter_context(sbt("a_d", [128, 4, nseg, L], BF16))
            XPE = [f"xpe{g}" for g in range(4)]
            t, r = load_w("w_in", l, O_XP)
            R.op("dve", lambda e: e.memset(tb_[0][:], 0.0), writes=["t0"])
            R.op("dve", lambda e: e.memset(tb_[1][:], 0.0), writes=["t1"])
            R.op("pool", lambda e: e.memset(tb_[2][:], 0.0), writes=["t2"])
            R.op("pool", lambda e: e.memset(tb_[3][:], 0.0), writes=["t3"])
            if is_sample:
                for s in range(2):
                    R.dma("sp", lambda e, s=s: e.dma_start(out=xpe[:, :, s, 0:15],
                                                           in_=spT[l, s].rearrange("(g p) t -> p g t", p=128)),
                          writes=XPE)
            else:
                R.op("dve", lambda e: e.tensor_copy(out=xpe[:, :, 0, 0:15], in_=phist[:]), reads=["phist"], writes=XPE)

            def ev_xp(j, b):
                R.op("act", lambda e, j=j, b=b: e.activation(
                    out=xpe[:, j, :, 15:15 + L], in_=PS[b][:, 0:ntok].rearrange("p (s t) -> p s t", s=nseg), func=AF.Copy),
                    reads=[f"ps{b}"], writes=[f"xpe{j}"])
            fm_block(t, r, 512, ntok, ev_xp)
            for g in (3, 0, 1, 2):
                w = 2 << g
                eng = "pool" if g == 3 else "dve"
                ti = (2, 3) if g == 3 else (0, 1)
                src = xpe[:, g]
                cur = None
                curn = None
                sh = 1
                for step in range(g + 1):
                    dst = tb_[ti[step % 2]]; dstn = f"t{ti[step % 2]}"
                    a_in = src if cur is None else cur
                    a_n = f"xpe{g}" if cur is None else curn
                    R.op(eng, lambda e, dst=dst, a_in=a_in, sh=sh: e.tensor_tensor(
                        out=dst[:, :, sh:15 + L], in0=a_in[:, :, sh:15 + L], in1=a_in[:, :, 0:15 + L - sh], op=ALU.add),
                        reads=[a_n], writes=[dstn])
                    cur = dst; curn = dstn
                    sh *= 2
                if eng == "pool":
                    oth = tb_[ti[(g + 1) % 2]]; othn = f"t{ti[(g + 1) % 2]}"
                    R.op(eng, lambda e, cur=cur, oth=oth, w=w: e.tensor_scalar(out=oth[:, :, 15:15 + L], in0=cur[:, :, 15:15 + L],
                                                                               scalar1=1.0 / w, scalar2=None, op0=ALU.mult),
                         reads=[curn], writes=[othn])
                    R.op(eng, lambda e, oth=oth, g=g: e.tensor_tensor(out=dd[:, g], in0=oth[:, :, 15:15 + L],
                                                                      in1=xpe[:, g, :, 15:15 + L], op=ALU.subtract),
                         reads=[othn, f"xpe{g}"], writes=[f"dd{g}"])
                else:
                    R.op(eng, lambda e, cur=cur, g=g, w=w: e.scalar_tensor_tensor(
                        out=dd[:, g], in0=cur[:, :, 15:15 + L], scalar=1.0 / w, in1=xpe[:, g, :, 15:15 + L],
                        op0=ALU.mult, op1=ALU.subtract), reads=[curn, f"xpe{g}"], writes=[f"dd{g}"])
                if (not is_sample) and tix == 0:
                    R.op(eng, lambda e, cur=cur, g=g: e.tensor_tensor(out=cur[:, 0, 15:31], in0=cur[:, 0, 15:31],
                                                                      in1=invc[:, g, :], op=ALU.mult),
                         reads=[curn, "const", f"dd{g}"], writes=[curn])
                    R.op(eng, lambda e, cur=cur, g=g: e.tensor_tensor(out=dd[:, g, 0, 0:16], in0=cur[:, 0, 15:31],
                                                                      in1=xpe[:, g, 0, 15:31], op=ALU.subtract),
                         reads=[curn, f"xpe{g}"], writes=[f"dd{g}"])
            for g in range(4):
                b = next_ps()
                pe_ops([mm(PS[b][:, 0:ntok], poolw_sb[:, g, :], dd[:, g].rearrange("p s t -> p (s t)"), True, True)],
                       reads=[f"dd{g}", "lw"], writes=[f"ps{b}"])
                R.op("act", lambda e, g=g, b=b: e.activation(out=ypool[:, g, 0:ntok], in_=PS[b][:, 0:ntok], func=AF.Copy,
                                                             scale=cols[:, C_PS + g:C_PS + g + 1]),
                     reads=[f"ps{b}", "lw"], writes=["ypool"])
            if is_sample:
                for s in range(2):
                    R.dma("sp", lambda e, s=s: e.dma_start(out=poolsT[l, s].rearrange("(g p) t -> p g t", p=128),
                                                           in_=xpe[:, :, s, L:L + 15]), reads=XPE)
            else:
                R.op("dve", lambda e: e.tensor_copy(out=phist[:], in_=xpe[:, :, 0, L:L + 15]), reads=XPE, writes=["phist"])
                if tix == NT - 1:
                    R.dma("sp", lambda e: e.dma_start(out=poolpT[l].rearrange("(g p) t -> p g t", p=128),
                                                      in_=xpe[:, :, 0, L:L + 15]), reads=XPE)
            R.retire()

        if DBG < 3:
            mx.close()
            return
        conv_step(1)
        with ExitStack() as pb_:
            def bsb(name, shape, dt=F32):
                return pb_.enter_context(sbt(name, list(shape), dt))
            qf = bsb("b_qf", [128, 4, TT]); kf = bsb("b_kf", [128, 4, TT])
            qn = bsb("b_qn", [128, 4, TT], BF16)
            rsb = [RSTD, bsb("b_rs2", [128, TT])]; rsn = ["rstd", "rs2"]
            vst = bsb("b_vst", [128, nsub, BW])
            QF = [f"qf{j}" for j in range(4)]; KF = [f"kf{j}" for j in range(4)]
            nstream = 1 if is_sample else 2
            strm = []
            for sid in range(nstream):
                strm.append({"sid": sid,
                             "e": [qf[:, 2 * sid + i, :] for i in range(2)], "en": [f"qf{2 * sid + i}" for i in range(2)],
                             "tmp": [kf[:, 2 * sid + i, :] for i in range(2)], "tn": [f"kf{2 * sid + i}" for i in range(2)],
                             "sp": [bsb(f"b_sp{sid}{i}", [128, 512], BF16) for i in range(2)],
                             "a": [bsb(f"b_a{sid}{i}", [128, 512], BF16) for i in range(2)],
                             "R": bsb(f"b_R{sid}", [128, 512])})
            t, r = load_w("w_in", l, O_Q)
            fm_block(t, r, 512, ntok, lambda j, b: R.op("act", lambda e, j=j, b=b: e.activation(
                out=qf[:, j, 0:ntok], in_=PS[b][:, 0:ntok], func=AF.Copy), reads=[f"ps{b}"], writes=[f"qf{j}"]))
            t, r = load_w("w_in", l, O_K)
            fm_block(t, r, 512, ntok, lambda j, b: R.op("act", lambda e, j=j, b=b: e.activation(
                out=kf[:, j, 0:ntok], in_=PS[b][:, 0:ntok], func=AF.Copy), reads=[f"ps{b}"], writes=[f"kf{j}"]))

            def qknorm(src, pre, outs):
                banks = {}
                def stA(j):
                    R.op("pool", lambda e, j=j: e.tensor_tensor(out=SQ[:, j % 2, 0:ntok], in0=src[:, j, 0:ntok],
                                                                in1=src[:, j, 0:ntok], op=ALU.mult),
                         reads=[f"{pre}{j}"], writes=[f"sq{j % 2}"])
                    b = next_ps(); banks[j] = b
                    pe_ops([mm(PS[b][:, 0:ntok], bdiag[:], SQ[:, j % 2, 0:ntok], True, True)], reads=[f"sq{j % 2}", "const2"],
                           writes=[f"ps{b}"])
                    R.op("dve", lambda e, b=b, j=j: e.tensor_scalar(out=rsb[j % 2][:, 0:ntok], in0=PS[b][:, 0:ntok],
                                                                    scalar1=1.0 / 64, scalar2=EPS, op0=ALU.mult, op1=ALU.add),
                         reads=[f"ps{b}"], writes=[rsn[j % 2]])
                def stB(j):
                    rsqrt_ip(rsb[j % 2][:, 0:ntok], rsn[j % 2])
                    outs(j)
                for st in range(5):
                    if st < 4:
                        stA(st)
                    if st >= 1:
                        stB(st - 1)
            if is_sample:
                qz = bsb("b_qz", [128, 8, 2 * DEC], BF16)
                R.op("dve", lambda e: e.memset(qz[:], 0.0), writes=["qn"])
            def q_out(j):
                if is_sample:
                    R.op("dve", lambda e, j=j: e.scalar_tensor_tensor(out=qf[:, j, 0:ntok], in0=qf[:, j, 0:ntok],
                                                                      scalar=cols[:, C_QN:C_QN + 1], in1=rsb[j % 2][:, 0:ntok],
                                                                      op0=ALU.mult, op1=ALU.mult),
                         reads=[f"qf{j}", rsn[j % 2], "lw"], writes=[f"qf{j}"])
                    for hh in range(2):
                        h = 2 * j + hh
                        R.op("act", lambda e, j=j, h=h, hh=hh: e.activation(
                            out=qz[hh * 64:hh * 64 + 64, h, :], in_=qf[hh * 64:hh * 64 + 64, j, 0:ntok], func=AF.Copy),
                            reads=[f"qf{j}"], writes=["qn"])
                else:
                    R.op("dve", lambda e, j=j: e.scalar_tensor_tensor(out=qn[:, j, 0:ntok], in0=qf[:, j, 0:ntok],
                                                                      scalar=cols[:, C_QN:C_QN + 1], in1=rsb[j % 2][:, 0:ntok],
                                                                      op0=ALU.mult, op1=ALU.mult),
                         reads=[f"qf{j}", rsn[j % 2], "lw"], writes=["qn"])
            qknorm(qf, "qf", q_out)
            def k_out(j):
                R.op("dve", lambda e, j=j: e.scalar_tensor_tensor(out=kf[:, j, 0:ntok], in0=kf[:, j, 0:ntok],
                                                                  scalar=cols[:, C_KN:C_KN + 1], in1=rsb[j % 2][:, 0:ntok],
                                                                  op0=ALU.mult, op1=ALU.mult),
                     reads=[f"kf{j}", rsn[j % 2], "lw"], writes=[f"kf{j}"])
            qknorm(kf, "kf", k_out)
            t, r = load_w("w_in", l, O_V)
            tm_block(t, r, subs, lambda si, b, n: R.op("act", lambda e, si=si, b=b, n=n: e.activation(
                out=vst[0:n, si, :], in_=PS[b][0:n, :], func=AF.Copy), reads=[f"ps{b}"], writes=["vst"]))
            if is_sample:
                R.dma("sp", lambda e: e.dma_start(out=ksT[l].rearrange("(j p) t -> p j t", p=128), in_=kf[:, :, 0:ntok]),
                      reads=KF)
                for si in range(2):
                    R.dma("sp", lambda e, si=si: e.dma_start(out=vs[l, si * DEC:(si + 1) * DEC, :], in_=vst[0:DEC, si, :]),
                          reads=["vst"])
            else:
                R.dma("sp", lambda e: e.dma_start(out=kpT[l].rearrange("(j p) t -> p j t", p=128)[:, :, c0g:c0g + ntok],
                                                  in_=kf[:, :, 0:ntok]), reads=KF)
                R.dma("sp", lambda e: e.dma_start(out=vp[l, c0g:c0g + ntok, :].rearrange("(s p) f -> p s f", p=128),
                                                  in_=vst[:, :, :]), reads=["vst"])
            if is_sample:
                knew = bsb("b_knew", [128, 4, 2 * DEC], BF16)
                R.op("act", lambda e: e.activation(out=knew[:], in_=kf[:, :, 0:ntok], func=AF.Copy), reads=KF, writes=["knew"])
                for s in range(2):
                    for j in range(4):
                        R.dma("pool", lambda e, s=s, j=j: e.dma_start(out=KT[:, j, 0:PAST],
                                                                      in_=ckT[l, s, j * 128:(j + 1) * 128, :]),
                              writes=["KT"])
                    for c8 in range(0, NPC, 8):
                        c9 = min(NPC, c8 + 8)
                        R.dma("pool", lambda e, s=s, c8=c8, c9=c9: e.dma_start(
                            out=VC[:, c8:c9, :], in_=cv[l, s, c8 * 128:c9 * 128, :].rearrange("(c p) f -> p c f", p=128)),
                            writes=["VC"])
                    R.op("act", lambda e, s=s: e.activation(out=KT[:, :, PAST:PAST + DEC], in_=knew[:, :, s * DEC:(s + 1) * DEC],
                                                            func=AF.Copy), reads=["knew"], writes=["KT"])
                    R.op("act", lambda e, s=s: e.activation(out=VC[0:DEC, NPC, :], in_=vst[0:DEC, s, :], func=AF.Copy),
                         reads=["vst"], writes=["VC"])
                    strm[0]["groups"] = [(h // 2, None, h, (h // 2) * DEC, DEC, h * DEC, s * DEC) for h in range(8)]
                    strm[0]["chunks"] = [(NPC, PAST, DEC, negs[:, :])] + [(c, c * 128, 128, None) for c in range(NPC - 1, -1, -1)]
                    sb_attend(strm, 8 * DEC, qz)
                    R.op("act", lambda e, s=s: e.activation(
                        out=ysb[:, :, s * DEC:(s + 1) * DEC], in_=PS[6][:, 0:4 * DEC].rearrange("p (m t) -> p m t", m=4),
                        func=AF.Copy), reads=["ps6"], writes=["ysb"])
            else:
                pc0 = c0g // 128
                R.op("act", lambda e: e.activation(out=KT[:, :, c0g:c0g + ntok], in_=kf[:, :, 0:ntok], func=AF.Copy),
                     reads=KF, writes=["KT"])
                R.op("act", lambda e: e.activation(out=VC[:, pc0:pc0 + 4, :], in_=vst[:, :, :], func=AF.Copy),
                     reads=["vst"], writes=["VC"])
                chunks = []
                for c in range(pc0 + 3, -1, -1):
                    dgi = c - pc0
                    chunks.append((c, c * 128, 128, negp[:, dgi * 512:(dgi + 1) * 512] if dgi >= 0 else None))
                for m in range(4):
                    for hh in range(2):
                        strm[hh]["groups"] = [(m, hh * 64, 2 * m + hh, 0, TT, 0, 0)]
                        strm[hh]["chunks"] = chunks
                    sb_attend(strm, TT, qn)
                    R.op("act", lambda e, m=m: e.activation(out=ysb[:, m, 0:ntok], in_=PS[6][:, 0:ntok], func=AF.Copy),
                         reads=["ps6"], writes=["ysb"])
            R.retire()

        if DBG < 4:
            mx.close()
            return
        conv_step(1)
        with ExitStack() as pc_:
            gu = pc_.enter_context(sbt("c_gu", [128, 4, TT], F32))
            gvb = pc_.enter_context(sbt("c_gvb", [128, nsub, BW], BF16))
            gvf = pc_.enter_context(sbt("c_gvf", [128, nsub, BW], F32))
            t, r = load_w("w_in", l, O_GU)
            fm_block(t, r, 512, ntok, lambda j, b: R.op("act", lambda e, j=j, b=b: e.activation(
                out=gu[:, j, 0:ntok], in_=PS[b][:, 0:ntok], func=AF.Gelu_apprx_tanh), reads=[f"ps{b}"], writes=["gu"]))
            t, r = load_w("w_in", l, O_GV)
            def ev_gv(si, b, n):
                R.op("act", lambda e: e.activation(out=gvf[0:n, si, :], in_=PS[b][0:n, :], func=AF.Gelu_apprx_tanh),
                     reads=[f"ps{b}"], writes=["gvf"])
                R.op("dve", lambda e: e.tensor_copy(out=gvb[0:n, si, :], in_=gvf[0:n, si, :]), reads=["gvf"], writes=["gvb"])
            tm_block(t, r, subs, ev_gv)
            if is_sample:
                for si in range(2):
                    R.dma("sp", lambda e, si=si: e.dma_start(out=gms[l, si * DEC:(si + 1) * DEC, :], in_=gvf[0:DEC, si, :]),
                          reads=["gvf"])
            for si, (c0, n) in enumerate(subs):
                b = next_ps()
                fns = []
                for g in range(4):
                    fns.append(mm(PS[b][:, g * 128:g * 128 + n], gvb[0:n, si, g * 128:(g + 1) * 128], wmT[0:n, g, 0:n],
                                  g == 0, False))
                    fns.append(mm(PS[b][:, g * 128:g * 128 + n], ones_bf[0:1, :], gb_hi[0:1, g * 128:g * 128 + n], False, False))
                    fns.append(mm(PS[b][:, g * 128:g * 128 + n], ones_bf[0:1, :], gb_lo[0:1, g * 128:g * 128 + n], False, g == 3))
                pe_ops(fns, reads=["gvb", "lw", "const2"], writes=[f"ps{b}"])
                R.op("dve", lambda e, b=b, c0=c0, n=n: e.tensor_tensor(
                    out=ygm[:, :, c0:c0 + n], in0=PS[b][:, :].rearrange("p (g i) -> p g i", g=4)[:, :, 0:n],
                    in1=gu[:, :, c0:c0 + n], op=ALU.mult), reads=[f"ps{b}", "gu"], writes=["ygm"])
            R.retire()

        if DBG < 5:
            mx.close()
            return
        conv_step(1)
        with ExitStack() as pd_:
            def dsb(name, shape, dt=F32):
                return pd_.enter_context(sbt(name, list(shape), dt))
            lq = dsb("d_lq", [128, 2, TT]); lk = dsb("d_lk", [128, 2, TT])
            lvb = dsb("d_lvb", [128, nsub, BW], BF16)
            laT = dsb("d_la", [16, TT], BF16)
            lr = dsb("d_lr", [128, 4, TT], BF16)
            eg = dsb("d_eg", [128, 2, 256]); spg = dsb("d_spg", [128, 2, 256])
            eb = dsb("d_eb", [128, 2, TT])
            qt = dsb("d_qt", [128, 2, TT], BF16); kt = dsb("d_kt", [128, 2, TT], BF16)
            ktok = dsb("d_ktok", [128, nsub, 256], BF16)
            attb = [dsb(f"d_att{i}", [128, 128], BF16) for i in range(2)]
            oT = dsb("d_oT", [128, 4, TT])
            enb = oT[:, 0:2, :]
            ors2 = dsb("d_ors2", [128, TT])
            t, r = load_w("w_in", l, O_LQ)
            def ev_lqk(j, b):
                dst = lq if j < 2 else lk
                R.op("act", lambda e: e.activation(out=dst[:, j % 2, 0:ntok], in_=PS[b][:, 0:ntok], func=AF.Copy),
                     reads=[f"ps{b}"], writes=["lq" if j < 2 else "lk"])
            fm_block(t, r, 512, ntok, ev_lqk)
            t, r = load_w("w_in", l, O_LV)
            tm_block(t, r, subs, lambda si, b, n: R.op("act", lambda e, si=si, b=b, n=n: e.activation(
                out=lvb[0:n, si, :], in_=PS[b][0:n, :], func=AF.Copy), reads=[f"ps{b}"], writes=["lvb"]))
            b = next_ps()
            ht_ops([mm(PS[b][0:16, 0:ntok], wla_sb[:, k, :], hT[:, k, 0:ntok], k == 0, k == KD - 1) for k in range(KD)], ["lw"], [f"ps{b}"])
            R.op("act", lambda e, b=b: e.activation(out=laT[:, 0:ntok], in_=PS[b][0:16, 0:ntok], func=AF.Copy),
                 reads=[f"ps{b}"], writes=["laT"])
            t, r = load_w("w_in", l, O_LR)
            fm_block(t, r, 512, ntok, lambda j, b: R.op("act", lambda e, j=j, b=b: e.activation(
                out=lr[:, j, 0:ntok], in_=PS[b][:, 0:ntok], func=AF.Silu), reads=[f"ps{b}"], writes=["lr"]))
            for si, (c0, n) in enumerate(subs):
                b = next_ps()
                pe_ops([mm(PS[b][0:n, 0:256], laT[:, c0:c0 + n], wa2_sb[:, :], True, False),
                        mm(PS[b][0:n, 0:256], ones_bf[0:1, 0:n], ba_sb[0:1, :], False, True)],
                       reads=["laT", "lw", "const2"], writes=[f"ps{b}"])
                R.op("act", lambda e, si=si, b=b, n=n: e.activation(out=eg[0:n, si % 2, :], in_=PS[b][0:n, 0:256], func=AF.Exp,
                                                                    scale=-1.0), reads=[f"ps{b}"], writes=[f"eg{si % 2}"])
                R.op("act", lambda e, si=si, n=n: e.activation(out=spg[0:n, si % 2, :], in_=eg[0:n, si % 2, :], func=AF.Ln, bias=1.0,
                                                               scale=1.0), reads=[f"eg{si % 2}"], writes=[f"spg{si % 2}"])
                for fc in range(2):
                    b2 = next_ps()
                    pe_ops([mm(PS[b2][:, 0:n], spg[0:n, si % 2, fc * 128:(fc + 1) * 128], trile[0:n, 0:n], True, True)],
                           reads=[f"spg{si % 2}", "const"], writes=[f"ps{b2}"])
                    R.op("act", lambda e, fc=fc, b2=b2, c0=c0, n=n: e.activation(
                        out=eb[:, fc, c0:c0 + n], in_=PS[b2][:, 0:n], func=AF.Exp, scale=-1.0 / 16),
                        reads=[f"ps{b2}"], writes=["eb"])
                    R.op("act", lambda e, fc=fc, b2=b2, c0=c0, n=n: e.activation(
                        out=enb[:, fc, c0:c0 + n], in_=PS[b2][:, 0:n], func=AF.Exp, scale=1.0 / 16),
                        reads=[f"ps{b2}"], writes=["oT0", "oT1"])
            for fc in range(2):
                R.op("dve", lambda e, fc=fc: e.scalar_tensor_tensor(out=qt[:, fc, 0:ntok], in0=lq[:, fc, 0:ntok], scalar=0.125,
                                                                    in1=eb[:, fc, 0:ntok], op0=ALU.mult, op1=ALU.mult),
                     reads=["lq", "eb"], writes=["qt"])
                R.op("dve", lambda e, fc=fc: e.tensor_tensor(out=kt[:, fc, 0:ntok], in0=lk[:, fc, 0:ntok],
                                                             in1=enb[:, fc, 0:ntok], op=ALU.mult),
                     reads=["lk", "oT0", "oT1"], writes=["kt"])
            for si, (c0, n) in enumerate(subs):
                for fc in range(2):
                    pe_ops([lambda pe, fc=fc, c0=c0, n=n: pe.transpose(PSB[0:n, fc * 128:(fc + 1) * 128],
                                                                        kt[:, fc, c0:c0 + n], ident[:, :])],
                           reads=["kt", "const"], writes=["psb"])
                R.op("act", lambda e, si=si, n=n: e.activation(out=ktok[0:n, si, :], in_=PSB[0:n, 0:256], func=AF.Copy),
                     reads=["psb"], writes=["ktok"])
            for si, (c0, n) in enumerate(subs):
                seq = si if is_sample else 0
                if is_sample or (tix == 0 and si == 0):
                    if is_sample:
                        for hh in range(4):
                            fp = (hh % 2) * 64
                            R.dma("sp", lambda e, hh=hh, fp=fp, seq=seq: e.dma_start(
                                out=S32[fp:fp + 64, hh * 128:(hh + 1) * 128], in_=sgl[l, seq, hh]), writes=[f"S32{hh}"])
                        for hh in range(4):
                            fp = (hh % 2) * 64
                            R.op("act", lambda e, hh=hh, fp=fp: e.activation(out=Sbf[fp:fp + 64, hh * 128:(hh + 1) * 128],
                                                                             in_=S32[fp:fp + 64, hh * 128:(hh + 1) * 128],
                                                                             func=AF.Copy), reads=[f"S32{hh}"], writes=[f"Sbf{hh}"])
                    else:
                        R.op("dve", lambda e: e.memset(S32[:], 0.0), writes=[f"S32{h_}" for h_ in range(4)])
                        R.op("dve", lambda e: e.memset(Sbf[:], 0.0), writes=[f"Sbf{h_}" for h_ in range(4)])
                for hh in range(4):
                    fc = hh // 2; fp = (hh % 2) * 64
                    ab = hh % 2
                    ob = 2 + hh % 2
                    pe_ops([mm(PS[ab][0:n, 0:n], kt[fp:fp + 64, fc, c0:c0 + n], qt[fp:fp + 64, fc, c0:c0 + n], True, True)],
                           reads=["kt", "qt"], writes=[f"ps{ab}"])
                    R.op("dve", lambda e, ab=ab, n=n: e.tensor_tensor(out=attb[ab][0:n, 0:n], in0=PS[ab][0:n, 0:n],
                                                                      in1=trile[0:n, 0:n], op=ALU.mult),
                         reads=[f"ps{ab}", "const"], writes=[f"att{ab}"])
                    pe_ops([mm(PS[ob][:, 0:n], Sbf[fp:fp + 64, hh * 128:(hh + 1) * 128], qt[fp:fp + 64, fc, c0:c0 + n], True, False),
                            mm(PS[ob][:, 0:n], lvb[0:n, si, hh * 128:(hh + 1) * 128], attb[ab][0:n, 0:n], False, True)],
                           reads=[f"Sbf{hh}", "qt", "lvb", f"att{ab}"], writes=[f"ps{ob}"])
                    R.op("act", lambda e, hh=hh, ob=ob, c0=c0, n=n: e.activation(out=oT[:, hh, c0:c0 + n], in_=PS[ob][:, 0:n],
                                                                                func=AF.Copy), reads=[f"ps{ob}"], writes=[f"oT{hh}"])
                    db = 4 + hh % 2
                    pe_ops([mm(PS[db][fp:fp + 64, hh * 128:(hh + 1) * 128], ktok[0:n, si, fc * 128 + fp:fc * 128 + fp + 64],
                               lvb[0:n, si, hh * 128:(hh + 1) * 128], True, True)], reads=["ktok", "lvb"], writes=[f"ps{db}"])
                    R.op("dve", lambda e, hh=hh, fp=fp, db=db: e.tensor_tensor(
                        out=S32[fp:fp + 64, hh * 128:(hh + 1) * 128], in0=PS[db][fp:fp + 64, hh * 128:(hh + 1) * 128],
                        in1=S32[fp:fp + 64, hh * 128:(hh + 1) * 128], op=ALU.add), reads=[f"ps{db}", f"S32{hh}"], writes=[f"S32{hh}"])
                    R.op("dve", lambda e, hh=hh, fp=fp, fc=fc, c0=c0, n=n: e.tensor_scalar(
                        out=S32[fp:fp + 64, hh * 128:(hh + 1) * 128], in0=S32[fp:fp + 64, hh * 128:(hh + 1) * 128],
                        scalar1=eb[fp:fp + 64, fc, c0 + n - 1:c0 + n], scalar2=None, op0=ALU.mult),
                        reads=[f"S32{hh}", "eb"], writes=[f"S32{hh}"])
                    R.op("act", lambda e, hh=hh, fp=fp: e.activation(out=Sbf[fp:fp + 64, hh * 128:(hh + 1) * 128],
                                                                     in_=S32[fp:fp + 64, hh * 128:(hh + 1) * 128], func=AF.Copy),
                         reads=[f"S32{hh}"], writes=[f"Sbf{hh}"])
                if is_sample or (tix == NT - 1 and si == nsub - 1):
                    for hh in range(4):
                        fp = (hh % 2) * 64
                        dst = glas[l, seq, hh] if is_sample else glap[l, hh]
                        R.dma("sp", lambda e, hh=hh, fp=fp, dst=dst: e.dma_start(
                            out=dst, in_=S32[fp:fp + 64, hh * 128:(hh + 1) * 128]), reads=[f"S32{hh}"])
            orsb = [RSTD, ors2]; orsn = ["rstd", "ors2"]
            for hh in range(4):
                R.op("act", lambda e, hh=hh: e.activation(out=SQ[:, hh % 2, 0:ntok], in_=oT[:, hh, 0:ntok], func=AF.Square),
                     reads=[f"oT{hh}"], writes=[f"sq{hh % 2}"])
                b = next_ps()
                pe_ops([mm(PS[b][:, 0:ntok], ones_bf[:], SQ[:, hh % 2, 0:ntok], True, True)], reads=[f"sq{hh % 2}", "const2"],
                       writes=[f"ps{b}"])
                R.op("dve", lambda e, b=b, hh=hh: e.tensor_scalar(out=orsb[hh % 2][:, 0:ntok], in0=PS[b][:, 0:ntok],
                                                                  scalar1=1.0 / 128, scalar2=EPS, op0=ALU.mult, op1=ALU.add),
                     reads=[f"ps{b}"], writes=[orsn[hh % 2]])
                rsqrt_ip(orsb[hh % 2][:, 0:ntok], orsn[hh % 2])
                R.op("dve", lambda e, hh=hh: e.scalar_tensor_tensor(out=oT[:, hh, 0:ntok], in0=oT[:, hh, 0:ntok],
                                                                    scalar=cols[:, C_GON:C_GON + 1], in1=orsb[hh % 2][:, 0:ntok],
                                                                    op0=ALU.mult, op1=ALU.mult),
                     reads=[f"oT{hh}", orsn[hh % 2], "lw"], writes=[f"oT{hh}"])
                R.op("pool", lambda e, hh=hh: e.tensor_tensor(out=ygl[:, hh, 0:ntok], in0=oT[:, hh, 0:ntok],
                                                              in1=lr[:, hh, 0:ntok], op=ALU.mult),
                     reads=[f"oT{hh}", "lr"], writes=["ygl"])
            R.retire()

        if DBG < 6:
            mx.close()
            return
        conv_step(1)
        with ExitStack() as pe_:
            acc = pe_.enter_context(sbt("e_acc", [128, 4, TT], F32))
            sg = [pe_.enter_context(sbt(f"e_sg{i}", [128, TT], F32)) for i in range(2)]
            tm = [pe_.enter_context(sbt(f"e_tm{i}", [128, TT], F32)) for i in range(2)]
            mg = pe_.enter_context(sbt("e_mg", [128, KD, TT], BF16))
            ybs = [ypool, ysb, ygm, ygl]
            ynm = ["ypool", "ysb", "ygm", "ygl"]
            ci = 0
            for nb in range(2):
                for bnum in range(4):
                    tb, rb = load_w("w_br", l, (bnum, nb))
                    tg, rg = load_w("w_in", l, O_G + bnum * D + nb * 512)
                    for j in range(4):
                        ip = next_ps(); ig = next_ps()
                        pe_ops([mm(PS[ip][:, 0:ntok], tb[:, k, j * 128:(j + 1) * 128], ybs[bnum][:, k, 0:ntok], k == 0, k == 3)
                                for k in range(4)], reads=[ynm[bnum], rb], writes=[f"ps{ip}"])
                        ht_ops([mm(PS[ig][:, 0:ntok], tg[:, k, j * 128:(j + 1) * 128], hT[:, k, 0:ntok], k == 0, k == KD - 1)
                                for k in range(KD)], [rg], [f"ps{ig}"])
                        s_ = sg[ci % 2]; t_ = tm[ci % 2]; sn = f"sg{ci % 2}"; tn = f"tm{ci % 2}"; ci += 1
                        R.op("act", lambda e, s_=s_, ig=ig: e.activation(out=s_[:, 0:ntok], in_=PS[ig][:, 0:ntok], func=AF.Sigmoid),
                             reads=[f"ps{ig}"], writes=[sn])
                        if bnum == 0:
                            R.op("dve", lambda e, s_=s_, ip=ip, j=j: e.tensor_tensor(out=acc[:, j, 0:ntok], in0=PS[ip][:, 0:ntok],
                                                                                     in1=s_[:, 0:ntok], op=ALU.mult),
                                 reads=[f"ps{ip}", sn], writes=[f"acc{j}"])
                        else:
                            R.op("dve", lambda e, s_=s_, t_=t_, ip=ip: e.tensor_tensor(out=t_[:, 0:ntok], in0=PS[ip][:, 0:ntok],
                                                                                       in1=s_[:, 0:ntok], op=ALU.mult),
                                 reads=[f"ps{ip}", sn], writes=[tn])
                            if bnum < 3:
                                R.op("dve", lambda e, t_=t_, j=j: e.tensor_tensor(out=acc[:, j, 0:ntok], in0=acc[:, j, 0:ntok],
                                                                                  in1=t_[:, 0:ntok], op=ALU.add),
                                     reads=[tn, f"acc{j}"], writes=[f"acc{j}"])
                            else:
                                R.op("dve", lambda e, t_=t_, j=j, nb=nb: e.tensor_tensor(
                                    out=mg[:, nb * 4 + j, 0:ntok], in0=acc[:, j, 0:ntok], in1=t_[:, 0:ntok], op=ALU.add),
                                    reads=[tn, f"acc{j}"], writes=[f"mg{nb * 4 + j}"])
            for nb in range(2):
                t, r = load_w("w_o", l, nb)
                for j in range(4):
                    n = nb * 4 + j
                    b = next_ps()
                    pe_ops([mm(PS[b][:, 0:ntok], t[:, k, j * 128:(j + 1) * 128], mg[:, k, 0:ntok], k == 0, k == KD - 1)
                            for k in range(KD)], reads=[f"mg{k}" for k in range(KD)] + [r], writes=[f"ps{b}"])
                    R.op("dve", lambda e, n=n, b=b: e.tensor_tensor(out=xT[:, n, 0:ntok], in0=PS[b][:, 0:ntok],
                                                                    in1=xT[:, n, 0:ntok], op=ALU.add),
                         reads=[f"ps{b}", "xT"], writes=["xT"])
            R.retire()
        mx.close()

    def ple(l, tile):
        ntok, subs, segs, c0g, is_sample, tix = tile
        cg = (S if is_sample else c0g)
        conv_step(1)
        with ExitStack() as pp_:
            ph = None
            sg = [pp_.enter_context(sbt(f"p_sg{i}", [128, TT], F32)) for i in range(2)]
            tm = [pp_.enter_context(sbt(f"p_tm{i}", [128, TT], F32)) for i in range(2)]
            R.dma("pool", lambda e: e.dma_start(out=pT[:, :, 0:ntok],
                                                in_=pin[l].rearrange("(k p) t -> p k t", p=128)[:, :, cg:cg + ntok]),
                  writes=["pT"])
            rmsnorm(ntok, C_NPL, ph)
            ci = 0
            for nb in range(2):
                t, r = load_w("w_pg", l, nb)
                for j in range(4):
                    n = nb * 4 + j
                    ig = next_ps(); ip = next_ps()
                    ht_ops([mm(PS[ig][:, 0:ntok], t[:, k, j * 128:(j + 1) * 128], hT[:, k, 0:ntok], k == 0, k == KD - 1)
                            for k in range(KD)], [r], [f"ps{ig}"])
                    pe_ops([mm(PS[ip][:, 0:ntok], wpp_sb[:, k, n * 128:(n + 1) * 128], pT[:, k, 0:ntok], k == 0, k == 1)
                            for k in range(2)], reads=["pT", "lw"], writes=[f"ps{ip}"])
                    s_ = sg[ci % 2]; t_ = tm[ci % 2]; sn = f"sg{ci % 2}"; tn = f"tm{ci % 2}"; ci += 1
                    R.op("act", lambda e, s_=s_, ig=ig: e.activation(out=s_[:, 0:ntok], in_=PS[ig][:, 0:ntok], func=AF.Sigmoid),
                         reads=[f"ps{ig}"], writes=[sn])
                    R.op("dve", lambda e, s_=s_, t_=t_, ip=ip: e.tensor_tensor(out=t_[:, 0:ntok], in0=PS[ip][:, 0:ntok],
                                                                               in1=s_[:, 0:ntok], op=ALU.mult),
                         reads=[f"ps{ip}", sn], writes=[tn])
                    R.op("dve", lambda e, t_=t_, n=n: e.tensor_tensor(out=xT[:, n, 0:ntok], in0=xT[:, n, 0:ntok],
                                                                      in1=t_[:, 0:ntok], op=ALU.add),
                         reads=[tn, "xT"], writes=["xT"])
            R.retire()

    tiles = [(2 * DEC, [(0, DEC), (DEC, DEC)], [(0, DEC), (DEC, DEC)], S, True, 0)]
    for t_ in range(NT):
        tiles.append((TT, [(i * 128, 128) for i in range(4)], [(0, TT)], t_ * TT, False, t_))

    convert_layer(0)
    for l in range(DEPTH):
        R.dma("sp", lambda e, l=l: e.dma_start(out=cols[:], in_=cols_d[l]), writes=["lw"])
        R.dma("pool", lambda e, l=l: e.dma_start(out=poolw_sb[:], in_=pool_w[l].rearrange("g c d -> c g d")), writes=["lw"])
        R.dma("pool", lambda e, l=l: e.dma_start(out=wmT[:], in_=wsT_d[l].rearrange("g j i -> j g i")), writes=["lw"])
        R.dma("sp", lambda e, l=l: e.dma_start(out=gb_f[:], in_=gb_d[l]), writes=["lw"])
        R.dma("pool", lambda e, l=l: e.dma_start(out=wa2_sb[:], in_=wa2_d[l]), writes=["lw"])
        R.dma("pool", lambda e, l=l: e.dma_start(out=ba_sb[:], in_=ba_d[l]), writes=["lw"])
        R.dma("pool", lambda e, l=l: e.dma_start(out=wla_sb[:], in_=w_in[l][:, O_LA:O_LA + 16].rearrange("(k p) n -> p k n", p=128)),
              writes=["lw"])
        R.dma("pool", lambda e, l=l: e.dma_start(out=wpp_sb[:], in_=w_pp[l].rearrange("(k p) n -> p k n", p=128)), writes=["lw"])
        for g in range(4):
            R.op("dve", lambda e, g=g: e.tensor_tensor(out=wmT[:, g, :], in0=wmT[:, g, :], in1=bmask[:], op=ALU.mult),
                 reads=["lw", "const"], writes=["lw"])
        R.op("act", lambda e: e.activation(out=gb_hi[:], in_=gb_f[:], func=AF.Copy), reads=["lw"], writes=["lw2"])
        R.op("dve", lambda e: e.tensor_tensor(out=gb_lo[:], in0=gb_f[:], in1=gb_hi[:], op=ALU.subtract), reads=["lw", "lw2"],
             writes=["lw3"])
        R.op("dve", lambda e: e.memset(phist[:], 0.0), writes=["phist"])
        R.barrier(("pe", "act", "dve", "sp", "pool"))
        conv_step(len(conv_q))
        if l + 1 < DEPTH:
            convert_layer(l + 1, defer=True)
        for tile in tiles:
            ntok, subs, segs, c0g, is_sample, tix = tile
            cg = S if is_sample else c0g
            src = xin if l == 0 else xscr
            dst = yout if l == DEPTH - 1 else xscr
            rname = f"xd{cg}"
            R.dma("sp", lambda e, src=src, cg=cg, ntok=ntok: e.dma_start(
                out=xT[:, :, 0:ntok], in_=src.rearrange("(k p) t -> p k t", p=128)[:, :, cg:cg + ntok]),
                reads=[rname], writes=["xT"])
            if DBG >= 1:
                ffn(l, ntok, "f1a", "f1b", "f1c", C_N1)
            if DBG >= 2:
                mixer(l, tile)
            if DBG >= 8:
                ffn(l, ntok, "f2a", "f2b", "f2c", C_N2)
            if DBG >= 9:
                ple(l, tile)
            R.dma("sp", lambda e, dst=dst, cg=cg, ntok=ntok: e.dma_start(
                out=dst.rearrange("(k p) t -> p k t", p=128)[:, :, cg:cg + ntok], in_=xT[:, :, 0:ntok]),
                reads=["xT"], writes=[rname])
    R.barrier(("pe", "act", "dve", "sp", "pool"))

    sems = {}
    for e in R.engs:
        sems[e] = es.enter_context(nc.semaphore(f"c_{e}"))
    for q in ("sp", "pool"):
        for i in range(R.dma_nsem[q]):
            sems[("dma", q, i)] = es.enter_context(nc.semaphore(f"d_{q}{i}"))
    block = es.enter_context(nc.Block())

    def replay(eng, name):
        for waits, fn, inc in R.ops[name]:
            for k, v in waits:
                eng.wait_ge(sems[k], v)
            if fn is None:
                continue
            ins = fn(eng)
            if inc[0] == "cnt":
                ins.then_inc(sems[inc[1]], 1)
            else:
                ins.then_inc(sems[inc[1]], 16)

    block.tensor(lambda e: replay(e, "pe"))
    block.scalar(lambda e: replay(e, "act"))
    block.vector(lambda e: replay(e, "dve"))
    block.gpsimd(lambda e: replay(e, "pool"))
    block.sync(lambda e: replay(e, "sp"))
    es.close()
    return nc


def make_consts():
    i = np.arange(128)
    c = {}
    c["c_ident"] = np.eye(128, dtype=np.float32)
    c["c_tinc"] = (i[:, None] >= i[None, :]).astype(np.float32)
    c["c_trile"] = (i[:, None] <= i[None, :]).astype(np.float32)
    q = np.arange(512)
    negp = np.zeros((128, 4, 512), np.float32)
    for d in range(4):
        negp[:, d, :] = np.where((i[:, None] + 128 * d) < q[None, :], 0.0, NEGV)
    c["c_negp"] = negp.reshape(128, 2048)
    j = np.arange(32)
    ns = np.where(j[:, None] < j[None, :], 0.0, NEGV).astype(np.float32)
    c["c_negs"] = np.tile(ns, (1, 8))
    c["c_bmask"] = ((i[:, None] // 64) <= (i[None, :] // 64)).astype(np.float32)
    invc = np.zeros((128, 4, 16), np.float32)
    for g in range(4):
        invc[:, g, :] = (1.0 / np.minimum(2 << g, q[:16] + 1)).astype(np.float32)[None, :]
    c["c_invc"] = invc.reshape(128, 64)
    return c


def layout_inputs(inp, S, PAST, DEPTH, n_cores):
    f = lambda a: np.ascontiguousarray(a, dtype=np.float32)
    shared = {}
    for src, dst in [("ffn1_w1", "f1a"), ("ffn1_w3", "f1b"), ("ffn1_w2", "f1c"), ("ffn2_w1", "f2a"), ("ffn2_w3", "f2b"),
                     ("ffn2_w2", "f2c"), ("w_in", "w_in"), ("pool_w", "pool_w"), ("gla_wa2", "wa2"), ("w_branch", "w_br"),
                     ("w_out", "w_o"), ("ple_w_gate", "w_pg"), ("ple_w_proj", "w_pp")]:
        shared[dst] = f(inp[src])
    shared["wsT"] = f(np.transpose(inp["gmlp_ws"], (0, 1, 3, 2)))
    shared["gb"] = f(np.reshape(inp["gmlp_b"], (DEPTH, 1, 512)))
    shared["ba"] = f(np.reshape(inp["gla_ba"], (DEPTH, 1, 256)))
    cols = np.zeros((DEPTH, 128, NCOL), np.float32)
    for nm, c0 in [("norm_ffn1", C_N1), ("norm_mix", C_NM), ("norm_ffn2", C_N2), ("norm_ple", C_NPL)]:
        cols[:, :, c0:c0 + 8] = np.transpose(np.reshape(inp[nm], (DEPTH, 8, 128)), (0, 2, 1))
    cols[:, :, C_PS:C_PS + 4] = np.transpose(np.reshape(inp["pool_scale"], (DEPTH, 4, 128)), (0, 2, 1))
    cols[:, :, C_QN] = np.tile(inp["sb_q_norm"], (1, 2))
    cols[:, :, C_KN] = np.tile(inp["sb_k_norm"], (1, 2))
    cols[:, :, C_GON] = inp["gla_out_norm"]
    shared["cols"] = cols
    shared.update(make_consts())
    maps = []
    for c in range(n_cores):
        m = dict(shared)
        xs = np.reshape(inp["x_sample"][2 * c:2 * c + 2], (2 * DEC, D))
        m["xin"] = f(np.concatenate([inp["x_prompt"][c].T, xs.T], axis=1))
        ps = np.reshape(inp["p_sample"][:, 2 * c:2 * c + 2], (DEPTH, 2 * DEC, PLE))
        m["pin"] = f(np.concatenate([np.transpose(inp["p_prompt"][:, c], (0, 2, 1)), np.transpose(ps, (0, 2, 1))], axis=2))
        m["ckT"] = f(np.transpose(np.reshape(inp["cache_sb_k"][:, 2 * c:2 * c + 2], (DEPTH, 2, PAST, BW)), (0, 1, 3, 2)))
        m["cv"] = f(np.reshape(inp["cache_sb_v"][:, 2 * c:2 * c + 2], (DEPTH, 2, PAST, BW)))
        m["spT"] = f(np.transpose(inp["state_pool"][:, 2 * c:2 * c + 2], (0, 1, 3, 2)))
        m["sgl"] = f(inp["state_gla"][:, 2 * c:2 * c + 2])
        maps.append(m)
    return maps


def assemble(results, S, DEPTH, n_cores):
    B = n_cores
    yp = np.zeros((B, S, D), np.float32); ys = np.zeros((2 * B, DEC, D), np.float32)
    kp = np.zeros((DEPTH, B, S, 8, 64), np.float32); vpo = np.zeros((DEPTH, B, S, 8, 64), np.float32)
    pp = np.zeros((DEPTH, B, 15, BW), np.float32); gp = np.zeros((DEPTH, B, 4, 64, 128), np.float32)
    ks = np.zeros((DEPTH, 2 * B, DEC, 8, 64), np.float32); vso = np.zeros((DEPTH, 2 * B, DEC, 8, 64), np.float32)
    pso = np.zeros((DEPTH, 2 * B, 15, BW), np.float32); gs = np.zeros((DEPTH, 2 * B, 4, 64, 128), np.float32)
    gm = np.zeros((DEPTH, 2 * B, DEC, BW), np.float32)
    for c, r in enumerate(results):
        yo = r["yout"]
        yp[c] = yo[:, :S].T
        ys[2 * c:2 * c + 2] = yo[:, S:].T.reshape(2, DEC, D)
        kp[:, c] = np.transpose(r["kpT"], (0, 2, 1)).reshape(DEPTH, S, 8, 64)
        vpo[:, c] = r["vp"].reshape(DEPTH, S, 8, 64)
        pp[:, c] = np.transpose(r["poolpT"], (0, 2, 1))
        gp[:, c] = r["glap"]
        ks[:, 2 * c:2 * c + 2] = np.transpose(r["ksT"], (0, 2, 1)).reshape(DEPTH, 2, DEC, 8, 64)
        vso[:, 2 * c:2 * c + 2] = r["vs"].reshape(DEPTH, 2, DEC, 8, 64)
        pso[:, 2 * c:2 * c + 2] = np.transpose(r["poolsT"], (0, 1, 3, 2))
        gs[:, 2 * c:2 * c + 2] = r["glas"]
        gm[:, 2 * c:2 * c + 2] = r["gms"].reshape(DEPTH, 2, DEC, BW)
    return (yp, ys, kp, vpo, pp, gp, ks, vso, pso, gs, gm)


def run(inp, S, PAST, DEPTH, n_cores=8):
    nc = build_program(S, PAST, DEPTH)
    maps = layout_inputs(inp, S, PAST, DEPTH, n_cores)
    res = run_bass_kernel_spmd(nc, maps, core_ids=list(range(n_cores)))
    return assemble(res.results, S, DEPTH, n_cores)


def kernel(**inputs):
    inp = {k: np.asarray(v) for k, v in inputs.items()}
    S = inp["x_prompt"].shape[1]
    PAST = inp["cache_sb_k"].shape[2]
    DEPTH = inp["w_in"].shape[0]
    return run(inp, S, PAST, DEPTH, 8)
```

```python
import numpy as np
from contextlib import ExitStack
import concourse.bass as bass
import concourse.mybir as mybir
from concourse.bass_utils import run_bass_kernel_spmd

F32 = mybir.dt.float32
BF16 = mybir.dt.bfloat16
AF = mybir.ActivationFunctionType
ALU = mybir.AluOpType

D = 1024
KD = 8
DFF = 2816
PLE = 256
BW = 512
NCOLS_IN = 8720
EPS = 1e-6
TT = 512
DEC = 32
NEGV = -30000.0
O_XP, O_Q, O_K, O_V, O_GU, O_GV, O_LQ, O_LK, O_LV, O_LA, O_LR, O_G = (
    0, 512, 1024, 1536, 2048, 2560, 3072, 3328, 3584, 4096, 4112, 4624)
NSLOT = 4
import os
DBG = int(os.environ.get("KDBG", "99"))
NCOL = 39
C_N1, C_NM, C_N2, C_NPL, C_PS, C_QN, C_KN, C_GON = 0, 8, 16, 24, 32, 36, 37, 38


class Rec:
    def __init__(self):
        self.engs = ["pe", "act", "dve", "pool", "sp"]
        self.ops = {e: [] for e in self.engs}
        self.count = {e: 0 for e in self.engs}
        self.waited = {e: {} for e in self.engs}
        self.res = {}
        self.dma_n = {"sp": 0, "pool": 0}
        self.dma_nsem = {"sp": 24, "pool": 68}
        self.dma_events = []
        self.pending = {}

    PERSIST = ("xT", "hT", "KT", "VC", "ws", "lw", "const", "phist", "S32", "Sbf", "ypool", "ysb", "ygm", "ygl",
               "sq0", "sq1", "rstd", "pT", "ps", "xd")

    def retire(self):
        for name in list(self.res.keys()):
            if name.startswith(self.PERSIST):
                continue
            w, rs = self.res.pop(name)
            for ev in ([w] if w else []) + rs:
                k, v = ev
                if self.pending.get(k, 0) < v:
                    self.pending[k] = v

    def _r(self, name):
        if name not in self.res:
            self.res[name] = [None, []]
        return self.res[name]

    def _deps(self, eng, reads, writes):
        deps = {}
        def add(ev):
            if ev is None:
                return
            k, v = ev
            if deps.get(k, 0) < v:
                deps[k] = v
        for nm in list(reads) + list(writes):
            if nm not in self.res and not nm.startswith(self.PERSIST):
                for k, v in self.pending.items():
                    add((k, v))
                break
        for r in reads:
            add(self._r(r)[0])
        for w in writes:
            e = self._r(w)
            add(e[0])
            for ev in e[1]:
                add(ev)
        waits = []
        for k, v in deps.items():
            if k == "pe" and eng == "pe":
                continue
            if self.waited[eng].get(k, 0) >= v:
                continue
            self.waited[eng][k] = v
            waits.append((k, v))
        return waits

    def _commit(self, ev, reads, writes):
        for r in reads:
            self._r(r)[1].append(ev)
        for w in writes:
            e = self._r(w)
            e[0] = ev
            e[1] = []

    def op(self, eng, fn, reads=(), writes=()):
        waits = self._deps(eng, reads, writes)
        self.count[eng] += 1
        ev = (eng, self.count[eng])
        self.ops[eng].append((waits, fn, ("cnt", eng)))
        self._commit(ev, reads, writes)

    def dma(self, q, fn, reads=(), writes=()):
        waits = self._deps(q, reads, writes)
        i = self.dma_n[q]
        self.dma_n[q] += 1
        n = self.dma_nsem[q]
        key = ("dma", q, i % n)
        if i >= n:
            prev = 16 * (i // n)
            if self.waited[q].get(key, 0) < prev:
                self.waited[q][key] = prev
                waits.append((key, prev))
        ev = (key, 16 * (i // n + 1))
        self.ops[q].append((waits, fn, ("dma", key)))
        self._commit(ev, reads, writes)
        self.dma_events.append(ev)
        return ev

    def fence(self, eng, events):
        waits = []
        for k, v in events:
            if self.waited[eng].get(k, 0) >= v:
                continue
            self.waited[eng][k] = v
            waits.append((k, v))
        if waits:
            self.ops[eng].append((waits, None, None))

    def barrier(self, engs=("pe", "act", "dve", "sp")):
        evs = [(e, self.count[e]) for e in ("pe", "act", "dve", "pool") if self.count[e] > 0]
        evs += self.dma_events
        self.dma_events = []
        for e in engs:
            waits = []
            for k, v in evs:
                if k == e:
                    continue
                if self.waited[e].get(k, 0) >= v:
                    continue
                self.waited[e][k] = v
                waits.append((k, v))
            if waits:
                self.ops[e].append((waits, None, None))

    def drop(self, names):
        for n in names:
            self.res.pop(n, None)


def build_program(S, PAST, DEPTH):
    NT = S // TT
    NTOK = S + 2 * DEC
    NPC = PAST // 128
    KTW = max(S, PAST + DEC)
    NVC = max(S // 128, NPC + 1)
    nc = bass.Bass("TRN2", target_bir_lowering=False)

    def din(name, shape):
        return nc.dram_tensor(name, list(shape), F32, kind="ExternalInput").ap()

    def dout(name, shape):
        return nc.dram_tensor(name, list(shape), F32, kind="ExternalOutput").ap()

    xin = din("xin", [D, NTOK])
    pin = din("pin", [DEPTH, PLE, NTOK])
    ckT = din("ckT", [DEPTH, 2, BW, PAST])
    cv = din("cv", [DEPTH, 2, PAST, BW])
    spT = din("spT", [DEPTH, 2, BW, 15])
    sgl = din("sgl", [DEPTH, 2, 4, 64, 128])
    cols_d = din("cols", [DEPTH, 128, NCOL])
    w_f1a = din("f1a", [DEPTH, D, DFF]); w_f1b = din("f1b", [DEPTH, D, DFF]); w_f1c = din("f1c", [DEPTH, DFF, D])
    w_f2a = din("f2a", [DEPTH, D, DFF]); w_f2b = din("f2b", [DEPTH, D, DFF]); w_f2c = din("f2c", [DEPTH, DFF, D])
    w_in = din("w_in", [DEPTH, D, NCOLS_IN])
    pool_w = din("pool_w", [DEPTH, 4, 128, 128])
    wsT_d = din("wsT", [DEPTH, 4, 128, 128])
    gb_d = din("gb", [DEPTH, 1, 512])
    wa2_d = din("wa2", [DEPTH, 16, 256])
    ba_d = din("ba", [DEPTH, 1, 256])
    w_br = din("w_br", [DEPTH, 4, BW, D])
    w_o = din("w_o", [DEPTH, D, D])
    w_pg = din("w_pg", [DEPTH, D, D])
    w_pp = din("w_pp", [DEPTH, PLE, D])
    c_ident = din("c_ident", [128, 128])
    c_tinc = din("c_tinc", [128, 128])
    c_trile = din("c_trile", [128, 128])
    c_negp = din("c_negp", [128, 4 * 512])
    c_negs = din("c_negs", [32, 256])
    c_bmask = din("c_bmask", [128, 128])
    c_invc = din("c_invc", [128, 4 * 16])

    yout = dout("yout", [D, NTOK])
    kpT = dout("kpT", [DEPTH, BW, S]); vp = dout("vp", [DEPTH, S, BW])
    poolpT = dout("poolpT", [DEPTH, BW, 15]); glap = dout("glap", [DEPTH, 4, 64, 128])
    ksT = dout("ksT", [DEPTH, BW, 2 * DEC]); vs = dout("vs", [DEPTH, 2 * DEC, BW])
    poolsT = dout("poolsT", [DEPTH, 2, BW, 15]); glas = dout("glas", [DEPTH, 2, 4, 64, 128])
    gms = dout("gms", [DEPTH, 2 * DEC, BW])
    xscr = nc.dram_tensor("xscr", [D, NTOK], F32, kind="Internal").ap()

    R = Rec()
    es = ExitStack()
    uniq = [0]

    def sbt(name, shape, dt=F32):
        uniq[0] += 1
        return nc.sbuf_tensor(f"{name}_{uniq[0]}", list(shape), dt)

    def sb(name, shape, dt=F32):
        return es.enter_context(sbt(name, list(shape), dt))

    xT = sb("xT", [128, KD, TT])
    hT = sb("hT", [128, KD, TT], BF16)
    KT = sb("KT", [128, 4, KTW], BF16)
    VC = sb("VC", [128, NVC, BW], BF16)
    WS = [sb(f"ws{i}", [128, 8, 512], BF16) for i in range(NSLOT)]
    cols = sb("cols_sb", [128, NCOL])
    poolw_sb = sb("poolw_sb", [128, 4, 128], BF16)
    wmT = sb("wmT", [128, 4, 128], BF16)
    gb_hi = sb("gb_hi", [1, 512], BF16); gb_lo = sb("gb_lo", [1, 512], BF16); gb_f = sb("gb_f", [1, 512])
    wa2_sb = sb("wa2_sb", [16, 256], BF16)
    ba_sb = sb("ba_sb", [1, 256], BF16)
    wla_sb = sb("wla_sb", [128, 8, 16], BF16)
    wpp_sb = sb("wpp_sb", [128, 2, D], BF16)
    ident = sb("ident", [128, 128], BF16)
    tinc = sb("tinc", [128, 128], BF16)
    ones_bf = sb("ones_bf", [128, 128], BF16)
    bdiag = sb("bdiag", [128, 128], BF16)
    trile = sb("trile", [128, 128])
    negp = sb("negp", [128, 4 * 512], BF16)
    negs = sb("negs", [32, 256], BF16)
    bmask = sb("bmask", [128, 128], BF16)
    invc = sb("invc", [128, 4, 16])
    phist = sb("phist", [128, 4, 15])
    S32 = sb("S32", [128, 512]); Sbf = sb("Sbf", [128, 512], BF16)
    SQ = sb("SQ", [128, 2, TT], BF16); RSTD = sb("RSTD", [128, TT])
    pT = sb("pT", [128, 2, TT], BF16)
    nident = sb("nident", [128, 128], BF16)
    ypool = sb("ypool", [128, 4, TT], BF16); ysb = sb("ysb", [128, 4, TT], BF16)
    ygm = sb("ygm", [128, 4, TT], BF16); ygl = sb("ygl", [128, 4, TT], BF16)
    PS = [es.enter_context(nc.psum_tensor(f"ps{i}", [128, 512], F32)) for i in range(7)]
    PSB = es.enter_context(nc.psum_tensor("psb", [128, 1024], BF16))

    slot_ctr = [0]

    WIN_OFFS = [O_XP, O_Q, O_K, O_V, O_GU, O_GV, O_LQ, O_LV, O_LR] + [O_G + i * 512 for i in range(8)]
    FBLK = [(cb * 512, min(512, DFF - cb * 512)) for cb in range((DFF + 511) // 512)]
    KBS = [(0, 8), (8, 8), (16, 6)]
    wsrc = {"f1a": w_f1a, "f1b": w_f1b, "f1c": w_f1c, "f2a": w_f2a, "f2b": w_f2b, "f2c": w_f2c, "w_in": w_in,
            "w_br": w_br, "w_o": w_o, "w_pg": w_pg}
    wblocks = {}
    for nm in ("f1a", "f1b", "f2a", "f2b"):
        wblocks[nm] = [(cb, (lambda l, nm=nm, c0=c0, nco=nco: wsrc[nm][l][:, c0:c0 + nco]), 8, nco)
                       for cb, (c0, nco) in enumerate(FBLK)]
    for nm in ("f1c", "f2c"):
        wblocks[nm] = [((nb, bi), (lambda l, nm=nm, nb=nb, k0=k0, kn=kn: wsrc[nm][l][k0 * 128:(k0 + kn) * 128,
                                                                                      nb * 512:(nb + 1) * 512]), kn, 512)
                       for nb in range(2) for bi, (k0, kn) in enumerate(KBS)]
    wblocks["w_in"] = [(off, (lambda l, off=off: w_in[l][:, off:off + 512]), 8, 512) for off in WIN_OFFS]
    wblocks["w_br"] = [((b_, nb), (lambda l, b_=b_, nb=nb: w_br[l, b_][:, nb * 512:(nb + 1) * 512]), 4, 512)
                       for b_ in range(4) for nb in range(2)]
    for nm in ("w_o", "w_pg"):
        wblocks[nm] = [(nb, (lambda l, nm=nm, nb=nb: wsrc[nm][l][:, nb * 512:(nb + 1) * 512]), 8, 512) for nb in range(2)]
    wscr = {}
    widx = {}
    for nm, bl in wblocks.items():
        wscr[nm] = nc.dram_tensor(f"{nm}_bf", [DEPTH, len(bl), 128, 8, 512], BF16, kind="Internal").ap()
        widx[nm] = {b[0]: (i, b[2], b[3]) for i, b in enumerate(bl)}
    conv_ev = {}

    conv_q = []

    def convert_layer(l, defer=False):
        for nm in ("f1a", "f1b", "f1c", "w_in", "w_br", "w_o", "f2a", "f2b", "f2c", "w_pg"):
            conv_ev[(nm, l)] = []
            for i, (idx, srcf, kcs, nco) in enumerate(wblocks[nm]):
                src = srcf(l).rearrange("(kc p) n -> p kc n", p=128)
                dst = wscr[nm][l, i][:, 0:kcs, 0:nco]
                conv_q.append((nm, l, dst, src))
        if not defer:
            conv_step(len(conv_q))

    def conv_step(n=1):
        for _ in range(min(n, len(conv_q))):
            nm, l, dst, src = conv_q.pop(0)
            conv_ev[(nm, l)].append(R.dma("pool", lambda e, dst=dst, src=src: e.dma_start(out=dst, in_=src)))

    def load_w(nm, l, idx):
        i, kcs, nco = widx[nm][idx]
        s = slot_ctr[0] % NSLOT
        slot_ctr[0] += 1
        t = WS[s]
        R.fence("sp", conv_ev[(nm, l)])
        src = wscr[nm][l, i][:, 0:kcs, 0:nco]
        R.dma("sp", lambda e, t=t, src=src, kcs=kcs, nco=nco: e.dma_start(out=t[:, 0:kcs, 0:nco], in_=src),
              writes=[f"ws{s}"])
        return t, f"ws{s}"

    def mm(out, lhsT, rhs, start, stop):
        return lambda pe: pe.matmul(out, lhsT, rhs, start=start, stop=stop, skip_group_check=True)

    def pe_ops(fns, reads, writes):
        def run(pe, fns=fns):
            last = None
            for f in fns:
                last = f(pe)
            return last
        R.op("pe", run, reads=reads, writes=writes)

    R.dma("pool", lambda e: e.dma_start(out=ident[:], in_=c_ident), writes=["const"])
    R.dma("pool", lambda e: e.dma_start(out=tinc[:], in_=c_tinc), writes=["const"])
    R.dma("pool", lambda e: e.dma_start(out=negp[:], in_=c_negp), writes=["const"])
    R.dma("pool", lambda e: e.dma_start(out=negs[:], in_=c_negs), writes=["const"])
    R.dma("pool", lambda e: e.dma_start(out=bmask[:], in_=c_bmask), writes=["const"])
    R.dma("sp", lambda e: e.dma_start(out=trile[:], in_=c_trile), writes=["const"])
    R.dma("sp", lambda e: e.dma_start(out=invc[:], in_=c_invc.rearrange("p (g t) -> p g t", g=4)), writes=["const"])
    R.op("dve", lambda e: e.memset(ones_bf[:], 1.0), writes=["const2"])
    R.op("dve", lambda e: e.memset(bdiag[:], 0.0), writes=["const2"])
    R.op("dve", lambda e: e.memset(bdiag[0:64, 0:64], 1.0), writes=["const2"])
    R.op("dve", lambda e: e.memset(bdiag[64:128, 64:128], 1.0), writes=["const2"])
    R.op("dve", lambda e: e.tensor_scalar(out=nident[:], in0=ident[:], scalar1=-1.0, scalar2=None, op0=ALU.mult),
         reads=["const"], writes=["const2"])
    R.barrier(("pe", "act", "dve", "sp", "pool"))

    ht_fresh = [False]

    def rsqrt_ip(ap, rn="rstd"):
        R.op("act", lambda e: e.activation(out=ap, in_=ap, func=AF.Ln), reads=[rn], writes=[rn])
        R.op("act", lambda e: e.activation(out=ap, in_=ap, func=AF.Exp, scale=-0.5), reads=[rn], writes=[rn])

    def rmsnorm(ntok, ccol, ph):
        rstd = RSTD
        for k in range(KD):
            if k % 2 == 0:
                R.op("act", lambda e, k=k: e.activation(out=SQ[:, 0, 0:ntok], in_=xT[:, k, 0:ntok], func=AF.Square),
                     reads=["xT", f"xTc{k}"], writes=["sq0"])
            else:
                R.op("pool", lambda e, k=k: e.tensor_tensor(out=SQ[:, 1, 0:ntok], in0=xT[:, k, 0:ntok], in1=xT[:, k, 0:ntok],
                                                            op=ALU.mult), reads=["xT", f"xTc{k}"], writes=["sq1"])
            pe_ops([mm(PS[6][:, 0:ntok], ones_bf[:], SQ[:, k % 2, 0:ntok], k == 0, k == KD - 1)],
                   reads=[f"sq{k % 2}", "const2"], writes=["ps6"])
        R.op("act", lambda e: e.activation(out=rstd[:, 0:ntok], in_=PS[6][:, 0:ntok], func=AF.Ln, bias=EPS, scale=1.0 / D),
             reads=["ps6"], writes=["rstd"])
        R.op("act", lambda e: e.activation(out=rstd[:, 0:ntok], in_=rstd[:, 0:ntok], func=AF.Exp, scale=-0.5),
             reads=["rstd"], writes=["rstd"])
        ht_fresh[0] = True
        for k in range(KD):
            R.op("dve", lambda e, k=k: e.scalar_tensor_tensor(out=hT[:, k, 0:ntok], in0=xT[:, k, 0:ntok],
                                                            scalar=cols[:, ccol + k:ccol + k + 1], in1=rstd[:, 0:ntok],
                                                            op0=ALU.mult, op1=ALU.mult),
                 reads=["xT", f"xTc{k}", "rstd", "lw"], writes=[f"hT{k}"])

    HT_ALL = [f"hT{k}" for k in range(KD)]

    def ht_ops(fns, extra_reads, writes):
        if ht_fresh[0] and len(fns) == KD:
            ht_fresh[0] = False
            for k, f in enumerate(fns):
                pe_ops([f], reads=[f"hT{k}"] + list(extra_reads), writes=writes)
        else:
            pe_ops(fns, reads=HT_ALL + list(extra_reads), writes=writes)
    psrot = [0]

    def next_ps(n=6):
        i = psrot[0] % n
        psrot[0] += 1
        return i

    def ffn(l, ntok, wa, wb, wc, ccol):
        conv_step(1)
        with ExitStack() as ph_es:
            ph = None
            act = ph_es.enter_context(sbt("f_act", [128, 22, TT], BF16))
            sl = [ph_es.enter_context(sbt(f"f_sl{i}", [128, TT], F32)) for i in range(2)]
            rmsnorm(ntok, ccol, ph)
            nblk = (DFF + 511) // 512
            ci = 0
            for cb in range(nblk):
                c0 = cb * 512
                ncol = min(512, DFF - c0)
                ta, ra = load_w(wa, l, cb)
                tb, rb = load_w(wb, l, cb)
                for j in range(ncol // 128):
                    f = cb * 4 + j
                    ia = next_ps(); ib = next_ps()
                    ht_ops([mm(PS[ia][:, 0:ntok], ta[:, k, j * 128:(j + 1) * 128], hT[:, k, 0:ntok], k == 0, k == KD - 1)
                            for k in range(KD)], [ra], [f"ps{ia}"])
                    ht_ops([mm(PS[ib][:, 0:ntok], tb[:, k, j * 128:(j + 1) * 128], hT[:, k, 0:ntok], k == 0, k == KD - 1)
                            for k in range(KD)], [rb], [f"ps{ib}"])
                    s = sl[ci % 2]; ci += 1
                    R.op("act", lambda e, s=s, ia=ia: e.activation(out=s[:, 0:ntok], in_=PS[ia][:, 0:ntok], func=AF.Silu),
                         reads=[f"ps{ia}"], writes=[f"sl{id(s)}"])
                    R.op("dve", lambda e, s=s, ib=ib, f=f: e.tensor_tensor(out=act[:, f, 0:ntok], in0=PS[ib][:, 0:ntok],
                                                                           in1=s[:, 0:ntok], op=ALU.mult),
                         reads=[f"ps{ib}", f"sl{id(s)}"], writes=[f"act{f}"])
            kbs = [(0, 8), (8, 8), (16, 6)]
            for nb in range(2):
                banks = [next_ps() for _ in range(4)]
                for bi, (k0, kn) in enumerate(kbs):
                    t, r = load_w(wc, l, (nb, bi))
                    for j in range(4):
                        pe_ops([mm(PS[banks[j]][:, 0:ntok], t[:, k, j * 128:(j + 1) * 128], act[:, k0 + k, 0:ntok],
                                   (bi == 0 and k == 0), (bi == 2 and k == kn - 1)) for k in range(kn)],
                               reads=[f"act{k0 + k}" for k in range(kn)] + [r], writes=[f"ps{banks[j]}"])
                for j in range(4):
                    n = nb * 4 + j
                    R.op("dve", lambda e, n=n, b=banks[j]: e.scalar_tensor_tensor(
                        out=xT[:, n, 0:ntok], in0=PS[b][:, 0:ntok], scalar=0.5, in1=xT[:, n, 0:ntok],
                        op0=ALU.mult, op1=ALU.add), reads=[f"ps{banks[j]}", "xT"], writes=["xT"])
            R.retire()

    def fm_block(t, r, ncol, ntok, evac):
        for j in range((ncol + 127) // 128):
            w = min(128, ncol - j * 128)
            b = next_ps()
            ht_ops([mm(PS[b][0:w, 0:ntok], t[:, k, j * 128:j * 128 + w], hT[:, k, 0:ntok], k == 0, k == KD - 1)
                    for k in range(KD)], [r], [f"ps{b}"])
            evac(j, b)

    def tm_block(t, r, subs, evac):
        for si, (c0, n) in enumerate(subs):
            b = next_ps()
            ht_ops([mm(PS[b][0:n, 0:512], hT[:, k, c0:c0 + n], t[:, k, 0:512], k == 0, k == KD - 1)
                    for k in range(KD)], [r], [f"ps{b}"])
            evac(si, b, n)

    def sb_attend(streams, N, qn):
        nch = len(streams[0]["chunks"])
        for st_ in streams:
            R.op("dve", lambda e, st_=st_: e.memset(st_["R"][:, 0:N], 0.0), writes=[f"R{st_['sid']}"])

        def stage1(st_, i):
            sid = st_["sid"]; groups = st_["groups"]
            c, key0, nk, ng = st_["chunks"][i]
            sbk = sid
            e_ap = st_["e"][i % 2]; en = st_["en"][i % 2]
            fns = []
            for gi, (hc, hp, h, oc, nq, col0, qc0) in enumerate(groups):
                last = (gi == len(groups) - 1) and ng is None
                if hp is None:
                    fns.append(mm(PS[sbk][0:nk, col0:col0 + nq], KT[:, hc, key0:key0 + nk],
                                  qn[:, h, qc0:qc0 + nq], gi == 0, last))
                else:
                    fns.append(mm(PS[sbk][0:nk, col0:col0 + nq], KT[hp:hp + 64, hc, key0:key0 + nk],
                                  qn[hp:hp + 64, hc, qc0:qc0 + nq], gi == 0, last))
            if ng is not None:
                fns.append(mm(PS[sbk][0:nk, 0:N], ident[0:nk, 0:nk], ng, False, True))
            pe_ops(fns, reads=["KT", "qn", "const", "const2"], writes=[f"ps{sbk}"])
            R.op("act", lambda e: e.activation(out=e_ap[0:nk, 0:N], in_=PS[sbk][0:nk, 0:N], func=AF.Exp, scale=0.125),
                 reads=[f"ps{sbk}"], writes=[en])
            R.op("act", lambda e: e.activation(out=st_["sp"][i % 2][0:nk, 0:N], in_=e_ap[0:nk, 0:N], func=AF.Ln,
                                               bias=1.0, scale=1.0), reads=[en], writes=[f"sp{sid}{i % 2}"])

        def stage2(st_, i):
            sid = st_["sid"]
            c, key0, nk, ng = st_["chunks"][i]
            pb = 2 + sid; qb = 4 + sid
            sp_ap = st_["sp"][i % 2]; spn = f"sp{sid}{i % 2}"
            e_ap = st_["e"][i % 2]; en = st_["en"][i % 2]
            t_ap = st_["tmp"][i % 2]; tn = st_["tn"][i % 2]
            a_ap = st_["a"][i % 2]; an = f"a{sid}{i % 2}"
            Rr = st_["R"]; rn = f"R{sid}"
            pe_ops([mm(PS[pb][0:nk, 0:N], tinc[0:nk, 0:nk], sp_ap[0:nk, 0:N], True, True)], reads=[spn, "const"],
                   writes=[f"ps{pb}"])
            if i < nch - 1:
                pe_ops([mm(PS[qb][:, 0:N], ones_bf[0:nk, :], sp_ap[0:nk, 0:N], True, True)], reads=[spn, "const2"],
                       writes=[f"ps{qb}"])
            R.op("dve", lambda e: e.tensor_tensor(out=t_ap[0:nk, 0:N], in0=PS[pb][0:nk, 0:N], in1=Rr[0:nk, 0:N], op=ALU.add),
                 reads=[f"ps{pb}", rn], writes=[tn])
            R.op("act", lambda e: e.activation(out=t_ap[0:nk, 0:N], in_=t_ap[0:nk, 0:N], func=AF.Exp, scale=-1.0),
                 reads=[tn], writes=[tn])
            R.op("pool", lambda e: e.tensor_tensor(out=a_ap[0:nk, 0:N], in0=e_ap[0:nk, 0:N], in1=t_ap[0:nk, 0:N], op=ALU.mult),
                 reads=[en, tn], writes=[an])
            if i < nch - 1:
                R.op("dve", lambda e: e.tensor_tensor(out=Rr[:, 0:N], in0=PS[qb][:, 0:N], in1=Rr[:, 0:N], op=ALU.add),
                     reads=[f"ps{qb}", rn], writes=[rn])

        def stage3(st_, i):
            sid = st_["sid"]; groups = st_["groups"]
            c, key0, nk, ng = st_["chunks"][i]
            a_ap = st_["a"][i % 2]; an = f"a{sid}{i % 2}"
            fns = []
            for gi, (hc, hp, h, oc, nq, col0, qc0) in enumerate(groups):
                ohp = (h % 2) * 64
                st0 = (i == 0) and (gi < 2)
                fns.append(mm(PS[6][ohp:ohp + 64, oc:oc + nq], VC[0:nk, c, h * 64:(h + 1) * 64],
                              a_ap[0:nk, col0:col0 + nq], st0, i == nch - 1))
            pe_ops(fns, reads=[an, "VC"], writes=["ps6"])

        for i in range(nch + 2):
            if i < nch:
                for st_ in streams:
                    stage1(st_, i)
            if 1 <= i <= nch:
                for st_ in streams:
                    stage2(st_, i - 1)
            if i >= 2:
                for st_ in streams:
                    stage3(st_, i - 2)

    def mixer(l, tile):
        ntok, subs, segs, c0g, is_sample, tix = tile
        nsub = len(subs)
        nseg = len(segs)
        L = segs[0][1]
        mx = ExitStack()

        def ph_sb(name, shape, dt=F32):
            return mx.enter_context(sbt(name, list(shape), dt))
        rmsnorm(ntok, C_NM, None)

        conv_step(1)
        with ExitStack() as pa:
            xpe = pa.enter_context(sbt("a_xpe", [128, 4, nseg, 15 + L], F32))
            tb_ = [pa.enter_context(sbt(f"a_t{i}", [128, nseg, 15 + L], F32)) for i in range(4)]
            dd = pa.enter_context(sbt("a_d", [128, 4, nseg, L], BF16))
            XPE = [f"xpe{g}" for g in range(4)]
            t, r = load_w("w_in", l, O_XP)
            R.op("dve", lambda e: e.memset(tb_[0][:], 0.0), writes=["t0"])
            R.op("dve", lambda e: e.memset(tb_[1][:], 0.0), writes=["t1"])
            R.op("pool", lambda e: e.memset(tb_[2][:], 0.0), writes=["t2"])
            R.op("pool", lambda e: e.memset(tb_[3][:], 0.0), writes=["t3"])
            if is_sample:
                for s in range(2):
                    R.dma("sp", lambda e, s=s: e.dma_start(out=xpe[:, :, s, 0:15],
                                                           in_=spT[l, s].rearrange("(g p) t -> p g t", p=128)),
                          writes=XPE)
            else:
                R.op("dve", lambda e: e.tensor_copy(out=xpe[:, :, 0, 0:15], in_=phist[:]), reads=["phist"], writes=XPE)

            def ev_xp(j, b):
                R.op("act", lambda e, j=j, b=b: e.activation(
                    out=xpe[:, j, :, 15:15 + L], in_=PS[b][:, 0:ntok].rearrange("p (s t) -> p s t", s=nseg), func=AF.Copy),
                    reads=[f"ps{b}"], writes=[f"xpe{j}"])
            fm_block(t, r, 512, ntok, ev_xp)
            for g in (3, 0, 1, 2):
                w = 2 << g
                eng = "pool" if g == 3 else "dve"
                ti = (2, 3) if g == 3 else (0, 1)
                src = xpe[:, g]
                cur = None
                curn = None
                sh = 1
                for step in range(g + 1):
                    dst = tb_[ti[step % 2]]; dstn = f"t{ti[step % 2]}"
                    a_in = src if cur is None else cur
                    a_n = f"xpe{g}" if cur is None else curn
                    R.op(eng, lambda e, dst=dst, a_in=a_in, sh=sh: e.tensor_tensor(
                        out=dst[:, :, sh:15 + L], in0=a_in[:, :, sh:15 + L], in1=a_in[:, :, 0:15 + L - sh], op=ALU.add),
                        reads=[a_n], writes=[dstn])
                    cur = dst; curn = dstn
                    sh *= 2
                if eng == "pool":
                    oth = tb_[ti[(g + 1) % 2]]; othn = f"t{ti[(g + 1) % 2]}"
                    R.op(eng, lambda e, cur=cur, oth=oth, w=w: e.tensor_scalar(out=oth[:, :, 15:15 + L], in0=cur[:, :, 15:15 + L],
                                                                               scalar1=1.0 / w, scalar2=None, op0=ALU.mult),
                         reads=[curn], writes=[othn])
                    R.op(eng, lambda e, oth=oth, g=g: e.tensor_tensor(out=dd[:, g], in0=oth[:, :, 15:15 + L],
                                                                      in1=xpe[:, g, :, 15:15 + L], op=ALU.subtract),
                         reads=[othn, f"xpe{g}"], writes=[f"dd{g}"])
                else:
                    R.op(eng, lambda e, cur=cur, g=g, w=w: e.scalar_tensor_tensor(
                        out=dd[:, g], in0=cur[:, :, 15:15 + L], scalar=1.0 / w, in1=xpe[:, g, :, 15:15 + L],
                        op0=ALU.mult, op1=ALU.subtract), reads=[curn, f"xpe{g}"], writes=[f"dd{g}"])
                if (not is_sample) and tix == 0:
                    R.op(eng, lambda e, cur=cur, g=g: e.tensor_tensor(out=cur[:, 0, 15:31], in0=cur[:, 0, 15:31],
                                                                      in1=invc[:, g, :], op=ALU.mult),
                         reads=[curn, "const", f"dd{g}"], writes=[curn])
                    R.op(eng, lambda e, cur=cur, g=g: e.tensor_tensor(out=dd[:, g, 0, 0:16], in0=cur[:, 0, 15:31],
                                                                      in1=xpe[:, g, 0, 15:31], op=ALU.subtract),
                         reads=[curn, f"xpe{g}"], writes=[f"dd{g}"])
            for g in range(4):
                b = next_ps()
                pe_ops([mm(PS[b][:, 0:ntok], poolw_sb[:, g, :], dd[:, g].rearrange("p s t -> p (s t)"), True, True)],
                       reads=[f"dd{g}", "lw"], writes=[f"ps{b}"])
                R.op("act", lambda e, g=g, b=b: e.activation(out=ypool[:, g, 0:ntok], in_=PS[b][:, 0:ntok], func=AF.Copy,
                                                             scale=cols[:, C_PS + g:C_PS + g + 1]),
                     reads=[f"ps{b}", "lw"], writes=["ypool"])
            if is_sample:
                for s in range(2):
                    R.dma("sp", lambda e, s=s: e.dma_start(out=poolsT[l, s].rearrange("(g p) t -> p g t", p=128),
                                                           in_=xpe[:, :, s, L:L + 15]), reads=XPE)
            else:
                R.op("dve", lambda e: e.tensor_copy(out=phist[:], in_=xpe[:, :, 0, L:L + 15]), reads=XPE, writes=["phist"])
                if tix == NT - 1:
                    R.dma("sp", lambda e: e.dma_start(out=poolpT[l].rearrange("(g p) t -> p g t", p=128),
                                                      in_=xpe[:, :, 0, L:L + 15]), reads=XPE)
            R.retire()

        if DBG < 3:
            mx.close()
            return
        conv_step(1)
        with ExitStack() as pb_:
            def bsb(name, shape, dt=F32):
                return pb_.enter_context(sbt(name, list(shape), dt))
            qf = bsb("b_qf", [128, 4, TT]); kf = bsb("b_kf", [128, 4, TT])
            qn = bsb("b_qn", [128, 4, TT], BF16)
            rsb = [RSTD, bsb("b_rs2", [128, TT])]; rsn = ["rstd", "rs2"]
            vst = bsb("b_vst", [128, nsub, BW])
            QF = [f"qf{j}" for j in range(4)]; KF = [f"kf{j}" for j in range(4)]
            nstream = 1 if is_sample else 2
            strm = []
            for sid in range(nstream):
                strm.append({"sid": sid,
                             "e": [qf[:, 2 * sid + i, :] for i in range(2)], "en": [f"qf{2 * sid + i}" for i in range(2)],
                             "tmp": [kf[:, 2 * sid + i, :] for i in range(2)], "tn": [f"kf{2 * sid + i}" for i in range(2)],
                             "sp": [bsb(f"b_sp{sid}{i}", [128, 512], BF16) for i in range(2)],
                             "a": [bsb(f"b_a{sid}{i}", [128, 512], BF16) for i in range(2)],
                             "R": bsb(f"b_R{sid}", [128, 512])})
            t, r = load_w("w_in", l, O_Q)
            fm_block(t, r, 512, ntok, lambda j, b: R.op("act", lambda e, j=j, b=b: e.activation(
                out=qf[:, j, 0:ntok], in_=PS[b][:, 0:ntok], func=AF.Copy), reads=[f"ps{b}"], writes=[f"qf{j}"]))
            t, r = load_w("w_in", l, O_K)
            fm_block(t, r, 512, ntok, lambda j, b: R.op("act", lambda e, j=j, b=b: e.activation(
                out=kf[:, j, 0:ntok], in_=PS[b][:, 0:ntok], func=AF.Copy), reads=[f"ps{b}"], writes=[f"kf{j}"]))

            def qknorm(src, pre, outs):
                banks = {}
                def stA(j):
                    R.op("pool", lambda e, j=j: e.tensor_tensor(out=SQ[:, j % 2, 0:ntok], in0=src[:, j, 0:ntok],
                                                                in1=src[:, j, 0:ntok], op=ALU.mult),
                         reads=[f"{pre}{j}"], writes=[f"sq{j % 2}"])
                    b = next_ps(); banks[j] = b
                    pe_ops([mm(PS[b][:, 0:ntok], bdiag[:], SQ[:, j % 2, 0:ntok], True, True)], reads=[f"sq{j % 2}", "const2"],
                           writes=[f"ps{b}"])
                    R.op("dve", lambda e, b=b, j=j: e.tensor_scalar(out=rsb[j % 2][:, 0:ntok], in0=PS[b][:, 0:ntok],
                                                                    scalar1=1.0 / 64, scalar2=EPS, op0=ALU.mult, op1=ALU.add),
                         reads=[f"ps{b}"], writes=[rsn[j % 2]])
                def stB(j):
                    rsqrt_ip(rsb[j % 2][:, 0:ntok], rsn[j % 2])
                    outs(j)
                for st in range(5):
                    if st < 4:
                        stA(st)
                    if st >= 1:
                        stB(st - 1)
            if is_sample:
                qz = bsb("b_qz", [128, 8, 2 * DEC], BF16)
                R.op("dve", lambda e: e.memset(qz[:], 0.0), writes=["qn"])
            def q_out(j):
                if is_sample:
                    R.op("dve", lambda e, j=j: e.scalar_tensor_tensor(out=qf[:, j, 0:ntok], in0=qf[:, j, 0:ntok],
                                                                      scalar=cols[:, C_QN:C_QN + 1], in1=rsb[j % 2][:, 0:ntok],
                                                                      op0=ALU.mult, op1=ALU.mult),
                         reads=[f"qf{j}", rsn[j % 2], "lw"], writes=[f"qf{j}"])
                    for hh in range(2):
                        h = 2 * j + hh
                        R.op("act", lambda e, j=j, h=h, hh=hh: e.activation(
                            out=qz[hh * 64:hh * 64 + 64, h, :], in_=qf[hh * 64:hh * 64 + 64, j, 0:ntok], func=AF.Copy),
                            reads=[f"qf{j}"], writes=["qn"])
                else:
                    R.op("dve", lambda e, j=j: e.scalar_tensor_tensor(out=qn[:, j, 0:ntok], in0=qf[:, j, 0:ntok],
                                                                      scalar=cols[:, C_QN:C_QN + 1], in1=rsb[j % 2][:, 0:ntok],
                                                                      op0=ALU.mult, op1=ALU.mult),
                         reads=[f"qf{j}", rsn[j % 2], "lw"], writes=["qn"])
            qknorm(qf, "qf", q_out)
            def k_out(j):
                R.op("dve", lambda e, j=j: e.scalar_tensor_tensor(out=kf[:, j, 0:ntok], in0=kf[:, j, 0:ntok],
                                                                  scalar=cols[:, C_KN:C_KN + 1], in1=rsb[j % 2][:, 0:ntok],
                                                                  op0=ALU.mult, op1=ALU.mult),
                     reads=[f"kf{j}", rsn[j % 2], "lw"], writes=[f"kf{j}"])
            qknorm(kf, "kf", k_out)
            t, r = load_w("w_in", l, O_V)
            tm_block(t, r, subs, lambda si, b, n: R.op("act", lambda e, si=si, b=b, n=n: e.activation(
                out=vst[0:n, si, :], in_=PS[b][0:n, :], func=AF.Copy), reads=[f"ps{b}"], writes=["vst"]))
            if is_sample:
                R.dma("sp", lambda e: e.dma_start(out=ksT[l].rearrange("(j p) t -> p j t", p=128), in_=kf[:, :, 0:ntok]),
                      reads=KF)
                for si in range(2):
                    R.dma("sp", lambda e, si=si: e.dma_start(out=vs[l, si * DEC:(si + 1) * DEC, :], in_=vst[0:DEC, si, :]),
                          reads=["vst"])
            else:
                R.dma("sp", lambda e: e.dma_start(out=kpT[l].rearrange("(j p) t -> p j t", p=128)[:, :, c0g:c0g + ntok],
                                                  in_=kf[:, :, 0:ntok]), reads=KF)
                R.dma("sp", lambda e: e.dma_start(out=vp[l, c0g:c0g + ntok, :].rearrange("(s p) f -> p s f", p=128),
                                                  in_=vst[:, :, :]), reads=["vst"])
            if is_sample:
                knew = bsb("b_knew", [128, 4, 2 * DEC], BF16)
                R.op("act", lambda e: e.activation(out=knew[:], in_=kf[:, :, 0:ntok], func=AF.Copy), reads=KF, writes=["knew"])
                for s in range(2):
                    for j in range(4):
                        R.dma("pool", lambda e, s=s, j=j: e.dma_start(out=KT[:, j, 0:PAST],
                                                                      in_=ckT[l, s, j * 128:(j + 1) * 128, :]),
                              writes=["KT"])
                    for c8 in range(0, NPC, 8):
                        c9 = min(NPC, c8 + 8)
                        R.dma("pool", lambda e, s=s, c8=c8, c9=c9: e.dma_start(
                            out=VC[:, c8:c9, :], in_=cv[l, s, c8 * 128:c9 * 128, :].rearrange("(c p) f -> p c f", p=128)),
                            writes=["VC"])
                    R.op("act", lambda e, s=s: e.activation(out=KT[:, :, PAST:PAST + DEC], in_=knew[:, :, s * DEC:(s + 1) * DEC],
                                                            func=AF.Copy), reads=["knew"], writes=["KT"])
                    R.op("act", lambda e, s=s: e.activation(out=VC[0:DEC, NPC, :], in_=vst[0:DEC, s, :], func=AF.Copy),
                         reads=["vst"], writes=["VC"])
                    strm[0]["groups"] = [(h // 2, None, h, (h // 2) * DEC, DEC, h * DEC, s * DEC) for h in range(8)]
                    strm[0]["chunks"] = [(NPC, PAST, DEC, negs[:, :])] + [(c, c * 128, 128, None) for c in range(NPC - 1, -1, -1)]
                    sb_attend(strm, 8 * DEC, qz)
                    R.op("act", lambda e, s=s: e.activation(
                        out=ysb[:, :, s * DEC:(s + 1) * DEC], in_=PS[6][:, 0:4 * DEC].rearrange("p (m t) -> p m t", m=4),
                        func=AF.Copy), reads=["ps6"], writes=["ysb"])
            else:
                pc0 = c0g // 128
                R.op("act", lambda e: e.activation(out=KT[:, :, c0g:c0g + ntok], in_=kf[:, :, 0:ntok], func=AF.Copy),
                     reads=KF, writes=["KT"])
                R.op("act", lambda e: e.activation(out=VC[:, pc0:pc0 + 4, :], in_=vst[:, :, :], func=AF.Copy),
                     reads=["vst"], writes=["VC"])
                chunks = []
                for c in range(pc0 + 3, -1, -1):
                    dgi = c - pc0
                    chunks.append((c, c * 128, 128, negp[:, dgi * 512:(dgi + 1) * 512] if dgi >= 0 else None))
                for m in range(4):
                    for hh in range(2):
                        strm[hh]["groups"] = [(m, hh * 64, 2 * m + hh, 0, TT, 0, 0)]
                        strm[hh]["chunks"] = chunks
                    sb_attend(strm, TT, qn)
                    R.op("act", lambda e, m=m: e.activation(out=ysb[:, m, 0:ntok], in_=PS[6][:, 0:ntok], func=AF.Copy),
                         reads=["ps6"], writes=["ysb"])
            R.retire()

        if DBG < 4:
            mx.close()
            return
        conv_step(1)
        with ExitStack() as pc_:
            gu = pc_.enter_context(sbt("c_gu", [128, 4, TT], F32))
            gvb = pc_.enter_context(sbt("c_gvb", [128, nsub, BW], BF16))
            gvf = pc_.enter_context(sbt("c_gvf", [128, nsub, BW], F32))
            t, r = load_w("w_in", l, O_GU)
            fm_block(t, r, 512, ntok, lambda j, b: R.op("act", lambda e, j=j, b=b: e.activation(
                out=gu[:, j, 0:ntok], in_=PS[b][:, 0:ntok], func=AF.Gelu_apprx_tanh), reads=[f"ps{b}"], writes=["gu"]))
            t, r = load_w("w_in", l, O_GV)
            def ev_gv(si, b, n):
                R.op("act", lambda e: e.activation(out=gvf[0:n, si, :], in_=PS[b][0:n, :], func=AF.Gelu_apprx_tanh),
                     reads=[f"ps{b}"], writes=["gvf"])
                R.op("dve", lambda e: e.tensor_copy(out=gvb[0:n, si, :], in_=gvf[0:n, si, :]), reads=["gvf"], writes=["gvb"])
            tm_block(t, r, subs, ev_gv)
            if is_sample:
                for si in range(2):
                    R.dma("sp", lambda e, si=si: e.dma_start(out=gms[l, si * DEC:(si + 1) * DEC, :], in_=gvf[0:DEC, si, :]),
                          reads=["gvf"])
            for si, (c0, n) in enumerate(subs):
                b = next_ps()
                fns = []
                for g in range(4):
                    fns.append(mm(PS[b][:, g * 128:g * 128 + n], gvb[0:n, si, g * 128:(g + 1) * 128], wmT[0:n, g, 0:n],
                                  g == 0, False))
                    fns.append(mm(PS[b][:, g * 128:g * 128 + n], ones_bf[0:1, :], gb_hi[0:1, g * 128:g * 128 + n], False, False))
                    fns.append(mm(PS[b][:, g * 128:g * 128 + n], ones_bf[0:1, :], gb_lo[0:1, g * 128:g * 128 + n], False, g == 3))
                pe_ops(fns, reads=["gvb", "lw", "const2"], writes=[f"ps{b}"])
                R.op("dve", lambda e, b=b, c0=c0, n=n: e.tensor_tensor(
                    out=ygm[:, :, c0:c0 + n], in0=PS[b][:, :].rearrange("p (g i) -> p g i", g=4)[:, :, 0:n],
                    in1=gu[:, :, c0:c0 + n], op=ALU.mult), reads=[f"ps{b}", "gu"], writes=["ygm"])
            R.retire()

        if DBG < 5:
            mx.close()
            return
        conv_step(1)
        with ExitStack() as pd_:
            def dsb(name, shape, dt=F32):
                return pd_.enter_context(sbt(name, list(shape), dt))
            lq = dsb("d_lq", [128, 2, TT]); lk = dsb("d_lk", [128, 2, TT])
            lvb = dsb("d_lvb", [128, nsub, BW], BF16)
            laT = dsb("d_la", [16, TT], BF16)
            lr = dsb("d_lr", [128, 4, TT], BF16)
            eg = dsb("d_eg", [128, 2, 256]); spg = dsb("d_spg", [128, 2, 256])
            eb = dsb("d_eb", [128, 2, TT])
            qt = dsb("d_qt", [128, 2, TT], BF16); kt = dsb("d_kt", [128, 2, TT], BF16)
            ktok = dsb("d_ktok", [128, nsub, 256], BF16)
            attb = [dsb(f"d_att{i}", [128, 128], BF16) for i in range(2)]
            oT = dsb("d_oT", [128, 4, TT])
            enb = oT[:, 0:2, :]
            ors2 = dsb("d_ors2", [128, TT])
            t, r = load_w("w_in", l, O_LQ)
            def ev_lqk(j, b):
                dst = lq if j < 2 else lk
                R.op("act", lambda e: e.activation(out=dst[:, j % 2, 0:ntok], in_=PS[b][:, 0:ntok], func=AF.Copy),
                     reads=[f"ps{b}"], writes=["lq" if j < 2 else "lk"])
            fm_block(t, r, 512, ntok, ev_lqk)
            t, r = load_w("w_in", l, O_LV)
            tm_block(t, r, subs, lambda si, b, n: R.op("act", lambda e, si=si, b=b, n=n: e.activation(
                out=lvb[0:n, si, :], in_=PS[b][0:n, :], func=AF.Copy), reads=[f"ps{b}"], writes=["lvb"]))
            b = next_ps()
            ht_ops([mm(PS[b][0:16, 0:ntok], wla_sb[:, k, :], hT[:, k, 0:ntok], k == 0, k == KD - 1) for k in range(KD)], ["lw"], [f"ps{b}"])
            R.op("act", lambda e, b=b: e.activation(out=laT[:, 0:ntok], in_=PS[b][0:16, 0:ntok], func=AF.Copy),
                 reads=[f"ps{b}"], writes=["laT"])
            t, r = load_w("w_in", l, O_LR)
            fm_block(t, r, 512, ntok, lambda j, b: R.op("act", lambda e, j=j, b=b: e.activation(
                out=lr[:, j, 0:ntok], in_=PS[b][:, 0:ntok], func=AF.Silu), reads=[f"ps{b}"], writes=["lr"]))
            for si, (c0, n) in enumerate(subs):
                b = next_ps()
                pe_ops([mm(PS[b][0:n, 0:256], laT[:, c0:c0 + n], wa2_sb[:, :], True, False),
                        mm(PS[b][0:n, 0:256], ones_bf[0:1, 0:n], ba_sb[0:1, :], False, True)],
                       reads=["laT", "lw", "const2"], writes=[f"ps{b}"])
                R.op("act", lambda e, si=si, b=b, n=n: e.activation(out=eg[0:n, si % 2, :], in_=PS[b][0:n, 0:256], func=AF.Exp,
                                                                    scale=-1.0), reads=[f"ps{b}"], writes=[f"eg{si % 2}"])
                R.op("act", lambda e, si=si, n=n: e.activation(out=spg[0:n, si % 2, :], in_=eg[0:n, si % 2, :], func=AF.Ln, bias=1.0,
                                                               scale=1.0), reads=[f"eg{si % 2}"], writes=[f"spg{si % 2}"])
                for fc in range(2):
                    b2 = next_ps()
                    pe_ops([mm(PS[b2][:, 0:n], spg[0:n, si % 2, fc * 128:(fc + 1) * 128], trile[0:n, 0:n], True, True)],
                           reads=[f"spg{si % 2}", "const"], writes=[f"ps{b2}"])
                    R.op("act", lambda e, fc=fc, b2=b2, c0=c0, n=n: e.activation(
                        out=eb[:, fc, c0:c0 + n], in_=PS[b2][:, 0:n], func=AF.Exp, scale=-1.0 / 16),
                        reads=[f"ps{b2}"], writes=["eb"])
                    R.op("act", lambda e, fc=fc, b2=b2, c0=c0, n=n: e.activation(
                        out=enb[:, fc, c0:c0 + n], in_=PS[b2][:, 0:n], func=AF.Exp, scale=1.0 / 16),
                        reads=[f"ps{b2}"], writes=["oT0", "oT1"])
            for fc in range(2):
                R.op("dve", lambda e, fc=fc: e.scalar_tensor_tensor(out=qt[:, fc, 0:ntok], in0=lq[:, fc, 0:ntok], scalar=0.125,
                                                                    in1=eb[:, fc, 0:ntok], op0=ALU.mult, op1=ALU.mult),
                     reads=["lq", "eb"], writes=["qt"])
                R.op("dve", lambda e, fc=fc: e.tensor_tensor(out=kt[:, fc, 0:ntok], in0=lk[:, fc, 0:ntok],
                                                             in1=enb[:, fc, 0:ntok], op=ALU.mult),
                     reads=["lk", "oT0", "oT1"], writes=["kt"])
            for si, (c0, n) in enumerate(subs):
                for fc in range(2):
                    pe_ops([lambda pe, fc=fc, c0=c0, n=n: pe.transpose(PSB[0:n, fc * 128:(fc + 1) * 128],
                                                                        kt[:, fc, c0:c0 + n], ident[:, :])],
                           reads=["kt", "const"], writes=["psb"])
                R.op("act", lambda e, si=si, n=n: e.activation(out=ktok[0:n, si, :], in_=PSB[0:n, 0:256], func=AF.Copy),
                     reads=["psb"], writes=["ktok"])
            for si, (c0, n) in enumerate(subs):
                seq = si if is_sample else 0
                if is_sample or (tix == 0 and si == 0):
                    if is_sample:
                        for hh in range(4):
                            fp = (hh % 2) * 64
                            R.dma("sp", lambda e, hh=hh, fp=fp, seq=seq: e.dma_start(
                                out=S32[fp:fp + 64, hh * 128:(hh + 1) * 128], in_=sgl[l, seq, hh]), writes=[f"S32{hh}"])
                        for hh in range(4):
                            fp = (hh % 2) * 64
                            R.op("act", lambda e, hh=hh, fp=fp: e.activation(out=Sbf[fp:fp + 64, hh * 128:(hh + 1) * 128],
                                                                             in_=S32[fp:fp + 64, hh * 128:(hh + 1) * 128],
                                                                             func=AF.Copy), reads=[f"S32{hh}"], writes=[f"Sbf{hh}"])
                    else:
                        R.op("dve", lambda e: e.memset(S32[:], 0.0), writes=[f"S32{h_}" for h_ in range(4)])
                        R.op("dve", lambda e: e.memset(Sbf[:], 0.0), writes=[f"Sbf{h_}" for h_ in range(4)])
                for hh in range(4):
                    fc = hh // 2; fp = (hh % 2) * 64
                    ab = hh % 2
                    ob = 2 + hh % 2
                    pe_ops([mm(PS[ab][0:n, 0:n], kt[fp:fp + 64, fc, c0:c0 + n], qt[fp:fp + 64, fc, c0:c0 + n], True, True)],
                           reads=["kt", "qt"], writes=[f"ps{ab}"])
                    R.op("dve", lambda e, ab=ab, n=n: e.tensor_tensor(out=attb[ab][0:n, 0:n], in0=PS[ab][0:n, 0:n],
                                                                      in1=trile[0:n, 0:n], op=ALU.mult),
                         reads=[f"ps{ab}", "const"], writes=[f"att{ab}"])
                    pe_ops([mm(PS[ob][:, 0:n], Sbf[fp:fp + 64, hh * 128:(hh + 1) * 128], qt[fp:fp + 64, fc, c0:c0 + n], True, False),
                            mm(PS[ob][:, 0:n], lvb[0:n, si, hh * 128:(hh + 1) * 128], attb[ab][0:n, 0:n], False, True)],
                           reads=[f"Sbf{hh}", "qt", "lvb", f"att{ab}"], writes=[f"ps{ob}"])
                    R.op("act", lambda e, hh=hh, ob=ob, c0=c0, n=n: e.activation(out=oT[:, hh, c0:c0 + n], in_=PS[ob][:, 0:n],
                                                                                func=AF.Copy), reads=[f"ps{ob}"], writes=[f"oT{hh}"])
                    db = 4 + hh % 2
                    pe_ops([mm(PS[db][fp:fp + 64, hh * 128:(hh + 1) * 128], ktok[0:n, si, fc * 128 + fp:fc * 128 + fp + 64],
                               lvb[0:n, si, hh * 128:(hh + 1) * 128], True, True)], reads=["ktok", "lvb"], writes=[f"ps{db}"])
                    R.op("dve", lambda e, hh=hh, fp=fp, db=db: e.tensor_tensor(
                        out=S32[fp:fp + 64, hh * 128:(hh + 1) * 128], in0=PS[db][fp:fp + 64, hh * 128:(hh + 1) * 128],
                        in1=S32[fp:fp + 64, hh * 128:(hh + 1) * 128], op=ALU.add), reads=[f"ps{db}", f"S32{hh}"], writes=[f"S32{hh}"])
                    R.op("dve", lambda e, hh=hh, fp=fp, fc=fc, c0=c0, n=n: e.tensor_scalar(
                        out=S32[fp:fp + 64, hh * 128:(hh + 1) * 128], in0=S32[fp:fp + 64, hh * 128:(hh + 1) * 128],
                        scalar1=eb[fp:fp + 64, fc, c0 + n - 1:c0 + n], scalar2=None, op0=ALU.mult),
                        reads=[f"S32{hh}", "eb"], writes=[f"S32{hh}"])
                    R.op("act", lambda e, hh=hh, fp=fp: e.activation(out=Sbf[fp:fp + 64, hh * 128:(hh + 1) * 128],
                                                                     in_=S32[fp:fp + 64, hh * 128:(hh + 1) * 128], func=AF.Copy),
                         reads=[f"S32{hh}"], writes=[f"Sbf{hh}"])
                if is_sample or (tix == NT - 1 and si == nsub - 1):
                    for hh in range(4):
                        fp = (hh % 2) * 64
                        dst = glas[l, seq, hh] if is_sample else glap[l, hh]
                        R.dma("sp", lambda e, hh=hh, fp=fp, dst=dst: e.dma_start(
                            out=dst, in_=S32[fp:fp + 64, hh * 128:(hh + 1) * 128]), reads=[f"S32{hh}"])
            orsb = [RSTD, ors2]; orsn = ["rstd", "ors2"]

            def onA(hh):
                R.op("act", lambda e: e.activation(out=SQ[:, hh % 2, 0:ntok], in_=oT[:, hh, 0:ntok], func=AF.Square),
                     reads=[f"oT{hh}"], writes=[f"sq{hh % 2}"])
                b = next_ps()
                pe_ops([mm(PS[b][:, 0:ntok], ones_bf[:], SQ[:, hh % 2, 0:ntok], True, True)], reads=[f"sq{hh % 2}", "const2"],
                       writes=[f"ps{b}"])
                R.op("act", lambda e: e.activation(out=orsb[hh % 2][:, 0:ntok], in_=PS[b][:, 0:ntok], func=AF.Ln, bias=EPS,
                                                   scale=1.0 / 128), reads=[f"ps{b}"], writes=[orsn[hh % 2]])

            def onB(hh):
                R.op("act", lambda e: e.activation(out=orsb[hh % 2][:, 0:ntok], in_=orsb[hh % 2][:, 0:ntok], func=AF.Exp,
                                                   scale=-0.5), reads=[orsn[hh % 2]], writes=[orsn[hh % 2]])
                R.op("dve", lambda e: e.scalar_tensor_tensor(out=oT[:, hh, 0:ntok], in0=oT[:, hh, 0:ntok],
                                                             scalar=cols[:, C_GON:C_GON + 1], in1=orsb[hh % 2][:, 0:ntok],
                                                             op0=ALU.mult, op1=ALU.mult),
                     reads=[f"oT{hh}", orsn[hh % 2], "lw"], writes=[f"oT{hh}"])
                R.op("pool", lambda e: e.tensor_tensor(out=ygl[:, hh, 0:ntok], in0=oT[:, hh, 0:ntok],
                                                       in1=lr[:, hh, 0:ntok], op=ALU.mult),
                     reads=[f"oT{hh}", "lr"], writes=["ygl"])
            for st in range(5):
                if st < 4:
                    onA(st)
                if st >= 1:
                    onB(st - 1)
            R.retire()

        if DBG < 6:
            mx.close()
            return
        conv_step(1)
        with ExitStack() as pe_:
            acc = pe_.enter_context(sbt("e_acc", [128, 4, TT], F32))
            sg = [pe_.enter_context(sbt(f"e_sg{i}", [128, TT], F32)) for i in range(2)]
            tm = [pe_.enter_context(sbt(f"e_tm{i}", [128, TT], F32)) for i in range(2)]
            mg = pe_.enter_context(sbt("e_mg", [128, KD, TT], BF16))
            ybs = [ypool, ysb, ygm, ygl]
            ynm = ["ypool", "ysb", "ygm", "ygl"]
            ci = 0
            for nb in range(2):
                for bnum in range(4):
                    tb, rb = load_w("w_br", l, (bnum, nb))
                    tg, rg = load_w("w_in", l, O_G + bnum * D + nb * 512)
                    for j in range(4):
                        ip = next_ps(); ig = next_ps()
                        pe_ops([mm(PS[ip][:, 0:ntok], tb[:, k, j * 128:(j + 1) * 128], ybs[bnum][:, k, 0:ntok], k == 0, k == 3)
                                for k in range(4)], reads=[ynm[bnum], rb], writes=[f"ps{ip}"])
                        ht_ops([mm(PS[ig][:, 0:ntok], tg[:, k, j * 128:(j + 1) * 128], hT[:, k, 0:ntok], k == 0, k == KD - 1)
                                for k in range(KD)], [rg], [f"ps{ig}"])
                        s_ = sg[ci % 2]; t_ = tm[ci % 2]; sn = f"sg{ci % 2}"; tn = f"tm{ci % 2}"; ci += 1
                        R.op("act", lambda e, s_=s_, ig=ig: e.activation(out=s_[:, 0:ntok], in_=PS[ig][:, 0:ntok], func=AF.Sigmoid),
                             reads=[f"ps{ig}"], writes=[sn])
                        if bnum == 0:
                            R.op("dve", lambda e, s_=s_, ip=ip, j=j: e.tensor_tensor(out=acc[:, j, 0:ntok], in0=PS[ip][:, 0:ntok],
                                                                                     in1=s_[:, 0:ntok], op=ALU.mult),
                                 reads=[f"ps{ip}", sn], writes=[f"acc{j}"])
                        else:
                            R.op("dve", lambda e, s_=s_, t_=t_, ip=ip: e.tensor_tensor(out=t_[:, 0:ntok], in0=PS[ip][:, 0:ntok],
                                                                                       in1=s_[:, 0:ntok], op=ALU.mult),
                                 reads=[f"ps{ip}", sn], writes=[tn])
                            if bnum < 3:
                                R.op("dve", lambda e, t_=t_, j=j: e.tensor_tensor(out=acc[:, j, 0:ntok], in0=acc[:, j, 0:ntok],
                                                                                  in1=t_[:, 0:ntok], op=ALU.add),
                                     reads=[tn, f"acc{j}"], writes=[f"acc{j}"])
                            else:
                                R.op("dve", lambda e, t_=t_, j=j, nb=nb: e.tensor_tensor(
                                    out=mg[:, nb * 4 + j, 0:ntok], in0=acc[:, j, 0:ntok], in1=t_[:, 0:ntok], op=ALU.add),
                                    reads=[tn, f"acc{j}"], writes=[f"mg{nb * 4 + j}"])
            for nb in range(2):
                t, r = load_w("w_o", l, nb)
                for j in range(4):
                    n = nb * 4 + j
                    b = next_ps()
                    pe_ops([mm(PS[b][:, 0:ntok], t[:, k, j * 128:(j + 1) * 128], mg[:, k, 0:ntok], k == 0, k == KD - 1)
                            for k in range(KD)], reads=[f"mg{k}" for k in range(KD)] + [r], writes=[f"ps{b}"])
                    R.op("dve", lambda e, n=n, b=b: e.tensor_tensor(out=xT[:, n, 0:ntok], in0=PS[b][:, 0:ntok],
                                                                    in1=xT[:, n, 0:ntok], op=ALU.add),
                         reads=[f"ps{b}", "xT"], writes=["xT"])
            R.retire()
        mx.close()

    def ple(l, tile, store_fn=None):
        ntok, subs, segs, c0g, is_sample, tix = tile
        cg = (S if is_sample else c0g)
        conv_step(1)
        with ExitStack() as pp_:
            ph = None
            sg = [pp_.enter_context(sbt(f"p_sg{i}", [128, TT], F32)) for i in range(2)]
            tm = [pp_.enter_context(sbt(f"p_tm{i}", [128, TT], F32)) for i in range(2)]
            R.dma("pool", lambda e: e.dma_start(out=pT[:, :, 0:ntok],
                                                in_=pin[l].rearrange("(k p) t -> p k t", p=128)[:, :, cg:cg + ntok]),
                  writes=["pT"])
            rmsnorm(ntok, C_NPL, ph)
            ci = 0
            for nb in range(2):
                t, r = load_w("w_pg", l, nb)
                for j in range(4):
                    n = nb * 4 + j
                    ig = next_ps(); ip = next_ps()
                    ht_ops([mm(PS[ig][:, 0:ntok], t[:, k, j * 128:(j + 1) * 128], hT[:, k, 0:ntok], k == 0, k == KD - 1)
                            for k in range(KD)], [r], [f"ps{ig}"])
                    pe_ops([mm(PS[ip][:, 0:ntok], wpp_sb[:, k, n * 128:(n + 1) * 128], pT[:, k, 0:ntok], k == 0, k == 1)
                            for k in range(2)], reads=["pT", "lw"], writes=[f"ps{ip}"])
                    s_ = sg[ci % 2]; t_ = tm[ci % 2]; sn = f"sg{ci % 2}"; tn = f"tm{ci % 2}"; ci += 1
                    R.op("act", lambda e, s_=s_, ig=ig: e.activation(out=s_[:, 0:ntok], in_=PS[ig][:, 0:ntok], func=AF.Sigmoid),
                         reads=[f"ps{ig}"], writes=[sn])
                    R.op("dve", lambda e, s_=s_, t_=t_, ip=ip: e.tensor_tensor(out=t_[:, 0:ntok], in0=PS[ip][:, 0:ntok],
                                                                               in1=s_[:, 0:ntok], op=ALU.mult),
                         reads=[f"ps{ip}", sn], writes=[tn])
                    R.op("dve", lambda e, t_=t_, n=n: e.tensor_tensor(out=xT[:, n, 0:ntok], in0=xT[:, n, 0:ntok],
                                                                      in1=t_[:, 0:ntok], op=ALU.add),
                         reads=[tn, "xT"], writes=["xT", f"xTc{n}"])
                    if store_fn is not None:
                        store_fn(n)
            R.retire()

    tiles = [(2 * DEC, [(0, DEC), (DEC, DEC)], [(0, DEC), (DEC, DEC)], S, True, 0)]
    for t_ in range(NT):
        tiles.append((TT, [(i * 128, 128) for i in range(4)], [(0, TT)], t_ * TT, False, t_))

    convert_layer(0)
    for l in range(DEPTH):
        R.dma("sp", lambda e, l=l: e.dma_start(out=cols[:], in_=cols_d[l]), writes=["lw"])
        R.dma("pool", lambda e, l=l: e.dma_start(out=poolw_sb[:], in_=pool_w[l].rearrange("g c d -> c g d")), writes=["lw"])
        R.dma("pool", lambda e, l=l: e.dma_start(out=wmT[:], in_=wsT_d[l].rearrange("g j i -> j g i")), writes=["lw"])
        R.dma("sp", lambda e, l=l: e.dma_start(out=gb_f[:], in_=gb_d[l]), writes=["lw"])
        R.dma("pool", lambda e, l=l: e.dma_start(out=wa2_sb[:], in_=wa2_d[l]), writes=["lw"])
        R.dma("pool", lambda e, l=l: e.dma_start(out=ba_sb[:], in_=ba_d[l]), writes=["lw"])
        R.dma("pool", lambda e, l=l: e.dma_start(out=wla_sb[:], in_=w_in[l][:, O_LA:O_LA + 16].rearrange("(k p) n -> p k n", p=128)),
              writes=["lw"])
        R.dma("pool", lambda e, l=l: e.dma_start(out=wpp_sb[:], in_=w_pp[l].rearrange("(k p) n -> p k n", p=128)), writes=["lw"])
        for g in range(4):
            R.op("dve", lambda e, g=g: e.tensor_tensor(out=wmT[:, g, :], in0=wmT[:, g, :], in1=bmask[:], op=ALU.mult),
                 reads=["lw", "const"], writes=["lw"])
        R.op("act", lambda e: e.activation(out=gb_hi[:], in_=gb_f[:], func=AF.Copy), reads=["lw"], writes=["lw2"])
        R.op("dve", lambda e: e.tensor_tensor(out=gb_lo[:], in0=gb_f[:], in1=gb_hi[:], op=ALU.subtract), reads=["lw", "lw2"],
             writes=["lw3"])
        R.op("dve", lambda e: e.memset(phist[:], 0.0), writes=["phist"])
        R.barrier(("pe", "act", "dve", "sp", "pool"))
        conv_step(len(conv_q))
        if l + 1 < DEPTH:
            convert_layer(l + 1, defer=True)
        for tile in tiles:
            ntok, subs, segs, c0g, is_sample, tix = tile
            cg = S if is_sample else c0g
            src = xin if l == 0 else xscr
            dst = yout if l == DEPTH - 1 else xscr
            rname = f"xd{cg}"
            srcv = src.rearrange("(k p) t -> k p t", p=128)
            dstv = dst.rearrange("(k p) t -> k p t", p=128)
            for k in range(KD):
                R.dma("sp", lambda e, srcv=srcv, cg=cg, ntok=ntok, k=k: e.dma_start(
                    out=xT[:, k, 0:ntok], in_=srcv[k, :, cg:cg + ntok]), reads=[f"{rname}_{k}"], writes=[f"xTc{k}"])

            def store_chunk(n, dstv=dstv, cg=cg, ntok=ntok, rname=rname):
                R.dma("sp", lambda e: e.dma_start(out=dstv[n, :, cg:cg + ntok], in_=xT[:, n, 0:ntok]),
                      reads=[f"xTc{n}"], writes=[f"{rname}_{n}"])
            if DBG >= 1:
                ffn(l, ntok, "f1a", "f1b", "f1c", C_N1)
            if DBG >= 2:
                mixer(l, tile)
            if DBG >= 8:
                ffn(l, ntok, "f2a", "f2b", "f2c", C_N2)
            if DBG >= 9:
                ple(l, tile, store_chunk)
    R.barrier(("pe", "act", "dve", "sp", "pool"))

    sems = {}
    for e in R.engs:
        sems[e] = es.enter_context(nc.semaphore(f"c_{e}"))
    for q in ("sp", "pool"):
        for i in range(R.dma_nsem[q]):
            sems[("dma", q, i)] = es.enter_context(nc.semaphore(f"d_{q}{i}"))
    block = es.enter_context(nc.Block())

    def replay(eng, name):
        for waits, fn, inc in R.ops[name]:
            for k, v in waits:
                eng.wait_ge(sems[k], v)
            if fn is None:
                continue
            ins = fn(eng)
            if inc[0] == "cnt":
                ins.then_inc(sems[inc[1]], 1)
            else:
                ins.then_inc(sems[inc[1]], 16)

    block.tensor(lambda e: replay(e, "pe"))
    block.scalar(lambda e: replay(e, "act"))
    block.vector(lambda e: replay(e, "dve"))
    block.gpsimd(lambda e: replay(e, "pool"))
    block.sync(lambda e: replay(e, "sp"))
    es.close()
    return nc


def make_consts():
    i = np.arange(128)
    c = {}
    c["c_ident"] = np.eye(128, dtype=np.float32)
    c["c_tinc"] = (i[:, None] >= i[None, :]).astype(np.float32)
    c["c_trile"] = (i[:, None] <= i[None, :]).astype(np.float32)
    q = np.arange(512)
    negp = np.zeros((128, 4, 512), np.float32)
    for d in range(4):
        negp[:, d, :] = np.where((i[:, None] + 128 * d) < q[None, :], 0.0, NEGV)
    c["c_negp"] = negp.reshape(128, 2048)
    j = np.arange(32)
    ns = np.where(j[:, None] < j[None, :], 0.0, NEGV).astype(np.float32)
    c["c_negs"] = np.tile(ns, (1, 8))
    c["c_bmask"] = ((i[:, None] // 64) <= (i[None, :] // 64)).astype(np.float32)
    invc = np.zeros((128, 4, 16), np.float32)
    for g in range(4):
        invc[:, g, :] = (1.0 / np.minimum(2 << g, q[:16] + 1)).astype(np.float32)[None, :]
    c["c_invc"] = invc.reshape(128, 64)
    return c


def layout_inputs(inp, S, PAST, DEPTH, n_cores):
    f = lambda a: np.ascontiguousarray(a, dtype=np.float32)
    shared = {}
    for src, dst in [("ffn1_w1", "f1a"), ("ffn1_w3", "f1b"), ("ffn1_w2", "f1c"), ("ffn2_w1", "f2a"), ("ffn2_w3", "f2b"),
                     ("ffn2_w2", "f2c"), ("w_in", "w_in"), ("pool_w", "pool_w"), ("gla_wa2", "wa2"), ("w_branch", "w_br"),
                     ("w_out", "w_o"), ("ple_w_gate", "w_pg"), ("ple_w_proj", "w_pp")]:
        shared[dst] = f(inp[src])
    shared["wsT"] = f(np.transpose(inp["gmlp_ws"], (0, 1, 3, 2)))
    shared["gb"] = f(np.reshape(inp["gmlp_b"], (DEPTH, 1, 512)))
    shared["ba"] = f(np.reshape(inp["gla_ba"], (DEPTH, 1, 256)))
    cols = np.zeros((DEPTH, 128, NCOL), np.float32)
    for nm, c0 in [("norm_ffn1", C_N1), ("norm_mix", C_NM), ("norm_ffn2", C_N2), ("norm_ple", C_NPL)]:
        cols[:, :, c0:c0 + 8] = np.transpose(np.reshape(inp[nm], (DEPTH, 8, 128)), (0, 2, 1))
    cols[:, :, C_PS:C_PS + 4] = np.transpose(np.reshape(inp["pool_scale"], (DEPTH, 4, 128)), (0, 2, 1))
    cols[:, :, C_QN] = np.tile(inp["sb_q_norm"], (1, 2))
    cols[:, :, C_KN] = np.tile(inp["sb_k_norm"], (1, 2))
    cols[:, :, C_GON] = inp["gla_out_norm"]
    shared["cols"] = cols
    shared.update(make_consts())
    maps = []
    for c in range(n_cores):
        m = dict(shared)
        xs = np.reshape(inp["x_sample"][2 * c:2 * c + 2], (2 * DEC, D))
        m["xin"] = f(np.concatenate([inp["x_prompt"][c].T, xs.T], axis=1))
        ps = np.reshape(inp["p_sample"][:, 2 * c:2 * c + 2], (DEPTH, 2 * DEC, PLE))
        m["pin"] = f(np.concatenate([np.transpose(inp["p_prompt"][:, c], (0, 2, 1)), np.transpose(ps, (0, 2, 1))], axis=2))
        m["ckT"] = f(np.transpose(np.reshape(inp["cache_sb_k"][:, 2 * c:2 * c + 2], (DEPTH, 2, PAST, BW)), (0, 1, 3, 2)))
        m["cv"] = f(np.reshape(inp["cache_sb_v"][:, 2 * c:2 * c + 2], (DEPTH, 2, PAST, BW)))
        m["spT"] = f(np.transpose(inp["state_pool"][:, 2 * c:2 * c + 2], (0, 1, 3, 2)))
        m["sgl"] = f(inp["state_gla"][:, 2 * c:2 * c + 2])
        maps.append(m)
    return maps


def assemble(results, S, DEPTH, n_cores):
    B = n_cores
    yp = np.zeros((B, S, D), np.float32); ys = np.zeros((2 * B, DEC, D), np.float32)
    kp = np.zeros((DEPTH, B, S, 8, 64), np.float32); vpo = np.zeros((DEPTH, B, S, 8, 64), np.float32)
    pp = np.zeros((DEPTH, B, 15, BW), np.float32); gp = np.zeros((DEPTH, B, 4, 64, 128), np.float32)
    ks = np.zeros((DEPTH, 2 * B, DEC, 8, 64), np.float32); vso = np.zeros((DEPTH, 2 * B, DEC, 8, 64), np.float32)
    pso = np.zeros((DEPTH, 2 * B, 15, BW), np.float32); gs = np.zeros((DEPTH, 2 * B, 4, 64, 128), np.float32)
    gm = np.zeros((DEPTH, 2 * B, DEC, BW), np.float32)
    for c, r in enumerate(results):
        yo = r["yout"]
        yp[c] = yo[:, :S].T
        ys[2 * c:2 * c + 2] = yo[:, S:].T.reshape(2, DEC, D)
        kp[:, c] = np.transpose(r["kpT"], (0, 2, 1)).reshape(DEPTH, S, 8, 64)
        vpo[:, c] = r["vp"].reshape(DEPTH, S, 8, 64)
        pp[:, c] = np.transpose(r["poolpT"], (0, 2, 1))
        gp[:, c] = r["glap"]
        ks[:, 2 * c:2 * c + 2] = np.transpose(r["ksT"], (0, 2, 1)).reshape(DEPTH, 2, DEC, 8, 64)
        vso[:, 2 * c:2 * c + 2] = r["vs"].reshape(DEPTH, 2, DEC, 8, 64)
        pso[:, 2 * c:2 * c + 2] = np.transpose(r["poolsT"], (0, 1, 3, 2))
        gs[:, 2 * c:2 * c + 2] = r["glas"]
        gm[:, 2 * c:2 * c + 2] = r["gms"].reshape(DEPTH, 2, DEC, BW)
    return (yp, ys, kp, vpo, pp, gp, ks, vso, pso, gs, gm)


def run(inp, S, PAST, DEPTH, n_cores=8):
    nc = build_program(S, PAST, DEPTH)
    maps = layout_inputs(inp, S, PAST, DEPTH, n_cores)
    res = run_bass_kernel_spmd(nc, maps, core_ids=list(range(n_cores)))
    return assemble(res.results, S, DEPTH, n_cores)


def kernel(**inputs):
    inp = {k: np.asarray(v) for k, v in inputs.items()}
    S = inp["x_prompt"].shape[1]
    PAST = inp["cache_sb_k"].shape[2]
    DEPTH = inp["w_in"].shape[0]
    return run(inp, S, PAST, DEPTH, 8)
```
